# Optimizing a Trainium2 kernel written in Bass

```python
import math
import jax, jax.numpy as jnp
from jax import lax
import numpy as np

D_MODEL = 4096
BATCH = 1
SEQ = 16384
DEPTH = 1
DEC_BATCH = 32
DEC_SEQ = 32
PAST_LEN = 1024

CHUNK = 64
N_META = 16
Q_BLOCK = 128
EPS = 1e-6

W_A = D_MODEL // 2
HEAD_DIM_A = 128
N_HEADS_A = W_A // (2 * HEAD_DIM_A)
N_BUCKETS = 32
MAX_DISTANCE = 1024

W_B = D_MODEL - W_A
HEAD_DIM_B = 128
N_HEADS_B = W_B // HEAD_DIM_B
CONV_W = 4

IN_COLS = 4 * W_A + 4 * W_B + 2 * N_HEADS_B
SPLIT_POINTS = (W_A, 2 * W_A, 3 * W_A, 4 * W_A, 4 * W_A + 3 * W_B, 4 * W_A + 4 * W_B,
                4 * W_A + 4 * W_B + N_HEADS_B)

kernel_name = 'hymba_diffattn_gdn_streaming_step'


def rms_norm(x, w):
    xf = x.astype(jnp.float32)
    y = xf * lax.rsqrt(jnp.mean(xf * xf, axis=-1, keepdims=True) + EPS)
    return (y * w.astype(jnp.float32)).astype(x.dtype)


def l2_norm(x):
    return x * lax.rsqrt(jnp.sum(x * x, axis=-1, keepdims=True) + EPS)


def lambda_init(layer):
    return 0.8 - 0.6 * math.exp(-0.3 * layer)


def diff_lambda(lq1, lk1, lq2, lk2, lam_init):
    f = jnp.float32
    return (jnp.exp(jnp.sum(lq1.astype(f) * lk1.astype(f)))
            - jnp.exp(jnp.sum(lq2.astype(f) * lk2.astype(f))) + lam_init)


def rel_bucket(rel):
    nb = N_BUCKETS // 2
    max_exact = nb // 2
    n = jnp.abs(rel)
    nf = jnp.maximum(n, 1).astype(jnp.float32)
    large = max_exact + (jnp.log(nf / max_exact) / math.log(MAX_DISTANCE / max_exact)
                         * (nb - max_exact)).astype(jnp.int32)
    large = jnp.minimum(large, nb - 1)
    return jnp.where(rel > 0, nb, 0) + jnp.where(n < max_exact, n, large)


def attend(q, k, v, q_pos, q_chunk, k_pos, k_chunk, rel_bias, lam):
    bias = jnp.transpose(rel_bias[rel_bucket(k_pos[None, :] - q_pos[:, None])], (2, 0, 1))
    visible = k_chunk[None, :] <= q_chunk[:, None]
    s = jnp.einsum('bqhcd,bkhcd->bhcqk', q, k).astype(jnp.float32) * (HEAD_DIM_A ** -0.5)
    s = jnp.where(visible, s + bias.astype(jnp.float32)[None, :, None], -jnp.inf)
    p = jax.nn.softmax(s, axis=-1)
    a = p[:, :, 0] - lam * p[:, :, 1]
    return jnp.einsum('bhqk,bkhe->bqhe', a.astype(v.dtype), v)


def in_proj(xn, w_in):
    B, L, _ = xn.shape
    q_a, k_a, v_a, z_a, qkv_b, z_b, b_raw, a_raw = jnp.split(xn @ w_in, SPLIT_POINTS, axis=-1)
    q_a = q_a.reshape(B, L, N_HEADS_A, 2, HEAD_DIM_A)
    k_a = k_a.reshape(B, L, N_HEADS_A, 2, HEAD_DIM_A)
    v_a = v_a.reshape(B, L, N_HEADS_A, 2 * HEAD_DIM_A)
    return q_a, k_a, v_a, z_a, qkv_b, z_b, b_raw, a_raw


def causal_conv_silu(x, buf, w):
    L = x.shape[1]
    xp = jnp.concatenate([buf.astype(x.dtype), x], axis=1)
    y = xp[:, 0:L] * w[0]
    for i in range(1, CONV_W):
        y = y + xp[:, i:i + L] * w[i]
    return jax.nn.silu(y), xp[:, -(CONV_W - 1):]


def gdn_inputs(qkv, b_raw, a_raw, a_log, dt_bias):
    B, L, _ = qkv.shape
    f = jnp.float32
    q, k, v = jnp.split(qkv.astype(f), 3, axis=-1)
    q = l2_norm(q.reshape(B, L, N_HEADS_B, HEAD_DIM_B)) * (HEAD_DIM_B ** -0.5)
    k = l2_norm(k.reshape(B, L, N_HEADS_B, HEAD_DIM_B))
    v = v.reshape(B, L, N_HEADS_B, HEAD_DIM_B)
    beta = jax.nn.sigmoid(b_raw.astype(f))
    g = -jnp.exp(a_log.astype(f)) * jax.nn.softplus(a_raw.astype(f) + dt_bias.astype(f))
    return q, k, v, g, beta


def gdn_chunk(S, q, k, v, g, beta):
    L = q.shape[1]
    dv = v.shape[-1]
    q, k, v = (jnp.swapaxes(t, 1, 2) for t in (q, k, v))
    g, beta = jnp.swapaxes(g, 1, 2), jnp.swapaxes(beta, 1, 2)
    G = jnp.cumsum(g, axis=-1)
    incl = jnp.tril(jnp.ones((L, L), dtype=bool))
    strict = jnp.tril(jnp.ones((L, L), dtype=bool), -1)
    gamma = jnp.exp(jnp.where(incl, G[..., :, None] - G[..., None, :], -jnp.inf))
    a_mat = jnp.where(strict, beta[..., :, None] * jnp.einsum('bhid,bhjd->bhij', k, k) * gamma, 0.0)
    a_mat = a_mat + jnp.eye(L, dtype=q.dtype)
    rhs = jnp.concatenate([v * beta[..., None], k * (beta * jnp.exp(G))[..., None]], axis=-1)
    sol = jax.lax.linalg.triangular_solve(a_mat, rhs, left_side=True, lower=True, unit_diagonal=True)
    u, w = sol[..., :dv], sol[..., dv:]
    v_new = u - jnp.einsum('bhlk,bhkv->bhlv', w, S)
    o = (jnp.einsum('bhlk,bhkv->bhlv', q * jnp.exp(G)[..., None], S)
         + jnp.einsum('bhij,bhjv->bhiv', jnp.einsum('bhid,bhjd->bhij', q, k) * gamma, v_new))
    g_last = G[..., -1:]
    S_new = (S * jnp.exp(g_last)[..., None]
             + jnp.einsum('bhlk,bhlv->bhkv', k * jnp.exp(g_last - G)[..., None], v_new))
    return S_new, jnp.swapaxes(o, 1, 2)


def attn_gate_out(o, z, subln, lam_init):
    B, L = o.shape[:2]
    o = (rms_norm(o, subln) * (1.0 - lam_init)).reshape(B, L, W_A)
    return o * jax.nn.silu(z.astype(o.dtype))


def gdn_gate_out(o, z, norm_w):
    B, L = o.shape[:2]
    o = rms_norm(o, norm_w).reshape(B, L, W_B)
    return o * jax.nn.silu(z.astype(o.dtype))


def merge(h, y_a, y_b, w_out, post_norm):
    y = jnp.concatenate([y_a.astype(h.dtype), y_b.astype(h.dtype)], axis=-1) @ w_out
    return h + rms_norm(y, post_norm)


def prompt_layer(h, layer, rel_bias, pre_norm, w_in, lq1, lk1, lq2, lk2, subln_a, conv_b,
                 a_log, dt_bias, norm_b, w_out, post_norm):
    B = h.shape[0]
    q_a, k_a, v_a, z_a, qkv_b, z_b, b_raw, a_raw = in_proj(rms_norm(h, pre_norm), w_in)
    lam_init = lambda_init(layer)
    lam = diff_lambda(lq1, lk1, lq2, lk2, lam_init)
    pos = jnp.concatenate([jnp.arange(-N_META, 0, dtype=jnp.int32), jnp.arange(SEQ, dtype=jnp.int32)])
    chunk = jnp.concatenate([jnp.full((N_META,), -1, jnp.int32),
                             jnp.arange(SEQ, dtype=jnp.int32) // CHUNK])
    o_meta = attend(q_a[:, :N_META], k_a, v_a, pos[:N_META], chunk[:N_META], pos, chunk, rel_bias, lam)
    n_blk = SEQ // Q_BLOCK
    q_blocks = jnp.moveaxis(q_a[:, N_META:].reshape(B, n_blk, Q_BLOCK, N_HEADS_A, 2, HEAD_DIM_A), 1, 0)
    o_frames = lax.map(
        lambda a: attend(a[0], k_a, v_a, a[1], a[2], pos, chunk, rel_bias, lam),
        (q_blocks, pos[N_META:].reshape(n_blk, Q_BLOCK), chunk[N_META:].reshape(n_blk, Q_BLOCK)))
    o_frames = jnp.moveaxis(o_frames, 0, 1).reshape(B, SEQ, N_HEADS_A, 2 * HEAD_DIM_A)
    y_a = attn_gate_out(jnp.concatenate([o_meta, o_frames], axis=1), z_a, subln_a, lam_init)
    qkv_c, conv_state = causal_conv_silu(qkv_b, jnp.zeros((B, CONV_W - 1, 3 * W_B), qkv_b.dtype), conv_b)
    q, k, v, g, beta = gdn_inputs(qkv_c, b_raw, a_raw, a_log, dt_bias)
    s0 = jnp.zeros((B, N_HEADS_B, HEAD_DIM_B, HEAD_DIM_B), jnp.float32)
    s1, o_meta_b = gdn_chunk(s0, q[:, :N_META], k[:, :N_META], v[:, :N_META], g[:, :N_META], beta[:, :N_META])
    n_ch = SEQ // CHUNK
    to_chunks = lambda t: jnp.moveaxis(t[:, N_META:].reshape((B, n_ch, CHUNK) + t.shape[2:]), 1, 0)
    s_fin, o_b = lax.scan(lambda s, xs: gdn_chunk(s, *xs), s1,
                          tuple(to_chunks(t) for t in (q, k, v, g, beta)))
    o_b = jnp.moveaxis(o_b, 0, 1).reshape(B, SEQ, N_HEADS_B, HEAD_DIM_B)
    y_b = gdn_gate_out(jnp.concatenate([o_meta_b, o_b], axis=1), z_b, norm_b)
    return merge(h, y_a, y_b, w_out, post_norm), (k_a, v_a, s_fin, conv_state)


def sample_layer(h, k_cache, v_cache, ssm, conv_buf, layer, rel_bias, pre_norm, w_in, lq1, lk1, lq2, lk2,
                 subln_a, conv_b, a_log, dt_bias, norm_b, w_out, post_norm):
    L = h.shape[1]
    q_a, k_a, v_a, z_a, qkv_b, z_b, b_raw, a_raw = in_proj(rms_norm(h, pre_norm), w_in)
    lam_init = lambda_init(layer)
    lam = diff_lambda(lq1, lk1, lq2, lk2, lam_init)
    k_all = jnp.concatenate([k_cache.astype(k_a.dtype), k_a], axis=1)
    v_all = jnp.concatenate([v_cache.astype(v_a.dtype), v_a], axis=1)
    k_pos = jnp.concatenate([jnp.arange(-N_META, 0, dtype=jnp.int32), jnp.arange(PAST_LEN + L, dtype=jnp.int32)])
    k_chunk = jnp.concatenate([jnp.full((N_META,), -1, jnp.int32),
                               jnp.arange(PAST_LEN + L, dtype=jnp.int32) // CHUNK])
    q_pos = PAST_LEN + jnp.arange(L, dtype=jnp.int32)
    o_a = attend(q_a, k_all, v_all, q_pos, q_pos // CHUNK, k_pos, k_chunk, rel_bias, lam)
    y_a = attn_gate_out(o_a, z_a, subln_a, lam_init)
    qkv_c, conv_new = causal_conv_silu(qkv_b, conv_buf, conv_b)
    q, k, v, g, beta = gdn_inputs(qkv_c, b_raw, a_raw, a_log, dt_bias)
    s_new, o_b = gdn_chunk(ssm.astype(jnp.float32), q, k, v, g, beta)
    y_b = gdn_gate_out(o_b, z_b, norm_b)
    return merge(h, y_a, y_b, w_out, post_norm), (k_a, v_a, s_new, conv_new)


def setup_inputs(seed: int = 0) -> dict:
    key = jax.random.key(seed)
    ks = jax.random.split(key, 24)
    f = jnp.float32
    nrm = lambda k, shape, s: jax.random.normal(k, shape, f) * s
    dt = jnp.exp(jax.random.uniform(ks[17], (DEPTH, N_HEADS_B), f, math.log(1e-3), math.log(1e-1)))
    return {
        'x_prompt': nrm(ks[0], (BATCH, SEQ, D_MODEL), 1.0),
        'x_sample': nrm(ks[1], (DEC_BATCH, DEC_SEQ, D_MODEL), 1.0),
        'cache_k_a': nrm(ks[2], (DEPTH, DEC_BATCH, N_META + PAST_LEN, N_HEADS_A, 2, HEAD_DIM_A), 1.0),
        'cache_v_a': nrm(ks[3], (DEPTH, DEC_BATCH, N_META + PAST_LEN, N_HEADS_A, 2 * HEAD_DIM_A), 1.0),
        'state_ssm_b': nrm(ks[4], (DEPTH, DEC_BATCH, N_HEADS_B, HEAD_DIM_B, HEAD_DIM_B), 0.1),
        'state_conv_b': nrm(ks[5], (DEPTH, DEC_BATCH, CONV_W - 1, 3 * W_B), 1.0),
        'meta_tokens': nrm(ks[6], (N_META, D_MODEL), 1.0),
        'rel_bias': nrm(ks[7], (N_BUCKETS, N_HEADS_A), 0.5),
        'pre_norm': 1.0 + nrm(ks[8], (DEPTH, D_MODEL), 0.05),
        'w_in': nrm(ks[9], (DEPTH, D_MODEL, IN_COLS), D_MODEL ** -0.5),
        'lambda_q1': nrm(ks[10], (DEPTH, HEAD_DIM_A), 0.1),
        'lambda_k1': nrm(ks[11], (DEPTH, HEAD_DIM_A), 0.1),
        'lambda_q2': nrm(ks[12], (DEPTH, HEAD_DIM_A), 0.1),
        'lambda_k2': nrm(ks[13], (DEPTH, HEAD_DIM_A), 0.1),
        'subln_a': 1.0 + nrm(ks[14], (DEPTH, 2 * HEAD_DIM_A), 0.05),
        'conv_b': nrm(ks[15], (DEPTH, CONV_W, 3 * W_B), CONV_W ** -0.5),
        'a_log_b': jnp.log(jax.random.uniform(ks[16], (DEPTH, N_HEADS_B), f, 1.0, 16.0)),
        'dt_bias_b': dt + jnp.log(-jnp.expm1(-dt)),
        'norm_b': 1.0 + nrm(ks[18], (DEPTH, HEAD_DIM_B), 0.05),
        'w_out': nrm(ks[19], (DEPTH, D_MODEL, D_MODEL), D_MODEL ** -0.5),
        'post_norm': 1.0 + nrm(ks[20], (DEPTH, D_MODEL), 0.05),
    }


def reference(x_prompt, x_sample, cache_k_a, cache_v_a, state_ssm_b, state_conv_b, meta_tokens, rel_bias,
              pre_norm, w_in, lambda_q1, lambda_k1, lambda_q2, lambda_k2, subln_a, conv_b, a_log_b, dt_bias_b,
              norm_b, w_out, post_norm):
    B = x_prompt.shape[0]
    hp = jnp.concatenate([jnp.broadcast_to(meta_tokens[None].astype(x_prompt.dtype), (B, N_META, D_MODEL)),
                          x_prompt], axis=1)
    hs = x_sample
    kp, vp, sp, cp, ksm, vsm, ssm_s, csm = [], [], [], [], [], [], [], []
    for l in range(DEPTH):
        lp = (pre_norm[l], w_in[l], lambda_q1[l], lambda_k1[l], lambda_q2[l], lambda_k2[l], subln_a[l],
              conv_b[l], a_log_b[l], dt_bias_b[l], norm_b[l], w_out[l], post_norm[l])
        hp, (k_p, v_p, s_p, c_p) = prompt_layer(hp, l, rel_bias, *lp)
        hs, (k_s, v_s, s_s, c_s) = sample_layer(hs, cache_k_a[l], cache_v_a[l], state_ssm_b[l],
                                                state_conv_b[l], l, rel_bias, *lp)
        kp.append(k_p); vp.append(v_p); sp.append(s_p); cp.append(c_p)
        ksm.append(k_s); vsm.append(v_s); ssm_s.append(s_s); csm.append(c_s)
    return (hp[:, N_META:], hs, jnp.stack(kp), jnp.stack(vp), jnp.stack(sp), jnp.stack(cp),
            jnp.stack(ksm), jnp.stack(vsm), jnp.stack(ssm_s), jnp.stack(csm))
```

```python
from contextlib import ExitStack
import math
import numpy as np
import concourse.bass as bass
import concourse.mybir as mybir
from concourse.bass_utils import run_bass_kernel_spmd

F32 = mybir.dt.float32
BF16 = mybir.dt.bfloat16
AF = mybir.ActivationFunctionType
ALU = mybir.AluOpType
AX = mybir.AxisListType
EPS = 1e-6
N_BUCKETS = 32
MAX_DISTANCE = 1024
CHUNK = 64
LAM_INIT = 0.8 - 0.6 * math.exp(0.0)
import os
GDN_FP32 = os.environ.get("GDN_FP32", "1") == "1"
LAST_E = None


class Cfg:
    def __init__(self, D=4096, NC=8, SEQ=16384, DECB=32, DECS=32, PAST=1024):
        self.D, self.NC, self.SEQ, self.DECB, self.DECS, self.PAST = D, NC, SEQ, DECB, DECS, PAST
        self.NMETA = 16
        self.WA = D // 2
        self.WB = D - self.WA
        self.HA = self.WA // 256
        self.HB = self.WB // 128
        self.INC = 4 * self.WA + 4 * self.WB + 2 * self.HB
        self.KC = D // 128
        self.OWN = SEQ // NC
        self.WL = 128 + SEQ
        self.SB = DECB // NC
        self.ST = self.SB * DECS
        assert self.ST == 128 and DECS == 32
        self.TL = self.WL + self.ST
        self.CL = self.NMETA + PAST
        self.OT = self.OWN + self.ST
        self.OK0 = self.WL - self.OWN - 128
        assert self.OK0 % 512 == 0 and self.WL % 512 == 128
        self.NF_W = 2 * self.HA + 3 * self.HB
        self.NF = self.NF_W + 2 * self.HA
        self.NT_W = self.WA + 2 * self.HB
        self.NT = self.NT_W + self.WA + self.WB


class Buf:
    __slots__ = ("name", "w", "r", "const", "psum")

    def __init__(self, name, const=False, psum=False):
        self.name = name
        self.w = None
        self.r = []
        self.const = const
        self.psum = psum


class Emitter:
    ENG = ("pe", "act", "dve", "pool", "sp")

    def __init__(self, nc, es, n_dma_sems=48):
        self.nc = nc
        self.eng = {"pe": nc.tensor, "act": nc.scalar, "dve": nc.vector, "pool": nc.gpsimd, "sp": nc.sync}
        self.psem = {e: es.enter_context(nc.semaphore("P_" + e)) for e in ("pe", "act", "dve", "pool")}
        self.pcnt = {e: 0 for e in self.psem}
        self.dsem = [es.enter_context(nc.semaphore("D%d" % i)) for i in range(n_dma_sems)]
        self.dval = [0] * n_dma_sems
        self.dnext = 0
        nq = n_dma_sems // 2
        self.qrange = {"sp": (0, nq), "pool": (nq, n_dma_sems), "act": (0, nq)}
        self.qnext = {"sp": 0, "pool": 0, "act": 0}
        self.waited = {}
        self.out_events = []

    def _need(self, waiter, ev):
        if ev is None:
            return None
        kind, src, val = ev
        if kind == "e" and src == waiter and waiter == "pe":
            return None
        key = (waiter, kind, src)
        if self.waited.get(key, 0) >= val:
            return None
        return key, val

    def _do_waits(self, waiter, evs):
        best = {}
        for ev in evs:
            n = self._need(waiter, ev)
            if n is not None:
                key, val = n
                if best.get(key, 0) < val:
                    best[key] = val
        for key, val in best.items():
            _, kind, src = key
            sem = self.psem[src] if kind == "e" else self.dsem[src]
            self.eng[waiter].wait_ge(sem, val)
            self.waited[key] = val

    def _deps(self, reads, writes, waiter=None):
        evs = []
        for b in reads:
            evs.append(b.w)
            if b.psum:
                evs.extend(ev for ev in b.r if not (ev[0] == "e" and ev[1] == waiter))
        for b in writes:
            evs.append(b.w)
            evs.extend(b.r)
        return evs

    def op(self, e, fn, reads=(), writes=()):
        self._do_waits(e, self._deps(reads, writes, e))
        ins = fn(self.eng[e])
        self.pcnt[e] += 1
        ins.then_inc(self.psem[e], 1)
        ev = ("e", e, self.pcnt[e])
        for b in reads:
            if not b.const:
                b.r.append(ev)
        for b in writes:
            b.w = ev
            b.r = []
        return ev

    def dma(self, q, out, in_, reads=(), writes=(), is_output=False, **kw):
        lo, hi = self.qrange[q]
        qk = "sp" if q == "act" else q
        i = lo + self.qnext[qk]
        self.qnext[qk] = (self.qnext[qk] + 1) % (hi - lo)
        evs = self._deps(reads, writes)
        if self.dval[i] > 0:
            evs.append(("d", i, self.dval[i]))
        self._do_waits(q, evs)
        ins = self.eng[q].dma_start(out=out, in_=in_, **kw)
        self.dval[i] += 16
        ins.then_inc(self.dsem[i], 16)
        ev = ("d", i, self.dval[i])
        for b in reads:
            if not b.const:
                b.r.append(ev)
        for b in writes:
            b.w = ev
            b.r = []
        if is_output:
            self.out_events.append(ev)
        return ev

    def barrier(self):
        evs = []
        for i, v in enumerate(self.dval):
            if v > 0:
                evs.append(("d", i, v))
        for e in self.psem:
            if self.pcnt[e] > 0:
                evs.append(("e", e, self.pcnt[e]))
        for w in ("pe", "act", "dve", "pool", "sp"):
            self._do_waits(w, [ev for ev in evs if not (ev[0] == "e" and ev[1] == w)])

    def finish(self):
        evs = list(self.out_events)
        for i, v in enumerate(self.dval):
            if v > 0:
                evs.append(("d", i, v))
        for e in self.psem:
            if self.pcnt[e] > 0:
                evs.append(("e", e, self.pcnt[e]))
        self._do_waits("sp", evs)


class Pool:
    def __init__(self, tiles, name, nsub=0, psum=False):
        self.tiles = tiles
        if nsub:
            self.bufs = [[Buf("%s%d_%d" % (name, i, j)) for j in range(nsub)] for i in range(len(tiles))]
        else:
            self.bufs = [Buf("%s%d" % (name, i), psum=psum) for i in range(len(tiles))]
        self.i = 0

    def get(self):
        t, b = self.tiles[self.i], self.bufs[self.i]
        self.i = (self.i + 1) % len(self.tiles)
        return t, b


def dram_view(t, offset, pattern):
    return bass.AP(t, offset, pattern)


def rel_bucket_np(rel):
    nb = N_BUCKETS // 2
    max_exact = nb // 2
    n = np.abs(rel)
    nf = np.maximum(n, 1).astype(np.float32)
    large = max_exact + (np.log(nf / np.float32(max_exact)) / np.float32(math.log(MAX_DISTANCE / max_exact))
                         * np.float32(nb - max_exact)).astype(np.int32)
    large = np.minimum(large, nb - 1)
    return np.where(rel > 0, nb, 0) + np.where(n < max_exact, n, large)


C_ID, C_AID, C_TRIU, C_LINC, C_LSTR, C_ONE = 0, 128, 256, 384, 512, 640
CST_W = 768
BV_R0 = 511
BV_LEN = 1664
NEAR_D = [-640, -512, -384, -256, -128, 0, 128, 256, 384]


def make_oh():
    i = np.arange(BV_LEN)
    bk = rel_bucket_np((BV_R0 - i).astype(np.int32))
    o = np.zeros((N_BUCKETS, BV_LEN), np.float32)
    o[bk, i] = 1.0
    return o


def make_dmask():
    m = np.zeros((128, 4, 512), np.float32)
    p = np.arange(128)[:, None]
    j = np.arange(512)[None, :]
    for a in range(4):
        m[:, a, :] = ((128 * a + p) // 64) <= (j // 64)
    return m.reshape(128, 2048)


def make_cst():
    c = np.zeros((128, CST_W), np.float32)
    i = np.arange(128)
    c[:, C_ID:C_ID + 128] = np.eye(128)
    c[:, C_AID:C_AID + 128] = np.eye(128)[::-1]
    c[:, C_TRIU:C_TRIU + 128] = (i[:, None] <= i[None, :])
    c[:, C_LINC:C_LINC + 128] = (i[:, None] >= i[None, :])
    c[:, C_LSTR:C_LSTR + 128] = (i[:, None] > i[None, :])
    c[:, C_ONE:C_ONE + 128] = 1.0
    return c


def build(cfg, phases=(0, 1, 2, 3, 4)):
    nc = bass.Bass("TRN2", target_bir_lowering=False)
    D, KC, TL, WL, OWN, OT, OK0 = cfg.D, cfg.KC, cfg.TL, cfg.WL, cfg.OWN, cfg.OT, cfg.OK0
    HA, HB, WA, WB, SB, ST, CL = cfg.HA, cfg.HB, cfg.WA, cfg.WB, cfg.SB, cfg.ST, cfg.CL
    KO = TL - OK0

    def din(name, shape, dt=F32):
        return nc.dram_tensor(name, list(shape), dt, kind="ExternalInput")

    def dout(name, shape, dt=F32):
        return nc.dram_tensor(name, list(shape), dt, kind="ExternalOutput")

    def dscr(name, shape, dt):
        return nc.dram_tensor(name, list(shape), dt)

    xT = din("xT", [D, TL])
    xown = din("xown", [OT, D])
    wF = din("wF", [D, cfg.NF * 128])
    wT = din("wT", [D, cfg.NT])
    wout = din("wout", [D, D])
    pn = din("pn", [128, KC])
    postn = din("postn", [1, D])
    convw = din("convw", [128, 3 * HB * 4])
    convst = din("convst", [128, 3 * HB * SB * 3])
    gconst = din("gconst", [1, 2 * HB])
    relb = din("relb", [N_BUCKETS, HA])
    lamv = din("lamv", [1, 4 * 128])
    subln = din("subln", [1, 256])
    normb = din("normb", [1, 128])
    ckT = din("ckT", [SB * 2 * HA * 128, CL])
    cv = din("cv", [SB * CL, WA])
    ssm = din("ssm", [SB * HB * 128, 128])
    cst = din("cst", [128, CST_W])
    NKT = WL // 128
    valid = din("valid", [128, NKT])
    oh = din("oh", [N_BUCKETS, BV_LEN])
    dmask = din("dmask", [128, 4 * 512])
    bv_s = dscr("bv_s", [HA, BV_LEN], F32)
    y_o = dout("y_o", [OT, D])
    kT_o = dout("kT_o", [2 * HA * 128, KO])
    v_o = dout("v_o", [KO, WA])
    ssm_o = dout("ssm_o", [HB * 128, 128])
    ssm_so = dout("ssm_so", [SB * HB * 128, 128])
    conv_o = dout("conv_o", [3 * HB * 128, 3])
    conv_so = dout("conv_so", [3 * HB * 128, SB * 3])
    xnT_s = dscr("xnT_s", [128, KC * TL], BF16)
    kT_s = dscr("kT_s", [2 * HA * 128, TL], BF16)
    v_s = dscr("v_s", [TL, WA], BF16)
    gT_s = dscr("gT_s", [3 * HB * 128, TL], BF16)
    gb_s = dscr("gb_s", [TL, 2 * HB], F32)
    qT_s = dscr("qT_s", [2 * HA * 128, OT], BF16)
    zs_s = dscr("zs_s", [OT, WA + WB], BF16)
    yT_s = dscr("yT_s", [D, OT], BF16)

    es = ExitStack()
    with es:
        E = Emitter(nc, es)

        def sb(stack, name, shape, dt):
            return stack.enter_context(nc.sbuf_tensor(name, list(shape), dt))

        def ps(stack, name, shape, dt=F32):
            return stack.enter_context(nc.psum_tensor(name, list(shape), dt))

        cst_sb = sb(es, "cst_sb", [128, CST_W], F32)
        cst_b = Buf("cst", const=True)
        E.dma("sp", cst_sb[:], cst.ap(), writes=[cst_b])
        cbf = sb(es, "cbf", [128, CST_W], BF16)
        cbf_b = Buf("cbf", const=True)
        E.op("dve", lambda e: e.tensor_copy(out=cbf[:], in_=cst_sb[:]), reads=[cst_b], writes=[cbf_b])
        onesD = sb(es, "onesD", [128, 128], BF16)
        onesD_b = Buf("onesD", const=True)
        E.op("dve", lambda e: e.tensor_scalar(out=onesD[:], in0=cst_sb[:, C_ONE:C_ONE + 128], scalar1=1.0 / D,
                                              scalar2=None, op0=ALU.mult), reads=[cst_b], writes=[onesD_b])
        eps_sb = sb(es, "eps_sb", [128, 1], F32)
        eps_b = Buf("eps", const=True)
        E.op("dve", lambda e: e.memset(eps_sb[:], EPS), writes=[eps_b])
        pn_sb = sb(es, "pn_sb", [128, KC], F32)
        pn_b = Buf("pn", const=True)
        E.dma("sp", pn_sb[:], pn.ap(), writes=[pn_b])

        xns_b = [Buf("xns%d" % i) for i in range(TL // 256)]

        def xns_deps(t0, tw):
            return [xns_b[i] for i in range(t0 // 256, (t0 + tw - 1) // 256 + 1)]

        kTs_b = Buf("kT_s")
        vs_b = Buf("v_s")
        gTs_b = Buf("gT_s")
        gbs_b = Buf("gb_s")
        qTs_b = Buf("qT_s")
        zss_b = Buf("zs_s")
        yTs_b = Buf("yT_s")

        convw_sb = sb(es, "convw_sb", [128, 3 * HB, 4], F32)
        convst_sb = sb(es, "convst_sb", [128, 3 * HB, SB, 3], F32)
        gc_sb = sb(es, "gc_sb", [128, 2 * HB], F32)
        nA_sb = sb(es, "nA_sb", [128, HB], F32)
        ones_bf = cbf[:, C_ONE:C_ONE + 128]
        ident_bf = cbf[:, C_ID:C_ID + 128]
        smallc_b = Buf("smallc", const=True)
        E.dma("sp", convw_sb[:], convw.ap(), writes=[smallc_b])
        E.dma("sp", convst_sb[:], convst.ap(), writes=[smallc_b])
        E.dma("sp", gc_sb[:], dram_view(gconst, 0, [[0, 128], [1, 2 * HB]]), writes=[smallc_b])
        nA_b = Buf("nA", const=True)
        E.op("act", lambda e: e.activation(out=nA_sb[:], in_=gc_sb[:, 0:HB], func=AF.Exp),
             reads=[smallc_b], writes=[nA_b])
        E.op("dve", lambda e: e.tensor_scalar(out=nA_sb[:], in0=nA_sb[:], scalar1=-1.0, scalar2=None, op0=ALU.mult),
             reads=[nA_b], writes=[nA_b])
        smallc_b.w = None if False else smallc_b.w

        def phase0():
            with ExitStack() as st:
                TB = 256
                xf_p = Pool([sb(st, "xf%d" % i, [128, KC, TB], F32) for i in range(2)], "xf")
                sq_p = Pool([sb(st, "sq%d" % i, [128, KC, TB], BF16) for i in range(1)], "sq")
                xn_p = Pool([sb(st, "xn%d" % i, [128, KC, TB], BF16) for i in range(2)], "xn", nsub=KC)
                rs_p = Pool([sb(st, "rs%d" % i, [128, TB], F32) for i in range(2)], "rs")
                ss_p = Pool([ps(st, "ss%d" % i, [128, 512]) for i in range(2)], "ss", psum=True)
                for blk in range(TL // TB):
                    t0 = blk * TB
                    xf, xf_b = xf_p.get()
                    E.dma("sp", xf[:], dram_view(xT, t0, [[TL, 128], [128 * TL, KC], [1, TB]]), writes=[xf_b])
                    sq, sq_b = sq_p.get()
                    E.op("act", lambda e: e.activation(out=sq[:], in_=xf[:], func=AF.Square),
                         reads=[xf_b], writes=[sq_b])
                    ss, ss_b = ss_p.get()
                    for kc in range(KC):
                        E.op("pe", lambda e: e.matmul(ss[:, 0:TB], lhsT=onesD[:], rhs=sq[:, kc, :],
                                                      start=(kc == 0), stop=(kc == KC - 1)),
                             reads=[onesD_b, sq_b], writes=[ss_b])
                    rs, rs_b = rs_p.get()
                    E.op("act", lambda e: e.activation(out=rs[:], in_=ss[:, 0:TB], func=AF.Sqrt, bias=eps_sb[:, 0:1]),
                         reads=[ss_b, eps_b], writes=[rs_b])
                    E.op("dve", lambda e: e.reciprocal(out=rs[:], in_=rs[:]), reads=[rs_b], writes=[rs_b])
                    xn, xn_bs = xn_p.get()
                    for kc in range(KC):
                        E.op("dve", lambda e: e.scalar_tensor_tensor(out=xn[:, kc, :], in0=xf[:, kc, :],
                                                                   scalar=pn_sb[:, kc:kc + 1], in1=rs[:],
                                                                   op0=ALU.mult, op1=ALU.mult),
                             reads=[xf_b, rs_b, pn_b], writes=[xn_bs[kc]])
                    E.dma("pool", dram_view(xnT_s, t0, [[KC * TL, 128], [TL, KC], [1, TB]]), xn[:],
                          reads=xn_bs, writes=[xns_b[blk]])

        if 0 in phases:
            phase0()
            E.barrier()

        def phase1():
            with ExitStack() as st:
                WGW = 1088
                Wg = sb(st, "Wg", [128, KC, WGW], BF16)
                Wg_b = Buf("Wg")
                xnb_p = Pool([sb(st, "xnb%d" % i, [128, KC, 512], BF16) for i in range(2)], "xnb")
                mm_p = Pool([ps(st, "mm%d" % i, [128, 512]) for i in range(5)], "mm", psum=True)
                ss_p = Pool([ps(st, "ssq%d" % i, [128, 512]) for i in range(2)], "ssq", psum=True)
                f32_p = Pool([sb(st, "ev%d" % i, [128, 512], F32) for i in range(4)], "ev")
                ext_p = Pool([sb(st, "ext%d" % i, [128, 520], F32) for i in range(3)], "ext")
                ext2_p = Pool([sb(st, "ext2_%d" % i, [128, SB, 36], F32) for i in range(2)], "ext2")
                yc_p = Pool([sb(st, "yc%d" % i, [128, 512], F32) for i in range(3)], "yc")
                ys_p = Pool([sb(st, "ys%d" % i, [128, 512], F32) for i in range(3)], "ys")
                sqb_p = Pool([sb(st, "sqb%d" % i, [128, 512], BF16) for i in range(2)], "sqb")
                rs_p = Pool([sb(st, "rsq%d" % i, [128, 512], F32) for i in range(2)], "rsq")
                obf_p = Pool([sb(st, "obf%d" % i, [128, 512], BF16) for i in range(4)], "obf")
                gb_p = Pool([sb(st, "gbt%d" % i, [128, 2 * HB], F32) for i in range(2)], "gbt")
                tmp_p = Pool([sb(st, "gtmp%d" % i, [128, HB], F32) for i in range(2)], "gtmp")
                car = sb(st, "car", [128, 3 * HB, 4], F32)
                car_b = [Buf("car%d" % i) for i in range(3 * HB)]
                E.op("pool", lambda e: e.memset(car[:], 0.0), writes=car_b)

                def win_blocks():
                    return [(t0, min(512, TL - t0)) for t0 in range(0, TL, 512)]

                def own_blocks():
                    return [(t0, min(512, TL - t0)) for t0 in range(WL - OWN, TL, 512)]

                def load_W(src, ncols, c0, gw):
                    E.dma("pool", Wg[:, :, 0:gw],
                          dram_view(src, c0, [[ncols, 128], [128 * ncols, KC], [1, gw]]), writes=[Wg_b])

                def load_xn(t0, tw):
                    xnb, xnb_b = xnb_p.get()
                    E.dma("sp", xnb[:, :, 0:tw],
                          dram_view(xnT_s, t0, [[KC * TL, 128], [TL, KC], [1, tw]]),
                          reads=xns_deps(t0, tw), writes=[xnb_b])
                    return xnb, xnb_b

                def mm_F(xnb, xnb_b, tw, j):
                    pt, pt_b = mm_p.get()
                    for kc in range(KC):
                        E.op("pe", lambda e: e.matmul(pt[:, 0:tw], lhsT=Wg[:, kc, j * 128:(j + 1) * 128],
                                                      rhs=xnb[:, kc, 0:tw], start=(kc == 0), stop=(kc == KC - 1)),
                             reads=[Wg_b, xnb_b], writes=[pt_b])
                    return pt, pt_b

                def mm_T(xnb, xnb_b, sub, c0, gw):
                    pt, pt_b = mm_p.get()
                    for kc in range(KC):
                        E.op("pe", lambda e: e.matmul(pt[:, 0:gw], lhsT=xnb[:, kc, sub * 128:(sub + 1) * 128],
                                                      rhs=Wg[:, kc, c0:c0 + gw], start=(kc == 0), stop=(kc == KC - 1)),
                             reads=[Wg_b, xnb_b], writes=[pt_b])
                    return pt, pt_b

                def ev_ka(f, t0, tw, pt, pt_b):
                    kf, kf_b = f32_p.get()
                    E.op("act", lambda e: e.activation(out=kf[:, 0:tw], in_=pt[:, 0:tw], func=AF.Copy),
                         reads=[pt_b], writes=[kf_b])
                    E.dma("pool", kT_s[f * 128:(f + 1) * 128, t0:t0 + tw], kf[:, 0:tw], reads=[kf_b], writes=[kTs_b])
                    if t0 >= OK0:
                        E.dma("sp", kT_o[f * 128:(f + 1) * 128, t0 - OK0:t0 - OK0 + tw], kf[:, 0:tw],
                              reads=[kf_b], is_output=True)

                def ev_qa(f, t0, tw, pt, pt_b):
                    ob, ob_b = obf_p.get()
                    E.op("act", lambda e: e.activation(out=ob[:, 0:tw], in_=pt[:, 0:tw], func=AF.Copy),
                         reads=[pt_b], writes=[ob_b])
                    o0 = t0 - (WL - OWN)
                    E.dma("sp", qT_s[f * 128:(f + 1) * 128, o0:o0 + tw], ob[:, 0:tw], reads=[ob_b], writes=[qTs_b])

                def conv_tail(t, kind, hb, ext_v, n, yc_v, ys_v, wr, t0, s3=None):
                    ext_b, yc_b, ys_b = wr
                    E.op("dve", lambda e: e.tensor_scalar(out=yc_v, in0=ext_v(0), scalar1=convw_sb[:, t, 0:1],
                                                          scalar2=None, op0=ALU.mult),
                         reads=[ext_b, smallc_b], writes=[yc_b])
                    for i in range(1, 4):
                        E.op("dve", lambda e: e.scalar_tensor_tensor(out=yc_v, in0=ext_v(i),
                                                                     scalar=convw_sb[:, t, i:i + 1], in1=yc_v,
                                                                     op0=ALU.mult, op1=ALU.add),
                             reads=[ext_b, smallc_b, yc_b], writes=[yc_b])
                    E.op("act", lambda e: e.activation(out=ys_v, in_=yc_v, func=AF.Silu), reads=[yc_b], writes=[ys_b])

                def norm_store(t, kind, ys2, ys_b, n, t0):
                    ob, ob_b = obf_p.get()
                    if kind == 2:
                        E.op("pool", lambda e: e.tensor_copy(out=ob[:, 0:n], in_=ys2), reads=[ys_b], writes=[ob_b])
                    else:
                        sq, sq_b = sqb_p.get()
                        E.op("act", lambda e: e.activation(out=sq[:, 0:n], in_=ys2, func=AF.Square),
                             reads=[ys_b], writes=[sq_b])
                        ss, ss_b = ss_p.get()
                        E.op("pe", lambda e: e.matmul(ss[:, 0:n], lhsT=ones_bf, rhs=sq[:, 0:n], start=True, stop=True),
                             reads=[cbf_b, sq_b], writes=[ss_b])
                        rs, rs_b = rs_p.get()
                        E.op("act", lambda e: e.activation(out=rs[:, 0:n], in_=ss[:, 0:n], func=AF.Sqrt,
                                                           bias=eps_sb[:, 0:1]), reads=[ss_b, eps_b], writes=[rs_b])
                        E.op("dve", lambda e: e.reciprocal(out=rs[:, 0:n], in_=rs[:, 0:n]), reads=[rs_b], writes=[rs_b])
                        sc = (128 ** -0.5) if kind == 0 else 1.0
                        E.op("dve", lambda e: e.scalar_tensor_tensor(out=ob[:, 0:n], in0=ys2, scalar=sc, in1=rs[:, 0:n],
                                                                     op0=ALU.mult, op1=ALU.mult),
                             reads=[ys_b, rs_b], writes=[ob_b])
                    E.dma("pool", gT_s[t * 128:(t + 1) * 128, t0:t0 + n], ob[:, 0:n], reads=[ob_b], writes=[gTs_b])

                def ev_g(t, t0, tw, pt, pt_b):
                    kind = t // HB
                    nw = tw if t0 + tw <= WL else WL - t0
                    ext, ext_b = ext_p.get()
                    E.op("pool", lambda e: e.tensor_copy(out=ext[:, 0:3], in_=car[:, t, 0:3]),
                         reads=[car_b[t]], writes=[ext_b])
                    E.op("act", lambda e: e.activation(out=ext[:, 3:3 + nw], in_=pt[:, 0:nw], func=AF.Copy),
                         reads=[pt_b], writes=[ext_b])
                    E.op("pool", lambda e: e.tensor_copy(out=car[:, t, 0:3], in_=ext[:, nw:nw + 3]),
                         reads=[ext_b], writes=[car_b[t]])
                    if t0 + nw == WL:
                        E.dma("sp", conv_o[t * 128:(t + 1) * 128, :], ext[:, nw:nw + 3], reads=[ext_b], is_output=True)
                    yc, yc_b = yc_p.get()
                    ys, ys_b = ys_p.get()
                    conv_tail(t, kind, None, lambda i: ext[:, i:i + nw], nw, yc[:, 0:nw], ys[:, 0:nw],
                              (ext_b, yc_b, ys_b), t0)
                    norm_store(t, kind, ys[:, 0:nw], ys_b, nw, t0)
                    if nw < tw:
                        assert tw - nw == ST
                        e2, e2_b = ext2_p.get()
                        E.op("pool", lambda e: e.tensor_copy(out=e2[:, :, 0:3], in_=convst_sb[:, t, :, :]),
                             reads=[smallc_b], writes=[e2_b])
                        E.op("act", lambda e: e.activation(out=e2[:, :, 3:35],
                                                           in_=pt[:, nw:tw].rearrange("p (b s) -> p b s", s=32),
                                                           func=AF.Copy), reads=[pt_b], writes=[e2_b])
                        E.dma("sp", conv_so[t * 128:(t + 1) * 128, :].rearrange("p (b s) -> p b s", s=3),
                              e2[:, :, 32:35], reads=[e2_b], is_output=True)
                        yc2, yc2_b = yc_p.get()
                        ys2, ys2_b = ys_p.get()
                        ycv = yc2[:, 0:ST].rearrange("p (b s) -> p b s", s=32)
                        ysv = ys2[:, 0:ST].rearrange("p (b s) -> p b s", s=32)
                        conv_tail(t, kind, None, lambda i: e2[:, :, i:i + 32], ST, ycv, ysv, (e2_b, yc2_b, ys2_b), t0)
                        norm_store(t, kind, ys2[:, 0:ST], ys2_b, ST, WL)

                def ev_va(c0, gw, t0, sub, pt, pt_b):
                    vf, vf_b = f32_p.get()
                    E.op("dve", lambda e: e.tensor_copy(out=vf[:, 0:gw], in_=pt[:, 0:gw]), reads=[pt_b], writes=[vf_b])
                    r0 = t0 + sub * 128
                    E.dma("pool", v_s[r0:r0 + 128, c0:c0 + gw], vf[:, 0:gw], reads=[vf_b], writes=[vs_b])
                    if r0 >= OK0:
                        E.dma("sp", v_o[r0 - OK0:r0 - OK0 + 128, c0:c0 + gw], vf[:, 0:gw], reads=[vf_b], is_output=True)

                def ev_ba(t0, sub, pt, pt_b):
                    gbt, gbt_b = gb_p.get()
                    tmp, tmp_b = tmp_p.get()
                    E.op("act", lambda e: e.activation(out=gbt[:, 0:HB], in_=pt[:, 0:HB], func=AF.Sigmoid),
                         reads=[pt_b], writes=[gbt_b])
                    E.op("dve", lambda e: e.tensor_tensor(out=tmp[:], in0=pt[:, HB:2 * HB], in1=gc_sb[:, HB:2 * HB],
                                                          op=ALU.add), reads=[pt_b, smallc_b], writes=[tmp_b])
                    E.op("act", lambda e: e.activation(out=tmp[:], in_=tmp[:], func=AF.Exp), reads=[tmp_b], writes=[tmp_b])
                    E.op("act", lambda e: e.activation(out=tmp[:], in_=tmp[:], func=AF.Ln, bias=1.0),
                         reads=[tmp_b], writes=[tmp_b])
                    E.op("dve", lambda e: e.tensor_tensor(out=gbt[:, HB:2 * HB], in0=tmp[:], in1=nA_sb[:], op=ALU.mult),
                         reads=[tmp_b, nA_b], writes=[gbt_b])
                    r0 = t0 + sub * 128
                    E.dma("pool", gb_s[r0:r0 + 128, :], gbt[:], reads=[gbt_b], writes=[gbs_b])

                def ev_z(c0, gw, zc0, t0, sub, pt, pt_b):
                    ob, ob_b = obf_p.get()
                    E.op("act", lambda e: e.activation(out=ob[:, 0:gw], in_=pt[:, 0:gw], func=AF.Silu),
                         reads=[pt_b], writes=[ob_b])
                    r0 = t0 + sub * 128 - (WL - OWN)
                    E.dma("sp", zs_s[r0:r0 + 128, zc0:zc0 + gw], ob[:, 0:gw], reads=[ob_b], writes=[zss_b])

                GF = 8
                f_tiles = [("ka", f) for f in range(2 * HA)] + [("g", t) for t in range(3 * HB)]
                for g0 in range(0, len(f_tiles), GF):
                    grp = f_tiles[g0:g0 + GF]
                    load_W(wF, cfg.NF * 128, g0 * 128, len(grp) * 128)
                    for (t0, tw) in win_blocks():
                        xnb, xnb_b = load_xn(t0, tw)
                        for j, (kind, idx) in enumerate(grp):
                            pt, pt_b = mm_F(xnb, xnb_b, tw, j)
                            if kind == "ka":
                                ev_ka(idx, t0, tw, pt, pt_b)
                            else:
                                ev_g(idx, t0, tw, pt, pt_b)
                qa_tiles = list(range(2 * HA))
                for g0 in range(0, len(qa_tiles), GF):
                    grp = qa_tiles[g0:g0 + GF]
                    load_W(wF, cfg.NF * 128, (cfg.NF_W + g0) * 128, len(grp) * 128)
                    for (t0, tw) in own_blocks():
                        xnb, xnb_b = load_xn(t0, tw)
                        for j, f in enumerate(grp):
                            pt, pt_b = mm_F(xnb, xnb_b, tw, j)
                            ev_qa(f, t0, tw, pt, pt_b)
                segs = [("va", c, min(512, WA - c)) for c in range(0, WA, 512)] + [("ba", WA, 2 * HB)]
                groups = []
                cur, curw = [], 0
                for sg in segs:
                    if curw + sg[2] > WGW:
                        groups.append(cur)
                        cur, curw = [], 0
                    cur.append(sg)
                    curw += sg[2]
                groups.append(cur)
                for grp in groups:
                    gc0 = grp[0][1]
                    gw_tot = sum(sg[2] for sg in grp)
                    load_W(wT, cfg.NT, gc0, gw_tot)
                    for (t0, tw) in win_blocks():
                        xnb, xnb_b = load_xn(t0, tw)
                        for sub in range(tw // 128):
                            for (kind, c, w) in grp:
                                pt, pt_b = mm_T(xnb, xnb_b, sub, c - gc0, w)
                                if kind == "va":
                                    ev_va(c, w, t0, sub, pt, pt_b)
                                else:
                                    ev_ba(t0, sub, pt, pt_b)
                zsegs = [(c, min(512, WA + WB - c)) for c in range(0, WA + WB, 512)]
                for g0 in range(0, len(zsegs), 2):
                    grp = zsegs[g0:g0 + 2]
                    gc0 = grp[0][0]
                    gw_tot = sum(w for _, w in grp)
                    load_W(wT, cfg.NT, cfg.NT_W + gc0, gw_tot)
                    for (t0, tw) in own_blocks():
                        xnb, xnb_b = load_xn(t0, tw)
                        for sub in range(tw // 128):
                            for (c, w) in grp:
                                pt, pt_b = mm_T(xnb, xnb_b, sub, c - gc0, w)
                                ev_z(c - gc0, w, c, t0, sub, pt, pt_b)

        if 1 in phases:
            phase1()
            E.barrier()


        def phase2():
            with ExitStack() as st:
                QB0 = WL - OWN
                SCALE = 128 ** -0.5
                KTt = sb(st, "KTt", [128, 2, WL], BF16)
                KT_b = Buf("KTt")
                Vt = sb(st, "Vt", [128, NKT, 256], BF16)
                V_b = Buf("Vt")
                QTt = sb(st, "QTt", [128, 2, OT], BF16)
                QT_b = Buf("QTt")
                val_f = sb(st, "val_f", [128, NKT], F32)
                val_h = sb(st, "val_h", [128, NKT], BF16)
                val_b = Buf("val", const=True)
                E.dma("sp", val_f[:], valid.ap(), writes=[val_b])
                E.op("dve", lambda e: e.tensor_copy(out=val_h[:], in_=val_f[:]), reads=[val_b], writes=[val_b])
                dm_f = sb(st, "dm_f", [128, 4, 512], F32)
                dm_b = Buf("dm", const=True)
                E.dma("sp", dm_f[:], dmask.ap().rearrange("p (a j) -> p a j", j=512), writes=[dm_b])
                lv = sb(st, "lv", [128, 4, 128], F32)
                lam_t = sb(st, "lam_t", [128, 4], F32)
                sl_sb = sb(st, "sl_sb", [128, 256], F32)
                rb_sb = sb(st, "rb_sb", [128, HA], F32)
                misc_b = Buf("misc2", const=True)
                E.dma("sp", lv[:], dram_view(lamv, 0, [[0, 128], [128, 4], [1, 128]]), writes=[misc_b])
                E.dma("sp", sl_sb[:], dram_view(subln, 0, [[0, 128], [1, 256]]), writes=[misc_b])
                E.dma("sp", rb_sb[:], dram_view(relb, 15 * HA, [[0, 128], [1, HA]]), writes=[misc_b])
                E.op("dve", lambda e: e.tensor_tensor(out=lv[:, 0, :], in0=lv[:, 0, :], in1=lv[:, 1, :], op=ALU.mult),
                     reads=[misc_b], writes=[misc_b])
                E.op("dve", lambda e: e.tensor_tensor(out=lv[:, 2, :], in0=lv[:, 2, :], in1=lv[:, 3, :], op=ALU.mult),
                     reads=[misc_b], writes=[misc_b])
                E.op("dve", lambda e: e.tensor_reduce(out=lam_t[:, 0:1], in_=lv[:, 0, :], axis=AX.X, op=ALU.add),
                     reads=[misc_b], writes=[misc_b])
                E.op("dve", lambda e: e.tensor_reduce(out=lam_t[:, 1:2], in_=lv[:, 2, :], axis=AX.X, op=ALU.add),
                     reads=[misc_b], writes=[misc_b])
                E.op("act", lambda e: e.activation(out=lam_t[:, 0:2], in_=lam_t[:, 0:2], func=AF.Exp), reads=[misc_b], writes=[misc_b])
                E.op("dve", lambda e: e.tensor_tensor(out=lam_t[:, 2:3], in0=lam_t[:, 0:1], in1=lam_t[:, 1:2], op=ALU.subtract),
                     reads=[misc_b], writes=[misc_b])
                E.op("dve", lambda e: e.tensor_scalar(out=lam_t[:, 3:4], in0=lam_t[:, 2:3], scalar1=LAM_INIT, scalar2=-1.0,
                                                      op0=ALU.add, op1=ALU.mult), reads=[misc_b], writes=[misc_b])
                E.op("dve", lambda e: e.tensor_scalar(out=sl_sb[:], in0=sl_sb[:], scalar1=1.0 - LAM_INIT, scalar2=None, op0=ALU.mult),
                     reads=[misc_b], writes=[misc_b])
                sc_p = Pool([ps(st, "sct%d" % i, [128, 512]) for i in range(4)], "sct", psum=True)
                bvs_b = Buf("bv_s")
                with ExitStack() as st2:
                    relb_sb = sb(st2, "relb_sb", [N_BUCKETS, HA], F32)
                    oh_sb = sb(st2, "oh_sb", [N_BUCKETS, BV_LEN], F32)
                    bvt = sb(st2, "bvt", [HA, BV_LEN], F32)
                    bv_b = Buf("bv")
                    E.dma("sp", relb_sb[:], relb.ap(), writes=[bv_b])
                    E.dma("sp", oh_sb[:], oh.ap(), writes=[bv_b])
                    for c0 in range(0, BV_LEN, 512):
                        w = min(512, BV_LEN - c0)
                        pt, pt_b = sc_p.get()
                        E.op("pe", lambda e: e.matmul(pt[0:HA, 0:w], lhsT=relb_sb[:, :], rhs=oh_sb[:, c0:c0 + w], start=True, stop=True),
                             reads=[bv_b], writes=[pt_b])
                        E.op("act", lambda e: e.activation(out=bvt[:, c0:c0 + w], in_=pt[0:HA, 0:w], func=AF.Copy), reads=[pt_b], writes=[bv_b])
                    E.dma("sp", bv_s.ap(), bvt[:], reads=[bv_b], writes=[bvs_b])
                E.barrier()

                hk_p = Pool([sb(st, "hk%d" % i, [128, 512], F32) for i in range(2)], "hk")
                eb = sb(st, "eb", [128, 9, 512], BF16)
                eb_b = Buf("eb")
                NKS = (CL + 32 + 127) // 128
                ebs = sb(st, "ebs", [128, NKS, 32], BF16)
                ebs_b = Buf("ebs")
                ebtmp_p = Pool([sb(st, "ebtmp%d" % i, [128, 512], F32) for i in range(2)], "ebtmp")
                PT_p = Pool([sb(st, "PT%d" % i, [128, 512], BF16) for i in range(3)], "PT")
                oacc = [ps(st, "oacc%d" % i, [128, 512]) for i in range(2)]
                oden = ps(st, "oden", [128, 512])
                oacc_b = Buf("oacc", psum=True)
                o1 = sb(st, "o1", [128, 4, 256], F32)
                o1_b = Buf("o1")
                ofin_p = Pool([sb(st, "ofin%d" % i, [128, 256], F32) for i in range(2)], "ofin")
                osq = sb(st, "osq", [128, 256], F32)
                osq_b = Buf("osq")
                rd = sb(st, "rd", [128, 8], F32)
                rd_b = Buf("rd")
                st_p = Pool([sb(st, "ast%d" % i, [128, 2], F32) for i in range(2)], "ast")
                zs_p = Pool([sb(st, "azs%d" % i, [128, 256], BF16) for i in range(2)], "azs")
                ybf_p = Pool([sb(st, "aybf%d" % i, [128, 256], BF16) for i in range(2)], "aybf")
                yT_p = Pool([sb(st, "ayT%d" % i, [128, 2, 128], BF16) for i in range(2)], "ayT")
                tp_p = Pool([ps(st, "atp", [128, 1024], BF16)[:, 0:128]], "atp", psum=True)
                KTs = sb(st, "KTs", [128, 2, CL + 32], BF16)
                KTs_b = Buf("KTs")
                NKS = (CL + 32 + 127) // 128
                Vs = sb(st, "Vs", [128, NKS, 256], BF16)
                Vs_b = Buf("Vs")
                PTs_p = Pool([sb(st, "PTs%d" % i, [128, NKS, 32], BF16) for i in range(2)], "PTs")

                def build_bias(h):
                    for i, dl in enumerate(NEAR_D):
                        hk, hk_b = hk_p.get()
                        base = BV_R0 - 127 - dl
                        E.dma("sp", hk[:], dram_view(bv_s, h * BV_LEN + base, [[1, 128], [1, 512]]), reads=[bvs_b], writes=[hk_b])
                        pt, pt_b = sc_p.get()
                        E.op("pe", lambda e: e.matmul(pt[:], lhsT=cst_sb[:, C_AID:C_AID + 128], rhs=hk[:], start=True, stop=True),
                             reads=[hk_b, cst_b], writes=[pt_b])
                        if dl < 0:
                            E.op("act", lambda e: e.activation(out=eb[:, i, :], in_=pt[:], func=AF.Exp), reads=[pt_b], writes=[eb_b])
                        else:
                            t, t_b = ebtmp_p.get()
                            E.op("act", lambda e: e.activation(out=t[:], in_=pt[:], func=AF.Exp), reads=[pt_b], writes=[t_b])
                            E.op("dve", lambda e: e.tensor_tensor(out=eb[:, i, :], in0=t[:], in1=dm_f[:, dl // 128, :], op=ALU.mult),
                                 reads=[t_b, dm_b], writes=[eb_b])
                    for kt in range(NKS):
                        hk, hk_b = hk_p.get()
                        base = BV_R0 - 127 - (128 * kt - CL)
                        E.dma("sp", hk[:, 0:32], dram_view(bv_s, h * BV_LEN + base, [[1, 128], [1, 32]]), reads=[bvs_b], writes=[hk_b])
                        pt, pt_b = sc_p.get()
                        E.op("pe", lambda e: e.matmul(pt[:, 0:32], lhsT=cst_sb[:, C_AID:C_AID + 128], rhs=hk[:, 0:32], start=True, stop=True),
                             reads=[hk_b, cst_b], writes=[pt_b])
                        E.op("act", lambda e: e.activation(out=ebs[:, kt, :], in_=pt[:, 0:32], func=AF.Exp), reads=[pt_b], writes=[ebs_b])

                def finish_o(h, L, o_ps_list, den_cols, c, orow_list):
                    ns = len(o_ps_list)
                    for sub in range(ns):
                        E.op("dve", lambda e: e.reciprocal(out=rd[0:L, c * 4 + sub:c * 4 + sub + 1], in_=den_cols[sub]),
                             reads=[oacc_b], writes=[rd_b])
                    if c == 0:
                        for sub in range(ns):
                            E.op("act", lambda e: e.activation(out=o1[0:L, sub, :], in_=o_ps_list[sub], func=AF.Copy,
                                                               scale=rd[0:L, sub:sub + 1]), reads=[oacc_b, rd_b], writes=[o1_b])
                        return
                    E.op("dve", lambda e: e.tensor_scalar(out=rd[0:L, 4:4 + ns], in0=rd[0:L, 4:4 + ns], scalar1=lam_t[0:L, 3:4],
                                                          scalar2=None, op0=ALU.mult), reads=[rd_b, misc_b], writes=[rd_b])
                    for sub in range(ns):
                        of, of_b = ofin_p.get()
                        E.op("dve", lambda e: e.scalar_tensor_tensor(out=of[0:L, :], in0=o_ps_list[sub], scalar=rd[0:L, 4 + sub:5 + sub],
                                                                     in1=o1[0:L, sub, :], op0=ALU.mult, op1=ALU.add),
                             reads=[oacc_b, rd_b, o1_b], writes=[of_b])
                        stt, stt_b = st_p.get()
                        E.op("act", lambda e: e.activation(out=osq[0:L, :], in_=of[0:L, :], func=AF.Square, accum_out=stt[0:L, 0:1]),
                             reads=[of_b], writes=[osq_b, stt_b])
                        E.op("act", lambda e: e.activation(out=stt[0:L, 1:2], in_=stt[0:L, 0:1], func=AF.Sqrt, scale=1.0 / 256,
                                                           bias=eps_sb[0:L, 0:1]), reads=[stt_b, eps_b], writes=[stt_b])
                        E.op("dve", lambda e: e.reciprocal(out=stt[0:L, 1:2], in_=stt[0:L, 1:2]), reads=[stt_b], writes=[stt_b])
                        orow = orow_list[sub]
                        zs, zs_b = zs_p.get()
                        E.dma("sp", zs[0:L, :], zs_s[orow:orow + L, h * 256:(h + 1) * 256], reads=[zss_b], writes=[zs_b])
                        E.op("dve", lambda e: e.scalar_tensor_tensor(out=of[0:L, :], in0=of[0:L, :], scalar=stt[0:L, 1:2], in1=sl_sb[0:L, :],
                                                                     op0=ALU.mult, op1=ALU.mult), reads=[of_b, stt_b, misc_b], writes=[of_b])
                        yb, yb_b = ybf_p.get()
                        E.op("dve", lambda e: e.tensor_tensor(out=yb[0:L, :], in0=of[0:L, :], in1=zs[0:L, :], op=ALU.mult),
                             reads=[of_b, zs_b], writes=[yb_b])
                        yT, yT_b = yT_p.get()
                        for e2 in range(2):
                            tp, tp_b = tp_p.get()
                            E.op("pe", lambda e: e.transpose(tp[:, 0:L], yb[0:L, e2 * 128:(e2 + 1) * 128], ident_bf[0:L, 0:L]),
                                 reads=[yb_b, cbf_b], writes=[tp_b])
                            E.op("act", lambda e: e.activation(out=yT[:, e2, 0:L], in_=tp[:, 0:L], func=AF.Copy), reads=[tp_b], writes=[yT_b])
                        E.dma("pool", dram_view(yT_s, h * 256 * OT + orow, [[OT, 128], [128 * OT, 2], [1, L]]), yT[:, :, 0:L],
                              reads=[yT_b], writes=[yTs_b])

                for h in range(HA):
                    E.dma("sp", KTt[:], dram_view(kT_s, h * 256 * TL, [[TL, 128], [128 * TL, 2], [1, WL]]), reads=[kTs_b], writes=[KT_b])
                    for kq in range(0, NKT, 32):
                        nk = min(32, NKT - kq)
                        E.dma("sp", Vt[:, kq:kq + nk, :],
                              dram_view(v_s, kq * 128 * WA + h * 256, [[WA, 128], [128 * WA, nk], [1, 256]]), reads=[vs_b], writes=[V_b])
                    E.dma("sp", QTt[:], dram_view(qT_s, h * 256 * OT, [[OT, 128], [128 * OT, 2], [1, OT]]), reads=[qTs_b], writes=[QT_b])
                    build_bias(h)
                    for qb in range(OWN // 512):
                        q0 = qb * 512
                        kt_hi = (QB0 + q0) // 128 + 4
                        for c in range(2):
                            for a in oacc + [oden]:
                                E.op("dve", lambda e: e.memset(a[:], 0.0), writes=[oacc_b])
                            for kt in range(kt_hi):
                                dl = 128 * kt - QB0 - q0
                                pt, pt_b = sc_p.get()
                                E.op("pe", lambda e: e.matmul(pt[:], lhsT=KTt[:, c, kt * 128:(kt + 1) * 128], rhs=QTt[:, c, q0:q0 + 512],
                                                              start=True, stop=True), reads=[KT_b, QT_b], writes=[pt_b])
                                PT, PT_b = PT_p.get()
                                if dl < NEAR_D[0]:
                                    E.op("act", lambda e: e.activation(out=PT[:], in_=pt[:], func=AF.Exp, scale=SCALE,
                                                                       bias=rb_sb[:, h:h + 1]), reads=[pt_b, misc_b], writes=[PT_b])
                                else:
                                    E.op("act", lambda e: e.activation(out=PT[:], in_=pt[:], func=AF.Exp, scale=SCALE),
                                         reads=[pt_b], writes=[PT_b])
                                    E.op("pool", lambda e: e.tensor_tensor(out=PT[:], in0=PT[:], in1=eb[:, NEAR_D.index(dl), :], op=ALU.mult),
                                         reads=[PT_b, eb_b], writes=[PT_b])
                                for sub in range(4):
                                    lt = PT[:, sub * 128:(sub + 1) * 128]
                                    E.op("pe", lambda e: e.matmul(oacc[sub // 2][:, (sub % 2) * 256:(sub % 2 + 1) * 256], lhsT=lt,
                                                                  rhs=Vt[:, kt, :], start=False, stop=(kt == kt_hi - 1),
                                                                  skip_group_check=True), reads=[PT_b, V_b], writes=[oacc_b])
                                    E.op("pe", lambda e: e.matmul(oden[:, sub:sub + 1], lhsT=lt, rhs=val_h[:, kt:kt + 1], start=False,
                                                                  stop=(kt == kt_hi - 1), skip_group_check=True),
                                         reads=[PT_b, val_b], writes=[oacc_b])
                            finish_o(h, 128, [oacc[sub // 2][:, (sub % 2) * 256:(sub % 2 + 1) * 256] for sub in range(4)],
                                     [oden[:, sub:sub + 1] for sub in range(4)], c, [q0 + sub * 128 for sub in range(4)])
                    for b in range(SB):
                        E.dma("pool", KTs[:, :, 0:CL],
                              dram_view(ckT, (b * 2 * HA + 2 * h) * 128 * CL, [[CL, 128], [128 * CL, 2], [1, CL]]), writes=[KTs_b])
                        E.dma("sp", KTs[:, :, CL:CL + 32],
                              dram_view(kT_s, h * 256 * TL + WL + 32 * b, [[TL, 128], [128 * TL, 2], [1, 32]]), reads=[kTs_b], writes=[KTs_b])
                        nfull = CL // 128
                        rem = CL - nfull * 128
                        E.dma("pool", Vs[:, 0:nfull, :],
                              dram_view(cv, b * CL * WA + h * 256, [[WA, 128], [128 * WA, nfull], [1, 256]]), writes=[Vs_b])
                        if rem:
                            E.dma("pool", Vs[0:rem, nfull, :],
                                  dram_view(cv, (b * CL + nfull * 128) * WA + h * 256, [[WA, rem], [1, 256]]), writes=[Vs_b])
                        E.dma("sp", Vs[rem:rem + 32, nfull, :],
                              dram_view(v_s, (WL + 32 * b) * WA + h * 256, [[WA, 32], [1, 256]]), reads=[vs_b], writes=[Vs_b])
                        qc0 = OWN + 32 * b
                        for c in range(2):
                            pt, pt_b = sc_p.get()
                            for kt in range(NKS):
                                n = min(128, CL + 32 - kt * 128)
                                E.op("pe", lambda e: e.matmul(pt[0:n, kt * 32:(kt + 1) * 32], lhsT=KTs[:, c, kt * 128:kt * 128 + n],
                                                              rhs=QTt[:, c, qc0:qc0 + 32], start=True, stop=True),
                                     reads=[KTs_b, QT_b], writes=[pt_b])
                            PTs, PTs_b = PTs_p.get()
                            E.op("act", lambda e: e.activation(out=PTs[:], in_=pt[:, 0:NKS * 32].rearrange("p (k q) -> p k q", q=32),
                                                               func=AF.Exp, scale=SCALE), reads=[pt_b], writes=[PTs_b])
                            E.op("pool", lambda e: e.tensor_tensor(out=PTs[:], in0=PTs[:], in1=ebs[:], op=ALU.mult),
                                 reads=[PTs_b, ebs_b], writes=[PTs_b])
                            for kt in range(NKS):
                                n = min(128, CL + 32 - kt * 128)
                                E.op("pe", lambda e: e.matmul(oacc[0][0:32, 0:256], lhsT=PTs[0:n, kt, :], rhs=Vs[0:n, kt, :],
                                                              start=(kt == 0), stop=(kt == NKS - 1)), reads=[PTs_b, Vs_b], writes=[oacc_b])
                            for kt in range(NKS):
                                n = min(128, CL + 32 - kt * 128)
                                E.op("pe", lambda e: e.matmul(oden[0:32, 0:1], lhsT=PTs[0:n, kt, :], rhs=ones_bf[0:n, 0:1],
                                                              start=(kt == 0), stop=(kt == NKS - 1)), reads=[PTs_b, cbf_b], writes=[oacc_b])
                            finish_o(h, 32, [oacc[0][0:32, 0:256]], [oden[0:32, 0:1]], c, [qc0])

        if 2 in phases:
            phase2()
            E.barrier()

        def phase3():
            with ExitStack() as st:
                GH = min(8, HB)
                S_f = sb(st, "S_f", [128, HB, 128], F32)
                S_h = sb(st, "S_h", [128, HB, 128], BF16)
                S_b = [Buf("S%d" % h) for h in range(HB)]
                Sh_b = [Buf("Sh%d" % h) for h in range(HB)]
                nb_sb = sb(st, "nb_sb", [128, 128], F32)
                nb_b = Buf("nb", const=True)
                E.dma("sp", nb_sb[:], dram_view(normb, 0, [[0, 128], [1, 128]]), writes=[nb_b])
                gbt_p = Pool([sb(st, "g3bt%d" % i, [128, 2 * HB], F32) for i in range(2)], "g3bt")
                qkv_p = Pool([sb(st, "qkv%d" % i, [128, 3 * HB, 128], BF16) for i in range(2)], "qkv")
                zs_p = Pool([sb(st, "zs%d" % i, [128, WB], BF16) for i in range(2)], "zs")
                sc_p = Pool([sb(st, "sc%d" % i, [128, 6 * HB], F32) for i in range(2)], "sc")
                o_p = Pool([sb(st, "osb%d" % i, [128, HB, 128], F32) for i in range(2)], "osb", nsub=HB)
                sq_t = sb(st, "o_sq", [128, HB, 128], F32)
                sq_tb = Buf("o_sq")
                rst_p = Pool([sb(st, "orst%d" % i, [128, 2 * HB], F32) for i in range(2)], "orst")
                yb_p = Pool([sb(st, "ybf%d" % i, [128, HB, 128], BF16) for i in range(2)], "ybf")
                yT_p = Pool([sb(st, "yTt%d" % i, [128, HB, 128], BF16) for i in range(2)], "yTt", nsub=HB)
                scps_p = Pool([ps(st, "scps", [128, 512])], "scps", psum=True)
                pf_p = Pool([ps(st, "pfb%d" % i, [128, 512])[:, 0:128] for i in range(5)], "pf", psum=True)
                pb_p = Pool([ps(st, "pbf%d" % i, [128, 1024], BF16)[:, 0:128] for i in range(2)], "pb", psum=True)
                NHS = GH
                TDT = F32 if GDN_FP32 else BF16
                ident_t = cst_sb[:, C_ID:C_ID + 128] if GDN_FP32 else ident_bf
                pT_p = pf_p if GDN_FP32 else pb_p
                def hs_tiles(i):
                    d = {}
                    for nm in ("kt", "gams", "gamT", "WT", "vnew", "AqkT"):
                        d[nm] = (sb(st, "h%d_%s" % (i, nm), [128, 128], BF16), Buf("h%d_%s" % (i, nm)))
                    for nm in ("kbe", "vb", "N", "M", "X0", "X1", "XT0", "XT1", "P0", "P1"):
                        d[nm] = (sb(st, "h%d_%s" % (i, nm), [128, 128], TDT), Buf("h%d_%s" % (i, nm)))
                    for nm in ("gtri", "gam", "U", "t1"):
                        d[nm] = (sb(st, "h%d_%s" % (i, nm), [128, 128], F32), Buf("h%d_%s" % (i, nm)))
                    return d
                HS = [hs_tiles(i) for i in range(NHS)]
                triu = cst_sb[:, C_TRIU:C_TRIU + 128]
                lstr = cst_sb[:, C_LSTR:C_LSTR + 128]
                linc = cst_sb[:, C_LINC:C_LINC + 128]
                ones_f = cst_sb[:, C_ONE:C_ONE + 128]
                lstr_bf = cbf[:, C_LSTR:C_LSTR + 128]
                triu_bf = cbf[:, C_TRIU:C_TRIU + 128]

                def chunk(t0, L, need_o, orow0):
                    nfac = int(math.ceil(math.log2(L)))
                    gbt, gbt_b = gbt_p.get()
                    E.dma("sp", gbt[0:L, :], gb_s[t0:t0 + L, :], reads=[gbs_b], writes=[gbt_b])
                    qkv, qkv_b = qkv_p.get()
                    E.dma("sp", qkv[:, :, 0:L], dram_view(gT_s, t0, [[TL, 128], [128 * TL, 3 * HB], [1, L]]),
                          reads=[gTs_b], writes=[qkv_b])
                    if need_o:
                        zs, zs_b = zs_p.get()
                        E.dma("sp", zs[0:L, :], zs_s[orow0:orow0 + L, WA:WA + WB], reads=[zss_b], writes=[zs_b])
                    scps, scps_b = scps_p.get()
                    E.op("pe", lambda e: e.matmul(scps[0:L, 0:HB], lhsT=triu[0:L, 0:L], rhs=gbt[0:L, HB:2 * HB],
                                                  start=True, stop=True), reads=[cst_b, gbt_b], writes=[scps_b])
                    E.op("pe", lambda e: e.matmul(scps[:, HB:2 * HB], lhsT=ones_f[0:L, :], rhs=gbt[0:L, HB:2 * HB],
                                                  start=True, stop=True), reads=[cst_b, gbt_b], writes=[scps_b])
                    sc, sc_b = sc_p.get()
                    c_eG, c_eGLG, c_eGL, c_beG, c_nb, c_tmp = [slice(i * HB, (i + 1) * HB) for i in range(6)]
                    E.op("act", lambda e: e.activation(out=sc[0:L, c_eG], in_=scps[0:L, 0:HB], func=AF.Exp),
                         reads=[scps_b], writes=[sc_b])
                    E.op("act", lambda e: e.activation(out=sc[0:L, c_tmp], in_=scps[0:L, 0:HB], func=AF.Copy),
                         reads=[scps_b], writes=[sc_b])
                    E.op("dve", lambda e: e.tensor_tensor(out=sc[0:L, c_tmp], in0=scps[0:L, HB:2 * HB], in1=sc[0:L, c_tmp],
                                                          op=ALU.subtract), reads=[scps_b, sc_b], writes=[sc_b])
                    E.op("act", lambda e: e.activation(out=sc[0:L, c_eGLG], in_=sc[0:L, c_tmp], func=AF.Exp),
                         reads=[sc_b], writes=[sc_b])
                    E.op("act", lambda e: e.activation(out=sc[:, c_eGL], in_=scps[:, HB:2 * HB], func=AF.Exp),
                         reads=[scps_b], writes=[sc_b])
                    E.op("dve", lambda e: e.tensor_tensor(out=sc[0:L, c_beG], in0=sc[0:L, c_eG], in1=gbt[0:L, 0:HB],
                                                          op=ALU.mult), reads=[sc_b, gbt_b], writes=[sc_b])
                    E.op("dve", lambda e: e.tensor_scalar(out=sc[0:L, c_nb], in0=gbt[0:L, 0:HB], scalar1=-1.0, scalar2=None,
                                                          op0=ALU.mult), reads=[gbt_b], writes=[sc_b])
                    if need_o:
                        osb, osb_bs = o_p.get()

                    def col(cs, h):
                        return sc[0:L, cs.start + h:cs.start + h + 1]

                    for g0 in range(0, HB, GH):
                        heads = list(range(g0, min(HB, g0 + GH)))
                        T = {h: HS[h - g0] for h in heads}
                        QT = {h: qkv[:, h, 0:L] for h in heads}
                        KT = {h: qkv[:, HB + h, 0:L] for h in heads}
                        VT = {h: qkv[:, 2 * HB + h, 0:L] for h in heads}
                        for h in heads:
                            pk, pk_b = pb_p.get()
                            E.op("pe", lambda e: e.transpose(pk[0:L, :], KT[h], ident_bf), reads=[qkv_b, cbf_b], writes=[pk_b])
                            t, b = T[h]["kbe"]
                            E.op("act", lambda e: e.activation(out=t[0:L, :], in_=pk[0:L, :], func=AF.Copy, scale=col(c_beG, h)),
                                 reads=[pk_b, sc_b], writes=[b])
                            t, b = T[h]["kt"]
                            E.op("dve", lambda e: e.tensor_scalar(out=t[0:L, :], in0=pk[0:L, :], scalar1=col(c_eGLG, h),
                                                                  scalar2=None, op0=ALU.mult), reads=[pk_b, sc_b], writes=[b])
                            pv, pv_b = pb_p.get()
                            E.op("pe", lambda e: e.transpose(pv[0:L, :], VT[h], ident_bf), reads=[qkv_b, cbf_b], writes=[pv_b])
                            t, b = T[h]["vb"]
                            E.op("act", lambda e: e.activation(out=t[0:L, :], in_=pv[0:L, :], func=AF.Copy,
                                                               scale=gbt[0:L, h:h + 1]), reads=[pv_b, gbt_b], writes=[b])
                        for h in heads:
                            gt, gt_b = T[h]["gtri"]
                            E.op("pool", lambda e: e.tensor_scalar(out=gt[0:L, 0:L], in0=triu[0:L, 0:L],
                                                                   scalar1=gbt[0:L, HB + h:HB + h + 1], scalar2=None,
                                                                   op0=ALU.mult), reads=[cst_b, gbt_b], writes=[gt_b])
                            pg, pg_b = pf_p.get()
                            E.op("pe", lambda e: e.matmul(pg[0:L, 0:L], lhsT=gt[0:L, 0:L], rhs=lstr[0:L, 0:L], start=True, stop=True),
                                 reads=[gt_b, cst_b], writes=[pg_b])
                            gm, gm_b = T[h]["gam"]
                            E.op("act", lambda e: e.activation(out=gm[0:L, 0:L], in_=pg[0:L, 0:L], func=AF.Exp),
                                 reads=[pg_b], writes=[gm_b])
                            gs, gs_b = T[h]["gams"]
                            E.op("pool", lambda e: e.tensor_tensor(out=gs[0:L, 0:L], in0=gm[0:L, 0:L], in1=lstr[0:L, 0:L], op=ALU.mult),
                                 reads=[gm_b, cst_b], writes=[gs_b])
                            pkk, pkk_b = pf_p.get()
                            E.op("pe", lambda e: e.matmul(pkk[0:L, 0:L], lhsT=KT[h], rhs=KT[h], start=True, stop=True),
                                 reads=[qkv_b], writes=[pkk_b])
                            n_, n_b = T[h]["N"]
                            E.op("dve", lambda e: e.scalar_tensor_tensor(out=n_[0:L, 0:L], in0=pkk[0:L, 0:L], scalar=col(c_nb, h),
                                                                         in1=gs[0:L, 0:L], op0=ALU.mult, op1=ALU.mult),
                                 reads=[pkk_b, sc_b, gs_b], writes=[n_b])
                            if need_o:
                                pgt, pgt_b = pf_p.get()
                                E.op("pe", lambda e: e.matmul(pgt[0:L, 0:L], lhsT=lstr[0:L, 0:L], rhs=gt[0:L, 0:L], start=True, stop=True),
                                     reads=[gt_b, cst_b], writes=[pgt_b])
                                t1, t1_b = T[h]["t1"]
                                E.op("act", lambda e: e.activation(out=t1[0:L, 0:L], in_=pgt[0:L, 0:L], func=AF.Exp),
                                     reads=[pgt_b], writes=[t1_b])
                                gT_, gT_b = T[h]["gamT"]
                                E.op("pool", lambda e: e.tensor_tensor(out=gT_[0:L, 0:L], in0=t1[0:L, 0:L], in1=triu[0:L, 0:L], op=ALU.mult),
                                     reads=[t1_b, cst_b], writes=[gT_b])
                        cur = {}
                        for h in heads:
                            n_, n_b = T[h]["N"]
                            pm, pm_b = pT_p.get()
                            E.op("pe", lambda e: e.transpose(pm[0:L, 0:L], n_[0:L, 0:L], ident_t[0:L, 0:L]), reads=[n_b, cbf_b, cst_b], writes=[pm_b])
                            m_, m_b = T[h]["M"]
                            E.op("act", lambda e: e.activation(out=m_[0:L, 0:L], in_=pm[0:L, 0:L], func=AF.Copy), reads=[pm_b], writes=[m_b])
                            p0, p0_b = T[h]["P0"]
                            E.op("dve", lambda e: e.tensor_tensor(out=p0[0:L, 0:L], in0=pm[0:L, 0:L], in1=ident_t[0:L, 0:L], op=ALU.add),
                                 reads=[pm_b, cbf_b, cst_b], writes=[p0_b])
                            cur[h] = dict(X=T[h]["M"], XT=T[h]["N"], P=T[h]["P0"], pi=0, xi=0)
                        for k in range(1, nfac):
                            last = (k == nfac - 1)
                            for h in heads:
                                c = cur[h]
                                (X, X_b), (XT, XT_b), (P, P_b) = c["X"], c["XT"], c["P"]
                                nXT = T[h]["XT%d" % c["xi"]]
                                pxt, pxt_b = pf_p.get()
                                E.op("pe", lambda e: e.matmul(pxt[0:L, 0:L], lhsT=X[0:L, 0:L], rhs=XT[0:L, 0:L], start=True, stop=True),
                                     reads=[X_b, XT_b], writes=[pxt_b])
                                E.op("dve", lambda e: e.tensor_copy(out=nXT[0][0:L, 0:L], in_=pxt[0:L, 0:L]), reads=[pxt_b], writes=[nXT[1]])
                                if not last:
                                    nX = T[h]["X%d" % c["xi"]]
                                    px, px_b = pf_p.get()
                                    E.op("pe", lambda e: e.matmul(px[0:L, 0:L], lhsT=XT[0:L, 0:L], rhs=X[0:L, 0:L], start=True, stop=True),
                                         reads=[X_b, XT_b], writes=[px_b])
                                    E.op("act", lambda e: e.activation(out=nX[0][0:L, 0:L], in_=px[0:L, 0:L], func=AF.Copy),
                                         reads=[px_b], writes=[nX[1]])
                                    c["X"] = nX
                                c["XT"] = nXT
                                c["xi"] ^= 1
                            for h in heads:
                                c = cur[h]
                                (XT, XT_b), (P, P_b) = c["XT"], c["P"]
                                nP = T[h]["P%d" % (c["pi"] ^ 1)]
                                pp, pp_b = pf_p.get()
                                E.op("pe", lambda e: e.matmul(pp[0:L, 0:L], lhsT=XT[0:L, 0:L], rhs=P[0:L, 0:L], start=True, stop=True),
                                     reads=[XT_b, P_b], writes=[pp_b])
                                E.op("dve", lambda e: e.tensor_tensor(out=nP[0][0:L, 0:L], in0=pp[0:L, 0:L], in1=P[0:L, 0:L], op=ALU.add),
                                     reads=[pp_b, P_b], writes=[nP[1]])
                                c["P"] = nP
                                c["pi"] ^= 1
                        for h in heads:
                            P, P_b = cur[h]["P"]
                            pu, pu_b = pf_p.get()
                            vb, vb_b = T[h]["vb"]
                            E.op("pe", lambda e: e.matmul(pu[0:L, :], lhsT=P[0:L, 0:L], rhs=vb[0:L, :], start=True, stop=True),
                                 reads=[P_b, vb_b], writes=[pu_b])
                            U, U_b = T[h]["U"]
                            E.op("act", lambda e: e.activation(out=U[0:L, :], in_=pu[0:L, :], func=AF.Copy), reads=[pu_b], writes=[U_b])
                            pw, pw_b = pf_p.get()
                            kbe, kbe_b = T[h]["kbe"]
                            E.op("pe", lambda e: e.matmul(pw[:, 0:L], lhsT=kbe[0:L, :], rhs=P[0:L, 0:L], start=True, stop=True),
                                 reads=[P_b, kbe_b], writes=[pw_b])
                            WT, WT_b = T[h]["WT"]
                            E.op("dve", lambda e: e.tensor_copy(out=WT[:, 0:L], in_=pw[:, 0:L]), reads=[pw_b], writes=[WT_b])
                        for h in heads:
                            WT, WT_b = T[h]["WT"]
                            pws, pws_b = pf_p.get()
                            E.op("pe", lambda e: e.matmul(pws[0:L, :], lhsT=WT[:, 0:L], rhs=S_h[:, h, :], start=True, stop=True),
                                 reads=[WT_b, Sh_b[h]], writes=[pws_b])
                            U, U_b = T[h]["U"]
                            vn, vn_b = T[h]["vnew"]
                            E.op("dve", lambda e: e.tensor_tensor(out=vn[0:L, :], in0=U[0:L, :], in1=pws[0:L, :], op=ALU.subtract),
                                 reads=[U_b, pws_b], writes=[vn_b])
                            if need_o:
                                pqs, pqs_b = pf_p.get()
                                E.op("pe", lambda e: e.matmul(pqs[0:L, :], lhsT=QT[h], rhs=S_h[:, h, :], start=True, stop=True),
                                     reads=[qkv_b, Sh_b[h]], writes=[pqs_b])
                                t1, t1_b = T[h]["t1"]
                                E.op("act", lambda e: e.activation(out=t1[0:L, :], in_=pqs[0:L, :], func=AF.Copy, scale=col(c_eG, h)),
                                     reads=[pqs_b, sc_b], writes=[t1_b])
                                pqk, pqk_b = pf_p.get()
                                E.op("pe", lambda e: e.matmul(pqk[0:L, 0:L], lhsT=KT[h], rhs=QT[h], start=True, stop=True),
                                     reads=[qkv_b], writes=[pqk_b])
                                aq, aq_b = T[h]["AqkT"]
                                gT_, gT_b = T[h]["gamT"]
                                E.op("dve", lambda e: e.tensor_tensor(out=aq[0:L, 0:L], in0=pqk[0:L, 0:L], in1=gT_[0:L, 0:L], op=ALU.mult),
                                     reads=[pqk_b, gT_b], writes=[aq_b])
                        for h in heads:
                            vn, vn_b = T[h]["vnew"]
                            if need_o:
                                aq, aq_b = T[h]["AqkT"]
                                pav, pav_b = pf_p.get()
                                E.op("pe", lambda e: e.matmul(pav[0:L, :], lhsT=aq[0:L, 0:L], rhs=vn[0:L, :], start=True, stop=True),
                                     reads=[aq_b, vn_b], writes=[pav_b])
                                t1, t1_b = T[h]["t1"]
                                E.op("dve", lambda e: e.tensor_tensor(out=osb[0:L, h, :], in0=pav[0:L, :], in1=t1[0:L, :], op=ALU.add),
                                     reads=[pav_b, t1_b], writes=[osb_bs[h]])
                            kt, kt_b = T[h]["kt"]
                            psu, psu_b = pf_p.get()
                            E.op("pe", lambda e: e.matmul(psu[:, :], lhsT=kt[0:L, :], rhs=vn[0:L, :], start=True, stop=True),
                                 reads=[kt_b, vn_b], writes=[psu_b])
                            E.op("dve", lambda e: e.scalar_tensor_tensor(out=S_f[:, h, :], in0=S_f[:, h, :],
                                                                         scalar=sc[:, c_eGL.start + h:c_eGL.start + h + 1],
                                                                         in1=psu[:, :], op0=ALU.mult, op1=ALU.add),
                                 reads=[psu_b, sc_b, S_b[h]], writes=[S_b[h]])
                            E.op("pool", lambda e: e.tensor_copy(out=S_h[:, h, :], in_=S_f[:, h, :]), reads=[S_b[h]], writes=[Sh_b[h]])
                    if need_o:
                        E.op("act", lambda e: e.activation(out=sq_t[0:L], in_=osb[0:L], func=AF.Square), reads=osb_bs, writes=[sq_tb])
                        rst, rst_b = rst_p.get()
                        E.op("dve", lambda e: e.tensor_reduce(out=rst[0:L, 0:HB], in_=sq_t[0:L], axis=AX.X, op=ALU.add),
                             reads=[sq_tb], writes=[rst_b])
                        E.op("act", lambda e: e.activation(out=rst[0:L, 0:HB], in_=rst[0:L, 0:HB], func=AF.Sqrt, scale=1.0 / 128,
                                                           bias=eps_sb[0:L, 0:1]), reads=[rst_b, eps_b], writes=[rst_b])
                        E.op("dve", lambda e: e.reciprocal(out=rst[0:L, 0:HB], in_=rst[0:L, 0:HB]), reads=[rst_b], writes=[rst_b])
                        E.op("dve", lambda e: e.tensor_tensor(out=sq_t[0:L], in0=osb[0:L],
                                                              in1=rst[0:L, 0:HB].unsqueeze(2).to_broadcast([L, HB, 128]), op=ALU.mult),
                             reads=osb_bs + [rst_b], writes=[sq_tb])
                        E.op("pool", lambda e: e.tensor_tensor(out=sq_t[0:L], in0=sq_t[0:L],
                                                               in1=nb_sb[0:L, :].unsqueeze(1).to_broadcast([L, HB, 128]), op=ALU.mult),
                             reads=[sq_tb, nb_b], writes=[sq_tb])
                        yb, yb_b = yb_p.get()
                        E.op("dve", lambda e: e.tensor_tensor(out=yb[0:L], in0=sq_t[0:L],
                                                              in1=zs[0:L, :].rearrange("p (h d) -> p h d", d=128), op=ALU.mult),
                             reads=[sq_tb, zs_b], writes=[yb_b])
                        yT, yT_bs = yT_p.get()
                        for h in range(HB):
                            pt_, pt_b = pb_p.get()
                            E.op("pe", lambda e: e.transpose(pt_[:, 0:L], yb[0:L, h, :], ident_bf[0:L, 0:L]), reads=[yb_b, cbf_b], writes=[pt_b])
                            E.op("act", lambda e: e.activation(out=yT[:, h, 0:L], in_=pt_[:, 0:L], func=AF.Copy), reads=[pt_b], writes=[yT_bs[h]])
                        E.dma("pool", dram_view(yT_s, WA * OT + orow0, [[OT, 128], [128 * OT, HB], [1, L]]), yT[:, :, 0:L],
                              reads=yT_bs, writes=[yTs_b])

                E.op("pool", lambda e: e.memset(S_f[:], 0.0), writes=S_b)
                E.op("pool", lambda e: e.memset(S_h[:], 0.0), writes=Sh_b)
                for t0 in range(0, WL, 128):
                    need = t0 >= WL - OWN
                    chunk(t0, 128, need, t0 - (WL - OWN))
                E.dma("sp", ssm_o.ap().rearrange("(h p) d -> p h d", p=128), S_f[:], reads=S_b, is_output=True)
                for b in range(SB):
                    E.dma("sp", S_f[:], ssm[b * HB * 128:(b + 1) * HB * 128, :].rearrange("(h p) d -> p h d", p=128), writes=S_b)
                    E.op("pool", lambda e: e.tensor_copy(out=S_h[:], in_=S_f[:]), reads=S_b, writes=Sh_b)
                    chunk(WL + 32 * b, 32, True, OWN + 32 * b)
                    E.dma("sp", ssm_so[b * HB * 128:(b + 1) * HB * 128, :].rearrange("(h p) d -> p h d", p=128), S_f[:],
                          reads=S_b, is_output=True)

        if 3 in phases:
            phase3()
            E.barrier()


        def phase4():
            with ExitStack() as st:
                pnb = sb(st, "pnb", [128, D], F32)
                pnb_b = Buf("pnb", const=True)
                E.dma("sp", pnb[:], dram_view(postn, 0, [[0, 128], [1, D]]), writes=[pnb_b])
                TG = 256
                YT_p = Pool([sb(st, "YTg%d" % i, [128, KC, TG], BF16) for i in range(1)], "YTg")
                wo_p = Pool([sb(st, "wo%d" % i, [128, KC, 512], BF16) for i in range(2)], "wo")
                yacc = sb(st, "yacc", [128, TG // 128, D], F32)
                yacc_b = [Buf("yacc%d" % i) for i in range(4)]
                ssq = sb(st, "ssq4", [128, 4, D // 512], F32)
                ssq_b = [Buf("ssq4_%d" % i) for i in range(4)]
                junk = sb(st, "junk4", [128, 512], F32)
                junk_b = Buf("junk4")
                r4 = sb(st, "r4", [128, 4, 2], F32)
                xo_p = Pool([sb(st, "xo%d" % i, [128, D], F32) for i in range(2)], "xo")
                mm_p = Pool([ps(st, "m4_%d" % i, [128, 512]) for i in range(4)], "m4", psum=True)
                NCB = D // 512
                if os.environ.get("P4ZERO"):
                    zt = sb(st, "zt", [128, OT], BF16)
                    zt_b = Buf("zt")
                    E.op("dve", lambda e: e.memset(zt[:], 0.0), writes=[zt_b])
                    for kc in range(KC):
                        E.dma("sp", yT_s[kc * 128:(kc + 1) * 128, :], zt[:], reads=[zt_b], writes=[yTs_b])
                for g0 in range(0, OT, TG):
                    gw = min(TG, OT - g0)
                    nsub = gw // 128
                    YT, YT_b = YT_p.get()
                    E.dma("sp", YT[:, :, 0:gw], dram_view(yT_s, g0, [[OT, 128], [128 * OT, KC], [1, gw]]), reads=[yTs_b], writes=[YT_b])
                    for cb in range(NCB):
                        wo, wo_b = wo_p.get()
                        E.dma("pool", wo[:], dram_view(wout, cb * 512, [[D, 128], [128 * D, KC], [1, 512]]), writes=[wo_b])
                        for sub in range(nsub):
                            pt, pt_b = mm_p.get()
                            for kc in range(KC):
                                E.op("pe", lambda e: e.matmul(pt[:], lhsT=YT[:, kc, sub * 128:(sub + 1) * 128], rhs=wo[:, kc, :],
                                                              start=(kc == 0), stop=(kc == KC - 1)), reads=[YT_b, wo_b], writes=[pt_b])
                            E.op("dve", lambda e: e.tensor_copy(out=yacc[:, sub, cb * 512:(cb + 1) * 512], in_=pt[:]),
                                 reads=[pt_b], writes=[yacc_b[sub]])
                            E.op("act", lambda e: e.activation(out=junk[:], in_=pt[:], func=AF.Square, accum_out=ssq[:, sub, cb:cb + 1]),
                                 reads=[pt_b], writes=[junk_b, ssq_b[sub]])
                    P4CUT = int(os.environ.get("P4CUT", "9"))
                    for sub in range(nsub):
                        if P4CUT < 2:
                            break
                        r0 = g0 + sub * 128
                        xo, xo_b = xo_p.get()
                        E.dma("sp", xo[:], xown[r0:r0 + 128, :], writes=[xo_b])
                        E.op("dve", lambda e: e.tensor_reduce(out=r4[:, sub, 0:1], in_=ssq[:, sub, :], axis=AX.X, op=ALU.add),
                             reads=[ssq_b[sub]], writes=[ssq_b[sub]])
                        E.op("act", lambda e: e.activation(out=r4[:, sub, 1:2], in_=r4[:, sub, 0:1], func=AF.Sqrt, scale=1.0 / D,
                                                           bias=eps_sb[:, 0:1]), reads=[ssq_b[sub], eps_b], writes=[ssq_b[sub]])
                        E.op("dve", lambda e: e.reciprocal(out=r4[:, sub, 1:2], in_=r4[:, sub, 1:2]), reads=[ssq_b[sub]], writes=[ssq_b[sub]])
                        if P4CUT < 3:
                            continue
                        E.op("dve", lambda e: e.scalar_tensor_tensor(out=yacc[:, sub, :], in0=yacc[:, sub, :], scalar=r4[:, sub, 1:2],
                                                                     in1=pnb[:], op0=ALU.mult, op1=ALU.mult),
                             reads=[yacc_b[sub], ssq_b[sub], pnb_b], writes=[yacc_b[sub]])
                        if P4CUT < 4:
                            continue
                        E.op("pool", lambda e: e.tensor_tensor(out=xo[:], in0=xo[:], in1=yacc[:, sub, :], op=ALU.add),
                             reads=[xo_b, yacc_b[sub]], writes=[xo_b])
                        if P4CUT < 5:
                            continue
                        E.dma("sp", y_o[r0:r0 + 128, :], xo[:], reads=[xo_b], is_output=True)

        if 4 in phases:
            phase4()
            E.barrier()

        E.finish()
        global LAST_E
        LAST_E = E
    return nc


def prep_inputs(cfg, inp):
    D, NC, OWN, SB, WL, TL, HA, HB, WA, WB = cfg.D, cfg.NC, cfg.OWN, cfg.SB, cfg.WL, cfg.TL, cfg.HA, cfg.HB, cfg.WA, cfg.WB
    f = np.float32
    w = np.asarray(inp["w_in"], f)[0]
    wF = np.ascontiguousarray(np.concatenate([w[:, WA:2 * WA], w[:, 4 * WA:4 * WA + 3 * WB], w[:, 0:WA]], axis=1))
    o = 4 * WA + 4 * WB
    wT = np.ascontiguousarray(np.concatenate([w[:, 2 * WA:3 * WA], w[:, o:o + 2 * HB], w[:, 3 * WA:4 * WA],
                                              w[:, 4 * WA + 3 * WB:4 * WA + 4 * WB]], axis=1))
    wout = np.ascontiguousarray(np.asarray(inp["w_out"], f)[0])
    pn = np.ascontiguousarray(np.asarray(inp["pre_norm"], f)[0].reshape(cfg.KC, 128).T)
    postn = np.ascontiguousarray(np.asarray(inp["post_norm"], f)[0][None])
    convw = np.ascontiguousarray(np.asarray(inp["conv_b"], f)[0].reshape(4, 3 * HB, 128).transpose(2, 1, 0)).reshape(128, -1)
    gconst = np.concatenate([np.asarray(inp["a_log_b"], f)[0], np.asarray(inp["dt_bias_b"], f)[0]])[None]
    relb = np.ascontiguousarray(np.asarray(inp["rel_bias"], f))
    lamv = np.concatenate([np.asarray(inp[k], f)[0] for k in ("lambda_q1", "lambda_k1", "lambda_q2", "lambda_k2")])[None]
    subln = np.asarray(inp["subln_a"], f)[0][None]
    normb = np.asarray(inp["norm_b"], f)[0][None]
    xp = np.asarray(inp["x_prompt"], f)[0]
    xs = np.asarray(inp["x_sample"], f)
    meta = np.asarray(inp["meta_tokens"], f)
    ck = np.asarray(inp["cache_k_a"], f)[0]
    cvv = np.asarray(inp["cache_v_a"], f)[0]
    ssm = np.asarray(inp["state_ssm_b"], f)[0]
    cs = np.asarray(inp["state_conv_b"], f)[0]
    cstc = make_cst()
    ohc = make_oh()
    dmc = make_dmask()
    maps = []
    for c in range(NC):
        pad = OWN * (NC - 1 - c)
        xa = np.zeros((TL, D), f)
        xa[pad + 112:pad + 128] = meta
        xa[pad + 128:WL] = xp[0:(c + 1) * OWN]
        xa[WL:] = xs[c * SB:(c + 1) * SB].reshape(-1, D)
        m = {
            "xT": np.ascontiguousarray(xa.T),
            "xown": np.ascontiguousarray(xa[WL - OWN:]),
            "wF": wF, "wT": wT, "wout": wout, "pn": pn, "postn": postn, "convw": convw,
            "convst": np.ascontiguousarray(cs[c * SB:(c + 1) * SB].reshape(SB, 3, 3 * HB, 128).transpose(3, 2, 0, 1)).reshape(128, -1),
            "gconst": gconst, "relb": relb, "lamv": lamv, "subln": subln, "normb": normb,
            "ckT": np.ascontiguousarray(ck[c * SB:(c + 1) * SB].transpose(0, 2, 3, 4, 1)).reshape(-1, cfg.CL),
            "cv": np.ascontiguousarray(cvv[c * SB:(c + 1) * SB]).reshape(SB * cfg.CL, WA),
            "ssm": np.ascontiguousarray(ssm[c * SB:(c + 1) * SB]).reshape(-1, 128),
            "cst": cstc,
            "valid": np.ascontiguousarray((np.arange(WL).reshape(-1, 128).T >= pad + 112).astype(f)),
            "oh": ohc, "dmask": dmc,
        }
        maps.append(m)
    return maps


def assemble(cfg, res):
    D, NC, OWN, SB, HA, HB, WA, WB, SEQ = cfg.D, cfg.NC, cfg.OWN, cfg.SB, cfg.HA, cfg.HB, cfg.WA, cfg.WB, cfg.SEQ
    f = np.float32
    DECB, DS = cfg.DECB, cfg.DECS
    y_p = np.zeros((1, SEQ, D), f)
    y_s = np.zeros((DECB, DS, D), f)
    k_p = np.zeros((1, 1, 16 + SEQ, HA, 2, 128), f)
    v_p = np.zeros((1, 1, 16 + SEQ, HA, 256), f)
    k_s = np.zeros((1, DECB, DS, HA, 2, 128), f)
    v_s = np.zeros((1, DECB, DS, HA, 256), f)
    c_s = np.zeros((1, DECB, 3, 3 * WB), f)
    s_s = np.zeros((1, DECB, HB, 128, 128), f)
    for c in range(NC):
        r = res[c]
        y = np.asarray(r["y_o"])
        y_p[0, c * OWN:(c + 1) * OWN] = y[0:OWN]
        y_s[c * SB:(c + 1) * SB] = y[OWN:].reshape(SB, DS, D)
        kt = np.asarray(r["kT_o"]).reshape(HA, 2, 128, -1).transpose(3, 0, 1, 2)
        vt = np.asarray(r["v_o"]).reshape(-1, HA, 256)
        if c == 0:
            k_p[0, 0, 0:16] = kt[112:128]
            v_p[0, 0, 0:16] = vt[112:128]
        k_p[0, 0, 16 + c * OWN:16 + (c + 1) * OWN] = kt[128:128 + OWN]
        v_p[0, 0, 16 + c * OWN:16 + (c + 1) * OWN] = vt[128:128 + OWN]
        k_s[0, c * SB:(c + 1) * SB] = kt[128 + OWN:].reshape(SB, DS, HA, 2, 128)
        v_s[0, c * SB:(c + 1) * SB] = vt[128 + OWN:].reshape(SB, DS, HA, 256)
        c_s[0, c * SB:(c + 1) * SB] = np.asarray(r["conv_so"]).reshape(3 * HB, 128, SB, 3).transpose(2, 3, 0, 1).reshape(SB, 3, 3 * WB)
        s_s[0, c * SB:(c + 1) * SB] = np.asarray(r["ssm_so"]).reshape(SB, HB, 128, 128)
    rl = res[NC - 1]
    s_p = np.asarray(rl["ssm_o"]).reshape(1, 1, HB, 128, 128).astype(f)
    c_p = np.asarray(rl["conv_o"]).reshape(3 * HB, 128, 3).transpose(2, 0, 1).reshape(1, 1, 3, 3 * WB).astype(f)
    return (y_p, y_s, k_p, v_p, s_p, c_p, k_s, v_s, s_s, c_s)


_NC_CACHE = {}


def kernel(**inputs):
    cfg = Cfg()
    if "nc" not in _NC_CACHE:
        _NC_CACHE["nc"] = build(cfg)
    nc = _NC_CACHE["nc"]
    maps = prep_inputs(cfg, inputs)
    res = run_bass_kernel_spmd(nc, maps, core_ids=list(range(cfg.NC)))
    return assemble(cfg, res.results)
```

```python
from contextlib import ExitStack
import math
import numpy as np
import concourse.bass as bass
import concourse.mybir as mybir
from concourse.bass_utils import run_bass_kernel_spmd

F32 = mybir.dt.float32
BF16 = mybir.dt.bfloat16
AF = mybir.ActivationFunctionType
ALU = mybir.AluOpType
AX = mybir.AxisListType
EPS = 1e-6
N_BUCKETS = 32
MAX_DISTANCE = 1024
CHUNK = 64
LAM_INIT = 0.8 - 0.6 * math.exp(0.0)
import os
GDN_FP32 = os.environ.get("GDN_FP32", "1") == "1"
LAST_E = None


class Cfg:
    def __init__(self, D=4096, NC=8, SEQ=16384, DECB=32, DECS=32, PAST=1024):
        self.D, self.NC, self.SEQ, self.DECB, self.DECS, self.PAST = D, NC, SEQ, DECB, DECS, PAST
        self.NMETA = 16
        self.WA = D // 2
        self.WB = D - self.WA
        self.HA = self.WA // 256
        self.HB = self.WB // 128
        self.INC = 4 * self.WA + 4 * self.WB + 2 * self.HB
        self.KC = D // 128
        self.OWN = SEQ // NC
        self.WL = 128 + SEQ
        self.SB = DECB // NC
        self.ST = self.SB * DECS
        assert self.ST == 128 and DECS == 32
        self.TL = self.WL + self.ST
        self.CL = self.NMETA + PAST
        self.OT = self.OWN + self.ST
        self.OK0 = self.WL - self.OWN - 128
        assert self.OK0 % 512 == 0 and self.WL % 512 == 128
        self.NF_W = 2 * self.HA + 3 * self.HB
        self.NF = self.NF_W + 2 * self.HA
        self.NT_W = self.WA + 2 * self.HB
        self.NT = self.NT_W + self.WA + self.WB


class Buf:
    __slots__ = ("name", "w", "r", "const", "psum")

    def __init__(self, name, const=False, psum=False):
        self.name = name
        self.w = None
        self.r = []
        self.const = const
        self.psum = psum


class Emitter:
    ENG = ("pe", "act", "dve", "pool", "sp")

    def __init__(self, nc, es, n_dma_sems=48):
        self.nc = nc
        self.eng = {"pe": nc.tensor, "act": nc.scalar, "dve": nc.vector, "pool": nc.gpsimd, "sp": nc.sync}
        self.psem = {e: es.enter_context(nc.semaphore("P_" + e)) for e in ("pe", "act", "dve", "pool")}
        self.pcnt = {e: 0 for e in self.psem}
        self.dsem = [es.enter_context(nc.semaphore("D%d" % i)) for i in range(n_dma_sems)]
        self.dval = [0] * n_dma_sems
        self.dnext = 0
        nq = n_dma_sems // 2
        self.qrange = {"sp": (0, nq), "pool": (nq, n_dma_sems), "act": (0, nq)}
        self.qnext = {"sp": 0, "pool": 0, "act": 0}
        self.waited = {}
        self.out_events = []

    def _need(self, waiter, ev):
        if ev is None:
            return None
        kind, src, val = ev
        if kind == "e" and src == waiter and waiter == "pe":
            return None
        key = (waiter, kind, src)
        if self.waited.get(key, 0) >= val:
            return None
        return key, val

    def _do_waits(self, waiter, evs):
        best = {}
        for ev in evs:
            n = self._need(waiter, ev)
            if n is not None:
                key, val = n
                if best.get(key, 0) < val:
                    best[key] = val
        for key, val in best.items():
            _, kind, src = key
            sem = self.psem[src] if kind == "e" else self.dsem[src]
            self.eng[waiter].wait_ge(sem, val)
            self.waited[key] = val

    def _deps(self, reads, writes, waiter=None):
        evs = []
        for b in reads:
            evs.append(b.w)
            if b.psum:
                evs.extend(ev for ev in b.r if not (ev[0] == "e" and ev[1] == waiter))
        for b in writes:
            evs.append(b.w)
            evs.extend(b.r)
        return evs

    def op(self, e, fn, reads=(), writes=()):
        self._do_waits(e, self._deps(reads, writes, e))
        ins = fn(self.eng[e])
        self.pcnt[e] += 1
        ins.then_inc(self.psem[e], 1)
        ev = ("e", e, self.pcnt[e])
        for b in reads:
            if not b.const:
                b.r.append(ev)
        for b in writes:
            b.w = ev
            b.r = []
        return ev

    def dma(self, q, out, in_, reads=(), writes=(), is_output=False, **kw):
        lo, hi = self.qrange[q]
        qk = "sp" if q == "act" else q
        i = lo + self.qnext[qk]
        self.qnext[qk] = (self.qnext[qk] + 1) % (hi - lo)
        evs = self._deps(reads, writes)
        if self.dval[i] > 0:
            evs.append(("d", i, self.dval[i]))
        self._do_waits(q, evs)
        ins = self.eng[q].dma_start(out=out, in_=in_, **kw)
        self.dval[i] += 16
        ins.then_inc(self.dsem[i], 16)
        ev = ("d", i, self.dval[i])
        for b in reads:
            if not b.const:
                b.r.append(ev)
        for b in writes:
            b.w = ev
            b.r = []
        if is_output:
            self.out_events.append(ev)
        return ev

    def barrier(self):
        evs = []
        for i, v in enumerate(self.dval):
            if v > 0:
                evs.append(("d", i, v))
        for e in self.psem:
            if self.pcnt[e] > 0:
                evs.append(("e", e, self.pcnt[e]))
        for w in ("pe", "act", "dve", "pool", "sp"):
            self._do_waits(w, [ev for ev in evs if not (ev[0] == "e" and ev[1] == w)])

    def finish(self):
        evs = list(self.out_events)
        for i, v in enumerate(self.dval):
            if v > 0:
                evs.append(("d", i, v))
        for e in self.psem:
            if self.pcnt[e] > 0:
                evs.append(("e", e, self.pcnt[e]))
        self._do_waits("sp", evs)


class Pool:
    def __init__(self, tiles, name, nsub=0, psum=False):
        self.tiles = tiles
        if nsub:
            self.bufs = [[Buf("%s%d_%d" % (name, i, j)) for j in range(nsub)] for i in range(len(tiles))]
        else:
            self.bufs = [Buf("%s%d" % (name, i), psum=psum) for i in range(len(tiles))]
        self.i = 0

    def get(self):
        t, b = self.tiles[self.i], self.bufs[self.i]
        self.i = (self.i + 1) % len(self.tiles)
        return t, b


def dram_view(t, offset, pattern):
    return bass.AP(t, offset, pattern)


def rel_bucket_np(rel):
    nb = N_BUCKETS // 2
    max_exact = nb // 2
    n = np.abs(rel)
    nf = np.maximum(n, 1).astype(np.float32)
    large = max_exact + (np.log(nf / np.float32(max_exact)) / np.float32(math.log(MAX_DISTANCE / max_exact))
                         * np.float32(nb - max_exact)).astype(np.int32)
    large = np.minimum(large, nb - 1)
    return np.where(rel > 0, nb, 0) + np.where(n < max_exact, n, large)


C_ID, C_AID, C_TRIU, C_LINC, C_LSTR, C_ONE = 0, 128, 256, 384, 512, 640
CST_W = 768
BV_R0 = 511
BV_LEN = 1664
NEAR_D = [-640, -512, -384, -256, -128, 0, 128, 256, 384]


def make_oh():
    i = np.arange(BV_LEN)
    bk = rel_bucket_np((BV_R0 - i).astype(np.int32))
    o = np.zeros((N_BUCKETS, BV_LEN), np.float32)
    o[bk, i] = 1.0
    return o


def make_dmask():
    m = np.zeros((128, 4, 512), np.float32)
    p = np.arange(128)[:, None]
    j = np.arange(512)[None, :]
    for a in range(4):
        m[:, a, :] = ((128 * a + p) // 64) <= (j // 64)
    return m.reshape(128, 2048)


def make_cst():
    c = np.zeros((128, CST_W), np.float32)
    i = np.arange(128)
    c[:, C_ID:C_ID + 128] = np.eye(128)
    c[:, C_AID:C_AID + 128] = np.eye(128)[::-1]
    c[:, C_TRIU:C_TRIU + 128] = (i[:, None] <= i[None, :])
    c[:, C_LINC:C_LINC + 128] = (i[:, None] >= i[None, :])
    c[:, C_LSTR:C_LSTR + 128] = (i[:, None] > i[None, :])
    c[:, C_ONE:C_ONE + 128] = 1.0
    return c


def build(cfg, phases=(0, 1, 2, 3, 4)):
    nc = bass.Bass("TRN2", target_bir_lowering=False)
    D, KC, TL, WL, OWN, OT, OK0 = cfg.D, cfg.KC, cfg.TL, cfg.WL, cfg.OWN, cfg.OT, cfg.OK0
    HA, HB, WA, WB, SB, ST, CL = cfg.HA, cfg.HB, cfg.WA, cfg.WB, cfg.SB, cfg.ST, cfg.CL
    KO = TL - OK0

    def din(name, shape, dt=F32):
        return nc.dram_tensor(name, list(shape), dt, kind="ExternalInput")

    def dout(name, shape, dt=F32):
        return nc.dram_tensor(name, list(shape), dt, kind="ExternalOutput")

    def dscr(name, shape, dt):
        return nc.dram_tensor(name, list(shape), dt)

    xT = din("xT", [D, TL])
    xown = din("xown", [OT, D])
    wF = din("wF", [D, cfg.NF * 128])
    wT = din("wT", [D, cfg.NT])
    wout = din("wout", [D, D])
    pn = din("pn", [128, KC])
    postn = din("postn", [1, D])
    convw = din("convw", [128, 3 * HB * 4])
    convst = din("convst", [128, 3 * HB * SB * 3])
    gconst = din("gconst", [1, 2 * HB])
    relb = din("relb", [N_BUCKETS, HA])
    lamv = din("lamv", [1, 4 * 128])
    subln = din("subln", [1, 256])
    normb = din("normb", [1, 128])
    ckT = din("ckT", [SB * 2 * HA * 128, CL])
    cv = din("cv", [SB * CL, WA])
    ssm = din("ssm", [SB * HB * 128, 128])
    cst = din("cst", [128, CST_W])
    NKT = WL // 128
    valid = din("valid", [128, NKT])
    oh = din("oh", [N_BUCKETS, BV_LEN])
    dmask = din("dmask", [128, 4 * 512])
    bv_s = dscr("bv_s", [HA, BV_LEN], F32)
    y_o = dout("y_o", [OT, D])
    kT_o = dout("kT_o", [2 * HA * 128, KO])
    v_o = dout("v_o", [KO, WA])
    ssm_o = dout("ssm_o", [HB * 128, 128])
    ssm_so = dout("ssm_so", [SB * HB * 128, 128])
    conv_o = dout("conv_o", [3 * HB * 128, 3])
    conv_so = dout("conv_so", [3 * HB * 128, SB * 3])
    xnT_s = dscr("xnT_s", [128, KC * TL], BF16)
    kT_s = dscr("kT_s", [2 * HA * 128, TL], BF16)
    v_s = dscr("v_s", [TL, WA], BF16)
    gT_s = dscr("gT_s", [3 * HB * 128, TL], BF16)
    gb_s = dscr("gb_s", [TL, 2 * HB], F32)
    qT_s = dscr("qT_s", [2 * HA * 128, OT], BF16)
    zs_s = dscr("zs_s", [OT, WA + WB], BF16)
    yT_s = dscr("yT_s", [D, OT], BF16)

    es = ExitStack()
    with es:
        E = Emitter(nc, es)

        def sb(stack, name, shape, dt):
            return stack.enter_context(nc.sbuf_tensor(name, list(shape), dt))

        def ps(stack, name, shape, dt=F32):
            return stack.enter_context(nc.psum_tensor(name, list(shape), dt))

        cst_sb = sb(es, "cst_sb", [128, CST_W], F32)
        cst_b = Buf("cst", const=True)
        E.dma("sp", cst_sb[:], cst.ap(), writes=[cst_b])
        cbf = sb(es, "cbf", [128, CST_W], BF16)
        cbf_b = Buf("cbf", const=True)
        E.op("dve", lambda e: e.tensor_copy(out=cbf[:], in_=cst_sb[:]), reads=[cst_b], writes=[cbf_b])
        onesD = sb(es, "onesD", [128, 128], BF16)
        onesD_b = Buf("onesD", const=True)
        E.op("dve", lambda e: e.tensor_scalar(out=onesD[:], in0=cst_sb[:, C_ONE:C_ONE + 128], scalar1=1.0 / D,
                                              scalar2=None, op0=ALU.mult), reads=[cst_b], writes=[onesD_b])
        eps_sb = sb(es, "eps_sb", [128, 1], F32)
        eps_b = Buf("eps", const=True)
        E.op("dve", lambda e: e.memset(eps_sb[:], EPS), writes=[eps_b])
        pn_sb = sb(es, "pn_sb", [128, KC], F32)
        pn_b = Buf("pn", const=True)
        E.dma("sp", pn_sb[:], pn.ap(), writes=[pn_b])

        xns_b = [Buf("xns%d" % i) for i in range(TL // 256)]

        def xns_deps(t0, tw):
            return [xns_b[i] for i in range(t0 // 256, (t0 + tw - 1) // 256 + 1)]

        kTs_b = Buf("kT_s")
        vs_b = Buf("v_s")
        gTs_b = Buf("gT_s")
        gbs_b = Buf("gb_s")
        qTs_b = Buf("qT_s")
        zss_b = Buf("zs_s")
        yTs_b = Buf("yT_s")

        convw_sb = sb(es, "convw_sb", [128, 3 * HB, 4], F32)
        convst_sb = sb(es, "convst_sb", [128, 3 * HB, SB, 3], F32)
        gc_sb = sb(es, "gc_sb", [128, 2 * HB], F32)
        nA_sb = sb(es, "nA_sb", [128, HB], F32)
        ones_bf = cbf[:, C_ONE:C_ONE + 128]
        ident_bf = cbf[:, C_ID:C_ID + 128]
        smallc_b = Buf("smallc", const=True)
        E.dma("sp", convw_sb[:], convw.ap(), writes=[smallc_b])
        E.dma("sp", convst_sb[:], convst.ap(), writes=[smallc_b])
        E.dma("sp", gc_sb[:], dram_view(gconst, 0, [[0, 128], [1, 2 * HB]]), writes=[smallc_b])
        nA_b = Buf("nA", const=True)
        E.op("act", lambda e: e.activation(out=nA_sb[:], in_=gc_sb[:, 0:HB], func=AF.Exp),
             reads=[smallc_b], writes=[nA_b])
        E.op("dve", lambda e: e.tensor_scalar(out=nA_sb[:], in0=nA_sb[:], scalar1=-1.0, scalar2=None, op0=ALU.mult),
             reads=[nA_b], writes=[nA_b])
        smallc_b.w = None if False else smallc_b.w

        def phase0():
            with ExitStack() as st:
                TB = 256
                xf_p = Pool([sb(st, "xf%d" % i, [128, KC, TB], F32) for i in range(2)], "xf")
                sq_p = Pool([sb(st, "sq%d" % i, [128, KC, TB], BF16) for i in range(1)], "sq")
                xn_p = Pool([sb(st, "xn%d" % i, [128, KC, TB], BF16) for i in range(2)], "xn", nsub=KC)
                rs_p = Pool([sb(st, "rs%d" % i, [128, TB], F32) for i in range(2)], "rs")
                ss_p = Pool([ps(st, "ss%d" % i, [128, 512]) for i in range(2)], "ss", psum=True)
                for blk in range(TL // TB):
                    t0 = blk * TB
                    xf, xf_b = xf_p.get()
                    E.dma("sp", xf[:], dram_view(xT, t0, [[TL, 128], [128 * TL, KC], [1, TB]]), writes=[xf_b])
                    sq, sq_b = sq_p.get()
                    E.op("act", lambda e: e.activation(out=sq[:], in_=xf[:], func=AF.Square),
                         reads=[xf_b], writes=[sq_b])
                    ss, ss_b = ss_p.get()
                    for kc in range(KC):
                        E.op("pe", lambda e: e.matmul(ss[:, 0:TB], lhsT=onesD[:], rhs=sq[:, kc, :],
                                                      start=(kc == 0), stop=(kc == KC - 1)),
                             reads=[onesD_b, sq_b], writes=[ss_b])
                    rs, rs_b = rs_p.get()
                    E.op("act", lambda e: e.activation(out=rs[:], in_=ss[:, 0:TB], func=AF.Sqrt, bias=eps_sb[:, 0:1]),
                         reads=[ss_b, eps_b], writes=[rs_b])
                    E.op("dve", lambda e: e.reciprocal(out=rs[:], in_=rs[:]), reads=[rs_b], writes=[rs_b])
                    xn, xn_bs = xn_p.get()
                    for kc in range(KC):
                        E.op("dve", lambda e: e.scalar_tensor_tensor(out=xn[:, kc, :], in0=xf[:, kc, :],
                                                                   scalar=pn_sb[:, kc:kc + 1], in1=rs[:],
                                                                   op0=ALU.mult, op1=ALU.mult),
                             reads=[xf_b, rs_b, pn_b], writes=[xn_bs[kc]])
                    E.dma("pool", dram_view(xnT_s, t0, [[KC * TL, 128], [TL, KC], [1, TB]]), xn[:],
                          reads=xn_bs, writes=[xns_b[blk]])

        if 0 in phases:
            phase0()
            E.barrier()

        def phase1():
            with ExitStack() as st:
                WGW = 1088
                Wg = sb(st, "Wg", [128, KC, WGW], BF16)
                Wg_b = Buf("Wg")
                xnb_p = Pool([sb(st, "xnb%d" % i, [128, KC, 512], BF16) for i in range(2)], "xnb")
                mm_p = Pool([ps(st, "mm%d" % i, [128, 512]) for i in range(5)], "mm", psum=True)
                ss_p = Pool([ps(st, "ssq%d" % i, [128, 512]) for i in range(2)], "ssq", psum=True)
                f32_p = Pool([sb(st, "ev%d" % i, [128, 512], F32) for i in range(4)], "ev")
                ext_p = Pool([sb(st, "ext%d" % i, [128, 520], F32) for i in range(4)], "ext")
                ext2_p = Pool([sb(st, "ext2_%d" % i, [128, SB, 36], F32) for i in range(3)], "ext2")
                yc_p = Pool([sb(st, "yc%d" % i, [128, 512], F32) for i in range(3)], "yc")
                ys_p = Pool([sb(st, "ys%d" % i, [128, 512], F32) for i in range(5)], "ys")
                sqb_p = Pool([sb(st, "sqb%d" % i, [128, 512], BF16) for i in range(2)], "sqb")
                rs_p = Pool([sb(st, "rsq%d" % i, [128, 512], F32) for i in range(2)], "rsq")
                obf_p = Pool([sb(st, "obf%d" % i, [128, 512], BF16) for i in range(4)], "obf")
                gb_p = Pool([sb(st, "gbt%d" % i, [128, 2 * HB], F32) for i in range(2)], "gbt")
                tmp_p = Pool([sb(st, "gtmp%d" % i, [128, HB], F32) for i in range(2)], "gtmp")
                car = sb(st, "car", [128, 3 * HB, 4], F32)
                car_b = [Buf("car%d" % i) for i in range(3 * HB)]
                E.op("pool", lambda e: e.memset(car[:], 0.0), writes=car_b)

                def win_blocks():
                    return [(t0, min(512, TL - t0)) for t0 in range(0, TL, 512)]

                def own_blocks():
                    return [(t0, min(512, TL - t0)) for t0 in range(WL - OWN, TL, 512)]

                def load_W(src, ncols, c0, gw):
                    E.dma("pool", Wg[:, :, 0:gw],
                          dram_view(src, c0, [[ncols, 128], [128 * ncols, KC], [1, gw]]), writes=[Wg_b])

                def load_xn(t0, tw):
                    xnb, xnb_b = xnb_p.get()
                    E.dma("sp", xnb[:, :, 0:tw],
                          dram_view(xnT_s, t0, [[KC * TL, 128], [TL, KC], [1, tw]]),
                          reads=xns_deps(t0, tw), writes=[xnb_b])
                    return xnb, xnb_b

                def mm_F(xnb, xnb_b, tw, j):
                    pt, pt_b = mm_p.get()
                    for kc in range(KC):
                        E.op("pe", lambda e: e.matmul(pt[:, 0:tw], lhsT=Wg[:, kc, j * 128:(j + 1) * 128],
                                                      rhs=xnb[:, kc, 0:tw], start=(kc == 0), stop=(kc == KC - 1)),
                             reads=[Wg_b, xnb_b], writes=[pt_b])
                    return pt, pt_b

                def mm_T(xnb, xnb_b, sub, c0, gw):
                    pt, pt_b = mm_p.get()
                    for kc in range(KC):
                        E.op("pe", lambda e: e.matmul(pt[:, 0:gw], lhsT=xnb[:, kc, sub * 128:(sub + 1) * 128],
                                                      rhs=Wg[:, kc, c0:c0 + gw], start=(kc == 0), stop=(kc == KC - 1)),
                             reads=[Wg_b, xnb_b], writes=[pt_b])
                    return pt, pt_b

                def ev_ka(f, t0, tw, pt, pt_b):
                    kf, kf_b = f32_p.get()
                    E.op("act", lambda e: e.activation(out=kf[:, 0:tw], in_=pt[:, 0:tw], func=AF.Copy),
                         reads=[pt_b], writes=[kf_b])
                    E.dma("pool", kT_s[f * 128:(f + 1) * 128, t0:t0 + tw], kf[:, 0:tw], reads=[kf_b], writes=[kTs_b])
                    if t0 >= OK0:
                        E.dma("sp", kT_o[f * 128:(f + 1) * 128, t0 - OK0:t0 - OK0 + tw], kf[:, 0:tw],
                              reads=[kf_b], is_output=True)

                def ev_qa(f, t0, tw, pt, pt_b):
                    ob, ob_b = obf_p.get()
                    E.op("act", lambda e: e.activation(out=ob[:, 0:tw], in_=pt[:, 0:tw], func=AF.Copy),
                         reads=[pt_b], writes=[ob_b])
                    o0 = t0 - (WL - OWN)
                    E.dma("sp", qT_s[f * 128:(f + 1) * 128, o0:o0 + tw], ob[:, 0:tw], reads=[ob_b], writes=[qTs_b])

                def conv_tail(t, kind, hb, ext_v, n, yc_v, ys_v, wr, t0, s3=None):
                    ext_b, yc_b, ys_b = wr
                    E.op("dve", lambda e: e.tensor_scalar(out=yc_v, in0=ext_v(0), scalar1=convw_sb[:, t, 0:1],
                                                          scalar2=None, op0=ALU.mult),
                         reads=[ext_b, smallc_b], writes=[yc_b])
                    for i in range(1, 4):
                        E.op("dve", lambda e: e.scalar_tensor_tensor(out=yc_v, in0=ext_v(i),
                                                                     scalar=convw_sb[:, t, i:i + 1], in1=yc_v,
                                                                     op0=ALU.mult, op1=ALU.add),
                             reads=[ext_b, smallc_b, yc_b], writes=[yc_b])
                    E.op("act", lambda e: e.activation(out=ys_v, in_=yc_v, func=AF.Silu), reads=[yc_b], writes=[ys_b])

                def norm_store(t, kind, ys2, ys_b, n, t0):
                    ob, ob_b = obf_p.get()
                    if kind == 2:
                        E.op("pool", lambda e: e.tensor_copy(out=ob[:, 0:n], in_=ys2), reads=[ys_b], writes=[ob_b])
                    else:
                        sq, sq_b = sqb_p.get()
                        E.op("act", lambda e: e.activation(out=sq[:, 0:n], in_=ys2, func=AF.Square),
                             reads=[ys_b], writes=[sq_b])
                        ss, ss_b = ss_p.get()
                        E.op("pe", lambda e: e.matmul(ss[:, 0:n], lhsT=ones_bf, rhs=sq[:, 0:n], start=True, stop=True),
                             reads=[cbf_b, sq_b], writes=[ss_b])
                        rs, rs_b = rs_p.get()
                        E.op("act", lambda e: e.activation(out=rs[:, 0:n], in_=ss[:, 0:n], func=AF.Sqrt,
                                                           bias=eps_sb[:, 0:1]), reads=[ss_b, eps_b], writes=[rs_b])
                        E.op("dve", lambda e: e.reciprocal(out=rs[:, 0:n], in_=rs[:, 0:n]), reads=[rs_b], writes=[rs_b])
                        sc = (128 ** -0.5) if kind == 0 else 1.0
                        E.op("dve", lambda e: e.scalar_tensor_tensor(out=ob[:, 0:n], in0=ys2, scalar=sc, in1=rs[:, 0:n],
                                                                     op0=ALU.mult, op1=ALU.mult),
                             reads=[ys_b, rs_b], writes=[ob_b])
                    E.dma("pool", gT_s[t * 128:(t + 1) * 128, t0:t0 + n], ob[:, 0:n], reads=[ob_b], writes=[gTs_b])

                def ev_g(t, t0, tw, pt, pt_b):
                    kind = t // HB
                    nw = tw if t0 + tw <= WL else WL - t0
                    ext, ext_b = ext_p.get()
                    E.op("pool", lambda e: e.tensor_copy(out=ext[:, 0:3], in_=car[:, t, 0:3]),
                         reads=[car_b[t]], writes=[ext_b])
                    E.op("act", lambda e: e.activation(out=ext[:, 3:3 + nw], in_=pt[:, 0:nw], func=AF.Copy),
                         reads=[pt_b], writes=[ext_b])
                    E.op("pool", lambda e: e.tensor_copy(out=car[:, t, 0:3], in_=ext[:, nw:nw + 3]),
                         reads=[ext_b], writes=[car_b[t]])
                    if t0 + nw == WL:
                        E.dma("sp", conv_o[t * 128:(t + 1) * 128, :], ext[:, nw:nw + 3], reads=[ext_b], is_output=True)
                    has_s = nw < tw
                    if has_s:
                        assert tw - nw == ST
                        e2, e2_b = ext2_p.get()
                        E.op("pool", lambda e: e.tensor_copy(out=e2[:, :, 0:3], in_=convst_sb[:, t, :, :]),
                             reads=[smallc_b], writes=[e2_b])
                        E.op("act", lambda e: e.activation(out=e2[:, :, 3:35],
                                                           in_=pt[:, nw:tw].rearrange("p (b s) -> p b s", s=32),
                                                           func=AF.Copy), reads=[pt_b], writes=[e2_b])
                        E.dma("sp", conv_so[t * 128:(t + 1) * 128, :].rearrange("p (b s) -> p b s", s=3),
                              e2[:, :, 32:35], reads=[e2_b], is_output=True)
                    hold = {}

                    def stage_b():
                        yc, yc_b = yc_p.get()
                        ys, ys_b = ys_p.get()
                        conv_tail(t, kind, None, lambda i: ext[:, i:i + nw], nw, yc[:, 0:nw], ys[:, 0:nw],
                                  (ext_b, yc_b, ys_b), t0)
                        hold["ys"] = (ys, ys_b)
                        if has_s:
                            yc2, yc2_b = yc_p.get()
                            ys2, ys2_b = ys_p.get()
                            ycv = yc2[:, 0:ST].rearrange("p (b s) -> p b s", s=32)
                            ysv = ys2[:, 0:ST].rearrange("p (b s) -> p b s", s=32)
                            conv_tail(t, kind, None, lambda i: e2[:, :, i:i + 32], ST, ycv, ysv, (e2_b, yc2_b, ys2_b), t0)
                            hold["ys2"] = (ys2, ys2_b)

                    def stage_c():
                        ys, ys_b = hold["ys"]
                        norm_store(t, kind, ys[:, 0:nw], ys_b, nw, t0)
                        if has_s:
                            ys2, ys2_b = hold["ys2"]
                            norm_store(t, kind, ys2[:, 0:ST], ys2_b, ST, WL)

                    return [stage_b, stage_c]

                def ev_va(c0, gw, t0, sub, pt, pt_b):
                    vf, vf_b = f32_p.get()
                    E.op("dve", lambda e: e.tensor_copy(out=vf[:, 0:gw], in_=pt[:, 0:gw]), reads=[pt_b], writes=[vf_b])
                    r0 = t0 + sub * 128
                    E.dma("pool", v_s[r0:r0 + 128, c0:c0 + gw], vf[:, 0:gw], reads=[vf_b], writes=[vs_b])
                    if r0 >= OK0:
                        E.dma("sp", v_o[r0 - OK0:r0 - OK0 + 128, c0:c0 + gw], vf[:, 0:gw], reads=[vf_b], is_output=True)

                def ev_ba(t0, sub, pt, pt_b):
                    gbt, gbt_b = gb_p.get()
                    tmp, tmp_b = tmp_p.get()
                    E.op("act", lambda e: e.activation(out=gbt[:, 0:HB], in_=pt[:, 0:HB], func=AF.Sigmoid),
                         reads=[pt_b], writes=[gbt_b])
                    E.op("dve", lambda e: e.tensor_tensor(out=tmp[:], in0=pt[:, HB:2 * HB], in1=gc_sb[:, HB:2 * HB],
                                                          op=ALU.add), reads=[pt_b, smallc_b], writes=[tmp_b])
                    E.op("act", lambda e: e.activation(out=tmp[:], in_=tmp[:], func=AF.Exp), reads=[tmp_b], writes=[tmp_b])
                    E.op("act", lambda e: e.activation(out=tmp[:], in_=tmp[:], func=AF.Ln, bias=1.0),
                         reads=[tmp_b], writes=[tmp_b])
                    E.op("dve", lambda e: e.tensor_tensor(out=gbt[:, HB:2 * HB], in0=tmp[:], in1=nA_sb[:], op=ALU.mult),
                         reads=[tmp_b, nA_b], writes=[gbt_b])
                    r0 = t0 + sub * 128
                    E.dma("pool", gb_s[r0:r0 + 128, :], gbt[:], reads=[gbt_b], writes=[gbs_b])

                def ev_z(c0, gw, zc0, t0, sub, pt, pt_b):
                    ob, ob_b = obf_p.get()
                    E.op("act", lambda e: e.activation(out=ob[:, 0:gw], in_=pt[:, 0:gw], func=AF.Silu),
                         reads=[pt_b], writes=[ob_b])
                    r0 = t0 + sub * 128 - (WL - OWN)
                    E.dma("sp", zs_s[r0:r0 + 128, zc0:zc0 + gw], ob[:, 0:gw], reads=[ob_b], writes=[zss_b])

                GF = 8
                f_tiles = [("ka", f) for f in range(2 * HA)] + [("g", t) for t in range(3 * HB)]
                for g0 in range(0, len(f_tiles), GF):
                    grp = f_tiles[g0:g0 + GF]
                    load_W(wF, cfg.NF * 128, g0 * 128, len(grp) * 128)
                    pend = []
                    for (t0, tw) in win_blocks():
                        xnb, xnb_b = load_xn(t0, tw)
                        for j, (kind, idx) in enumerate(grp):
                            pt, pt_b = mm_F(xnb, xnb_b, tw, j)
                            if kind == "ka":
                                ev_ka(idx, t0, tw, pt, pt_b)
                            else:
                                for item in pend:
                                    item.pop(0)()
                                pend = [it for it in pend if it]
                                pend.append(ev_g(idx, t0, tw, pt, pt_b))
                    while pend:
                        for item in pend:
                            item.pop(0)()
                        pend = [it for it in pend if it]
                qa_tiles = list(range(2 * HA))
                for g0 in range(0, len(qa_tiles), GF):
                    grp = qa_tiles[g0:g0 + GF]
                    load_W(wF, cfg.NF * 128, (cfg.NF_W + g0) * 128, len(grp) * 128)
                    for (t0, tw) in own_blocks():
                        xnb, xnb_b = load_xn(t0, tw)
                        for j, f in enumerate(grp):
                            pt, pt_b = mm_F(xnb, xnb_b, tw, j)
                            ev_qa(f, t0, tw, pt, pt_b)
                segs = [("va", c, min(512, WA - c)) for c in range(0, WA, 512)] + [("ba", WA, 2 * HB)]
                groups = []
                cur, curw = [], 0
                for sg in segs:
                    if curw + sg[2] > WGW:
                        groups.append(cur)
                        cur, curw = [], 0
                    cur.append(sg)
                    curw += sg[2]
                groups.append(cur)
                for grp in groups:
                    gc0 = grp[0][1]
                    gw_tot = sum(sg[2] for sg in grp)
                    load_W(wT, cfg.NT, gc0, gw_tot)
                    for (t0, tw) in win_blocks():
                        xnb, xnb_b = load_xn(t0, tw)
                        for sub in range(tw // 128):
                            for (kind, c, w) in grp:
                                pt, pt_b = mm_T(xnb, xnb_b, sub, c - gc0, w)
                                if kind == "va":
                                    ev_va(c, w, t0, sub, pt, pt_b)
                                else:
                                    ev_ba(t0, sub, pt, pt_b)
                zsegs = [(c, min(512, WA + WB - c)) for c in range(0, WA + WB, 512)]
                for g0 in range(0, len(zsegs), 2):
                    grp = zsegs[g0:g0 + 2]
                    gc0 = grp[0][0]
                    gw_tot = sum(w for _, w in grp)
                    load_W(wT, cfg.NT, cfg.NT_W + gc0, gw_tot)
                    for (t0, tw) in own_blocks():
                        xnb, xnb_b = load_xn(t0, tw)
                        for sub in range(tw // 128):
                            for (c, w) in grp:
                                pt, pt_b = mm_T(xnb, xnb_b, sub, c - gc0, w)
                                ev_z(c - gc0, w, c, t0, sub, pt, pt_b)

        if 1 in phases:
            phase1()
            E.barrier()


        def phase2():
            with ExitStack() as st:
                QB0 = WL - OWN
                SCALE = 128 ** -0.5
                KTt = sb(st, "KTt", [128, 2, WL], BF16)
                KT_b = Buf("KTt")
                Vt = sb(st, "Vt", [128, NKT, 256], BF16)
                V_b = Buf("Vt")
                QTt = sb(st, "QTt", [128, 2, OT], BF16)
                QT_b = Buf("QTt")
                val_f = sb(st, "val_f", [128, NKT], F32)
                val_h = sb(st, "val_h", [128, NKT], BF16)
                val_b = Buf("val", const=True)
                E.dma("sp", val_f[:], valid.ap(), writes=[val_b])
                E.op("dve", lambda e: e.tensor_copy(out=val_h[:], in_=val_f[:]), reads=[val_b], writes=[val_b])
                dm_f = sb(st, "dm_f", [128, 4, 512], F32)
                dm_b = Buf("dm", const=True)
                E.dma("sp", dm_f[:], dmask.ap().rearrange("p (a j) -> p a j", j=512), writes=[dm_b])
                lv = sb(st, "lv", [128, 4, 128], F32)
                lam_t = sb(st, "lam_t", [128, 4], F32)
                sl_sb = sb(st, "sl_sb", [128, 256], F32)
                rb_sb = sb(st, "rb_sb", [128, HA], F32)
                misc_b = Buf("misc2", const=True)
                E.dma("sp", lv[:], dram_view(lamv, 0, [[0, 128], [128, 4], [1, 128]]), writes=[misc_b])
                E.dma("sp", sl_sb[:], dram_view(subln, 0, [[0, 128], [1, 256]]), writes=[misc_b])
                E.dma("sp", rb_sb[:], dram_view(relb, 15 * HA, [[0, 128], [1, HA]]), writes=[misc_b])
                E.op("dve", lambda e: e.tensor_tensor(out=lv[:, 0, :], in0=lv[:, 0, :], in1=lv[:, 1, :], op=ALU.mult),
                     reads=[misc_b], writes=[misc_b])
                E.op("dve", lambda e: e.tensor_tensor(out=lv[:, 2, :], in0=lv[:, 2, :], in1=lv[:, 3, :], op=ALU.mult),
                     reads=[misc_b], writes=[misc_b])
                E.op("dve", lambda e: e.tensor_reduce(out=lam_t[:, 0:1], in_=lv[:, 0, :], axis=AX.X, op=ALU.add),
                     reads=[misc_b], writes=[misc_b])
                E.op("dve", lambda e: e.tensor_reduce(out=lam_t[:, 1:2], in_=lv[:, 2, :], axis=AX.X, op=ALU.add),
                     reads=[misc_b], writes=[misc_b])
                E.op("act", lambda e: e.activation(out=lam_t[:, 0:2], in_=lam_t[:, 0:2], func=AF.Exp), reads=[misc_b], writes=[misc_b])
                E.op("dve", lambda e: e.tensor_tensor(out=lam_t[:, 2:3], in0=lam_t[:, 0:1], in1=lam_t[:, 1:2], op=ALU.subtract),
                     reads=[misc_b], writes=[misc_b])
                E.op("dve", lambda e: e.tensor_scalar(out=lam_t[:, 3:4], in0=lam_t[:, 2:3], scalar1=LAM_INIT, scalar2=-1.0,
                                                      op0=ALU.add, op1=ALU.mult), reads=[misc_b], writes=[misc_b])
                E.op("dve", lambda e: e.tensor_scalar(out=sl_sb[:], in0=sl_sb[:], scalar1=1.0 - LAM_INIT, scalar2=None, op0=ALU.mult),
                     reads=[misc_b], writes=[misc_b])
                sc_p = Pool([ps(st, "sct%d" % i, [128, 512]) for i in range(4)], "sct", psum=True)
                bvs_b = Buf("bv_s")
                with ExitStack() as st2:
                    relb_sb = sb(st2, "relb_sb", [N_BUCKETS, HA], F32)
                    oh_sb = sb(st2, "oh_sb", [N_BUCKETS, BV_LEN], F32)
                    bvt = sb(st2, "bvt", [HA, BV_LEN], F32)
                    bv_b = Buf("bv")
                    E.dma("sp", relb_sb[:], relb.ap(), writes=[bv_b])
                    E.dma("sp", oh_sb[:], oh.ap(), writes=[bv_b])
                    for c0 in range(0, BV_LEN, 512):
                        w = min(512, BV_LEN - c0)
                        pt, pt_b = sc_p.get()
                        E.op("pe", lambda e: e.matmul(pt[0:HA, 0:w], lhsT=relb_sb[:, :], rhs=oh_sb[:, c0:c0 + w], start=True, stop=True),
                             reads=[bv_b], writes=[pt_b])
                        E.op("act", lambda e: e.activation(out=bvt[:, c0:c0 + w], in_=pt[0:HA, 0:w], func=AF.Copy), reads=[pt_b], writes=[bv_b])
                    E.dma("sp", bv_s.ap(), bvt[:], reads=[bv_b], writes=[bvs_b])
                E.barrier()

                hk_p = Pool([sb(st, "hk%d" % i, [128, 512], F32) for i in range(2)], "hk")
                eb = sb(st, "eb", [128, 9, 512], BF16)
                eb_b = Buf("eb")
                NKS = (CL + 32 + 127) // 128
                ebs = sb(st, "ebs", [128, NKS, 32], BF16)
                ebs_b = Buf("ebs")
                ebtmp_p = Pool([sb(st, "ebtmp%d" % i, [128, 512], F32) for i in range(2)], "ebtmp")
                PT_p = Pool([sb(st, "PT%d" % i, [128, 512], BF16) for i in range(4)], "PT")
                oacc = [ps(st, "oacc%d" % i, [128, 512]) for i in range(2)]
                oden = ps(st, "oden", [128, 512])
                oacc_b = Buf("oacc", psum=True)
                o1 = sb(st, "o1", [128, 4, 256], F32)
                o1_b = Buf("o1")
                ofin_p = Pool([sb(st, "ofin%d" % i, [128, 256], F32) for i in range(2)], "ofin")
                osq = sb(st, "osq", [128, 256], F32)
                osq_b = Buf("osq")
                rd = sb(st, "rd", [128, 8], F32)
                rd_b = Buf("rd")
                st_p = Pool([sb(st, "ast%d" % i, [128, 2], F32) for i in range(2)], "ast")
                zs_p = Pool([sb(st, "azs%d" % i, [128, 256], BF16) for i in range(2)], "azs")
                ybf_p = Pool([sb(st, "aybf%d" % i, [128, 256], BF16) for i in range(2)], "aybf")
                yT_p = Pool([sb(st, "ayT%d" % i, [128, 2, 128], BF16) for i in range(2)], "ayT")
                tp_p = Pool([ps(st, "atp", [128, 1024], BF16)[:, 0:128]], "atp", psum=True)
                KTs = sb(st, "KTs", [128, 2, CL + 32], BF16)
                KTs_b = Buf("KTs")
                NKS = (CL + 32 + 127) // 128
                Vs = sb(st, "Vs", [128, NKS, 256], BF16)
                Vs_b = Buf("Vs")
                PTs_p = Pool([sb(st, "PTs%d" % i, [128, NKS, 32], BF16) for i in range(2)], "PTs")

                def build_bias(h):
                    for i, dl in enumerate(NEAR_D):
                        hk, hk_b = hk_p.get()
                        base = BV_R0 - 127 - dl
                        E.dma("sp", hk[:], dram_view(bv_s, h * BV_LEN + base, [[1, 128], [1, 512]]), reads=[bvs_b], writes=[hk_b])
                        pt, pt_b = sc_p.get()
                        E.op("pe", lambda e: e.matmul(pt[:], lhsT=cst_sb[:, C_AID:C_AID + 128], rhs=hk[:], start=True, stop=True),
                             reads=[hk_b, cst_b], writes=[pt_b])
                        if dl < 0:
                            E.op("act", lambda e: e.activation(out=eb[:, i, :], in_=pt[:], func=AF.Exp), reads=[pt_b], writes=[eb_b])
                        else:
                            t, t_b = ebtmp_p.get()
                            E.op("act", lambda e: e.activation(out=t[:], in_=pt[:], func=AF.Exp), reads=[pt_b], writes=[t_b])
                            E.op("dve", lambda e: e.tensor_tensor(out=eb[:, i, :], in0=t[:], in1=dm_f[:, dl // 128, :], op=ALU.mult),
                                 reads=[t_b, dm_b], writes=[eb_b])
                    for kt in range(NKS):
                        hk, hk_b = hk_p.get()
                        base = BV_R0 - 127 - (128 * kt - CL)
                        E.dma("sp", hk[:, 0:32], dram_view(bv_s, h * BV_LEN + base, [[1, 128], [1, 32]]), reads=[bvs_b], writes=[hk_b])
                        pt, pt_b = sc_p.get()
                        E.op("pe", lambda e: e.matmul(pt[:, 0:32], lhsT=cst_sb[:, C_AID:C_AID + 128], rhs=hk[:, 0:32], start=True, stop=True),
                             reads=[hk_b, cst_b], writes=[pt_b])
                        E.op("act", lambda e: e.activation(out=ebs[:, kt, :], in_=pt[:, 0:32], func=AF.Exp), reads=[pt_b], writes=[ebs_b])

                def finish_o(h, L, o_ps_list, den_cols, c, orow_list):
                    ns = len(o_ps_list)
                    for sub in range(ns):
                        E.op("dve", lambda e: e.reciprocal(out=rd[0:L, c * 4 + sub:c * 4 + sub + 1], in_=den_cols[sub]),
                             reads=[oacc_b], writes=[rd_b])
                    if c == 0:
                        for sub in range(ns):
                            E.op("act", lambda e: e.activation(out=o1[0:L, sub, :], in_=o_ps_list[sub], func=AF.Copy,
                                                               scale=rd[0:L, sub:sub + 1]), reads=[oacc_b, rd_b], writes=[o1_b])
                        return
                    E.op("dve", lambda e: e.tensor_scalar(out=rd[0:L, 4:4 + ns], in0=rd[0:L, 4:4 + ns], scalar1=lam_t[0:L, 3:4],
                                                          scalar2=None, op0=ALU.mult), reads=[rd_b, misc_b], writes=[rd_b])
                    for sub in range(ns):
                        of, of_b = ofin_p.get()
                        E.op("dve", lambda e: e.scalar_tensor_tensor(out=of[0:L, :], in0=o_ps_list[sub], scalar=rd[0:L, 4 + sub:5 + sub],
                                                                     in1=o1[0:L, sub, :], op0=ALU.mult, op1=ALU.add),
                             reads=[oacc_b, rd_b, o1_b], writes=[of_b])
                        stt, stt_b = st_p.get()
                        E.op("act", lambda e: e.activation(out=osq[0:L, :], in_=of[0:L, :], func=AF.Square, accum_out=stt[0:L, 0:1]),
                             reads=[of_b], writes=[osq_b, stt_b])
                        E.op("act", lambda e: e.activation(out=stt[0:L, 1:2], in_=stt[0:L, 0:1], func=AF.Sqrt, scale=1.0 / 256,
                                                           bias=eps_sb[0:L, 0:1]), reads=[stt_b, eps_b], writes=[stt_b])
                        E.op("dve", lambda e: e.reciprocal(out=stt[0:L, 1:2], in_=stt[0:L, 1:2]), reads=[stt_b], writes=[stt_b])
                        orow = orow_list[sub]
                        zs, zs_b = zs_p.get()
                        E.dma("sp", zs[0:L, :], zs_s[orow:orow + L, h * 256:(h + 1) * 256], reads=[zss_b], writes=[zs_b])
                        E.op("dve", lambda e: e.scalar_tensor_tensor(out=of[0:L, :], in0=of[0:L, :], scalar=stt[0:L, 1:2], in1=sl_sb[0:L, :],
                                                                     op0=ALU.mult, op1=ALU.mult), reads=[of_b, stt_b, misc_b], writes=[of_b])
                        yb, yb_b = ybf_p.get()
                        E.op("dve", lambda e: e.tensor_tensor(out=yb[0:L, :], in0=of[0:L, :], in1=zs[0:L, :], op=ALU.mult),
                             reads=[of_b, zs_b], writes=[yb_b])
                        yT, yT_b = yT_p.get()
                        for e2 in range(2):
                            tp, tp_b = tp_p.get()
                            E.op("pe", lambda e: e.transpose(tp[:, 0:L], yb[0:L, e2 * 128:(e2 + 1) * 128], ident_bf[0:L, 0:L]),
                                 reads=[yb_b, cbf_b], writes=[tp_b])
                            E.op("act", lambda e: e.activation(out=yT[:, e2, 0:L], in_=tp[:, 0:L], func=AF.Copy), reads=[tp_b], writes=[yT_b])
                        E.dma("pool", dram_view(yT_s, h * 256 * OT + orow, [[OT, 128], [128 * OT, 2], [1, L]]), yT[:, :, 0:L],
                              reads=[yT_b], writes=[yTs_b])

                for h in range(HA):
                    E.dma("sp", KTt[:], dram_view(kT_s, h * 256 * TL, [[TL, 128], [128 * TL, 2], [1, WL]]), reads=[kTs_b], writes=[KT_b])
                    for kq in range(0, NKT, 32):
                        nk = min(32, NKT - kq)
                        E.dma("sp", Vt[:, kq:kq + nk, :],
                              dram_view(v_s, kq * 128 * WA + h * 256, [[WA, 128], [128 * WA, nk], [1, 256]]), reads=[vs_b], writes=[V_b])
                    E.dma("sp", QTt[:], dram_view(qT_s, h * 256 * OT, [[OT, 128], [128 * OT, 2], [1, OT]]), reads=[qTs_b], writes=[QT_b])
                    build_bias(h)
                    for qb in range(OWN // 512):
                        q0 = qb * 512
                        kt_hi = (QB0 + q0) // 128 + 4
                        for c in range(2):
                            for a in oacc + [oden]:
                                E.op("dve", lambda e: e.memset(a[:], 0.0), writes=[oacc_b])
                            def emit_pv(kt, PT, PT_b):
                                for sub in range(4):
                                    lt = PT[:, sub * 128:(sub + 1) * 128]
                                    E.op("pe", lambda e: e.matmul(oacc[sub // 2][:, (sub % 2) * 256:(sub % 2 + 1) * 256], lhsT=lt,
                                                                  rhs=Vt[:, kt, :], start=False, stop=(kt == kt_hi - 1),
                                                                  skip_group_check=True), reads=[PT_b, V_b], writes=[oacc_b])
                                    E.op("pe", lambda e: e.matmul(oden[:, sub:sub + 1], lhsT=lt, rhs=val_h[:, kt:kt + 1], start=False,
                                                                  stop=(kt == kt_hi - 1), skip_group_check=True),
                                         reads=[PT_b, val_b], writes=[oacc_b])

                            prev = None
                            for kt in range(kt_hi):
                                dl = 128 * kt - QB0 - q0
                                pt, pt_b = sc_p.get()
                                E.op("pe", lambda e: e.matmul(pt[:], lhsT=KTt[:, c, kt * 128:(kt + 1) * 128], rhs=QTt[:, c, q0:q0 + 512],
                                                              start=True, stop=True), reads=[KT_b, QT_b], writes=[pt_b])
                                PT, PT_b = PT_p.get()
                                if dl < NEAR_D[0]:
                                    E.op("act", lambda e: e.activation(out=PT[:], in_=pt[:], func=AF.Exp, scale=SCALE,
                                                                       bias=rb_sb[:, h:h + 1]), reads=[pt_b, misc_b], writes=[PT_b])
                                else:
                                    E.op("act", lambda e: e.activation(out=PT[:], in_=pt[:], func=AF.Exp, scale=SCALE),
                                         reads=[pt_b], writes=[PT_b])
                                    E.op("pool", lambda e: e.tensor_tensor(out=PT[:], in0=PT[:], in1=eb[:, NEAR_D.index(dl), :], op=ALU.mult),
                                         reads=[PT_b, eb_b], writes=[PT_b])
                                if prev is not None:
                                    emit_pv(*prev)
                                prev = (kt, PT, PT_b)
                            emit_pv(*prev)
                            finish_o(h, 128, [oacc[sub // 2][:, (sub % 2) * 256:(sub % 2 + 1) * 256] for sub in range(4)],
                                     [oden[:, sub:sub + 1] for sub in range(4)], c, [q0 + sub * 128 for sub in range(4)])
                    for b in range(SB):
                        E.dma("pool", KTs[:, :, 0:CL],
                              dram_view(ckT, (b * 2 * HA + 2 * h) * 128 * CL, [[CL, 128], [128 * CL, 2], [1, CL]]), writes=[KTs_b])
                        E.dma("sp", KTs[:, :, CL:CL + 32],
                              dram_view(kT_s, h * 256 * TL + WL + 32 * b, [[TL, 128], [128 * TL, 2], [1, 32]]), reads=[kTs_b], writes=[KTs_b])
                        nfull = CL // 128
                        rem = CL - nfull * 128
                        E.dma("pool", Vs[:, 0:nfull, :],
                              dram_view(cv, b * CL * WA + h * 256, [[WA, 128], [128 * WA, nfull], [1, 256]]), writes=[Vs_b])
                        if rem:
                            E.dma("pool", Vs[0:rem, nfull, :],
                                  dram_view(cv, (b * CL + nfull * 128) * WA + h * 256, [[WA, rem], [1, 256]]), writes=[Vs_b])
                        E.dma("sp", Vs[rem:rem + 32, nfull, :],
                              dram_view(v_s, (WL + 32 * b) * WA + h * 256, [[WA, 32], [1, 256]]), reads=[vs_b], writes=[Vs_b])
                        qc0 = OWN + 32 * b
                        for c in range(2):
                            pt, pt_b = sc_p.get()
                            for kt in range(NKS):
                                n = min(128, CL + 32 - kt * 128)
                                E.op("pe", lambda e: e.matmul(pt[0:n, kt * 32:(kt + 1) * 32], lhsT=KTs[:, c, kt * 128:kt * 128 + n],
                                                              rhs=QTt[:, c, qc0:qc0 + 32], start=True, stop=True),
                                     reads=[KTs_b, QT_b], writes=[pt_b])
                            PTs, PTs_b = PTs_p.get()
                            E.op("act", lambda e: e.activation(out=PTs[:], in_=pt[:, 0:NKS * 32].rearrange("p (k q) -> p k q", q=32),
                                                               func=AF.Exp, scale=SCALE), reads=[pt_b], writes=[PTs_b])
                            E.op("pool", lambda e: e.tensor_tensor(out=PTs[:], in0=PTs[:], in1=ebs[:], op=ALU.mult),
                                 reads=[PTs_b, ebs_b], writes=[PTs_b])
                            for kt in range(NKS):
                                n = min(128, CL + 32 - kt * 128)
                                E.op("pe", lambda e: e.matmul(oacc[0][0:32, 0:256], lhsT=PTs[0:n, kt, :], rhs=Vs[0:n, kt, :],
                                                              start=(kt == 0), stop=(kt == NKS - 1)), reads=[PTs_b, Vs_b], writes=[oacc_b])
                            for kt in range(NKS):
                                n = min(128, CL + 32 - kt * 128)
                                E.op("pe", lambda e: e.matmul(oden[0:32, 0:1], lhsT=PTs[0:n, kt, :], rhs=ones_bf[0:n, 0:1],
                                                              start=(kt == 0), stop=(kt == NKS - 1)), reads=[PTs_b, cbf_b], writes=[oacc_b])
                            finish_o(h, 32, [oacc[0][0:32, 0:256]], [oden[0:32, 0:1]], c, [qc0])

        if 2 in phases:
            phase2()
            E.barrier()

        def phase3():
            with ExitStack() as st:
                GH = min(8, HB)
                S_f = sb(st, "S_f", [128, HB, 128], F32)
                S_h = sb(st, "S_h", [128, HB, 128], BF16)
                S_b = [Buf("S%d" % h) for h in range(HB)]
                Sh_b = [Buf("Sh%d" % h) for h in range(HB)]
                nb_sb = sb(st, "nb_sb", [128, 128], F32)
                nb_b = Buf("nb", const=True)
                E.dma("sp", nb_sb[:], dram_view(normb, 0, [[0, 128], [1, 128]]), writes=[nb_b])
                gbt_p = Pool([sb(st, "g3bt%d" % i, [128, 2 * HB], F32) for i in range(2)], "g3bt")
                qkv_p = Pool([sb(st, "qkv%d" % i, [128, 3 * HB, 128], BF16) for i in range(2)], "qkv")
                zs_p = Pool([sb(st, "zs%d" % i, [128, WB], BF16) for i in range(2)], "zs")
                sc_p = Pool([sb(st, "sc%d" % i, [128, 6 * HB], F32) for i in range(2)], "sc")
                o_p = Pool([sb(st, "osb%d" % i, [128, HB, 128], F32) for i in range(2)], "osb", nsub=HB)
                sq_t = sb(st, "o_sq", [128, HB, 128], F32)
                sq_tb = Buf("o_sq")
                rst_p = Pool([sb(st, "orst%d" % i, [128, 2 * HB], F32) for i in range(2)], "orst")
                yb_p = Pool([sb(st, "ybf%d" % i, [128, HB, 128], BF16) for i in range(2)], "ybf")
                yT_p = Pool([sb(st, "yTt%d" % i, [128, HB, 128], BF16) for i in range(2)], "yTt", nsub=HB)
                scps_p = Pool([ps(st, "scps", [128, 512])], "scps", psum=True)
                pf_p = Pool([ps(st, "pfb%d" % i, [128, 512])[:, 0:128] for i in range(5)], "pf", psum=True)
                pb_p = Pool([ps(st, "pbf%d" % i, [128, 1024], BF16)[:, 0:128] for i in range(2)], "pb", psum=True)
                NHS = GH
                TDT = F32 if GDN_FP32 else BF16
                ident_t = cst_sb[:, C_ID:C_ID + 128] if GDN_FP32 else ident_bf
                pT_p = pf_p if GDN_FP32 else pb_p
                def hs_tiles(i):
                    d = {}
                    for nm in ("kt", "gams", "gamT", "WT", "vnew", "AqkT"):
                        d[nm] = (sb(st, "h%d_%s" % (i, nm), [128, 128], BF16), Buf("h%d_%s" % (i, nm)))
                    for nm in ("kbe", "vb", "N", "M", "X0", "X1", "XT0", "XT1", "P0", "P1"):
                        d[nm] = (sb(st, "h%d_%s" % (i, nm), [128, 128], TDT), Buf("h%d_%s" % (i, nm)))
                    for nm in ("gtri", "gam", "U", "t1"):
                        d[nm] = (sb(st, "h%d_%s" % (i, nm), [128, 128], F32), Buf("h%d_%s" % (i, nm)))
                    return d
                HS = [hs_tiles(i) for i in range(NHS)]
                triu = cst_sb[:, C_TRIU:C_TRIU + 128]
                lstr = cst_sb[:, C_LSTR:C_LSTR + 128]
                linc = cst_sb[:, C_LINC:C_LINC + 128]
                ones_f = cst_sb[:, C_ONE:C_ONE + 128]
                lstr_bf = cbf[:, C_LSTR:C_LSTR + 128]
                triu_bf = cbf[:, C_TRIU:C_TRIU + 128]

                def chunk(t0, L, need_o, orow0):
                    nfac = int(math.ceil(math.log2(L)))
                    gbt, gbt_b = gbt_p.get()
                    E.dma("sp", gbt[0:L, :], gb_s[t0:t0 + L, :], reads=[gbs_b], writes=[gbt_b])
                    qkv, qkv_b = qkv_p.get()
                    E.dma("sp", qkv[:, :, 0:L], dram_view(gT_s, t0, [[TL, 128], [128 * TL, 3 * HB], [1, L]]),
                          reads=[gTs_b], writes=[qkv_b])
                    if need_o:
                        zs, zs_b = zs_p.get()
                        E.dma("sp", zs[0:L, :], zs_s[orow0:orow0 + L, WA:WA + WB], reads=[zss_b], writes=[zs_b])
                    scps, scps_b = scps_p.get()
                    E.op("pe", lambda e: e.matmul(scps[0:L, 0:HB], lhsT=triu[0:L, 0:L], rhs=gbt[0:L, HB:2 * HB],
                                                  start=True, stop=True), reads=[cst_b, gbt_b], writes=[scps_b])
                    E.op("pe", lambda e: e.matmul(scps[:, HB:2 * HB], lhsT=ones_f[0:L, :], rhs=gbt[0:L, HB:2 * HB],
                                                  start=True, stop=True), reads=[cst_b, gbt_b], writes=[scps_b])
                    sc, sc_b = sc_p.get()
                    c_eG, c_eGLG, c_eGL, c_beG, c_nb, c_tmp = [slice(i * HB, (i + 1) * HB) for i in range(6)]
                    E.op("act", lambda e: e.activation(out=sc[0:L, c_eG], in_=scps[0:L, 0:HB], func=AF.Exp),
                         reads=[scps_b], writes=[sc_b])
                    E.op("act", lambda e: e.activation(out=sc[0:L, c_tmp], in_=scps[0:L, 0:HB], func=AF.Copy),
                         reads=[scps_b], writes=[sc_b])
                    E.op("dve", lambda e: e.tensor_tensor(out=sc[0:L, c_tmp], in0=scps[0:L, HB:2 * HB], in1=sc[0:L, c_tmp],
                                                          op=ALU.subtract), reads=[scps_b, sc_b], writes=[sc_b])
                    E.op("act", lambda e: e.activation(out=sc[0:L, c_eGLG], in_=sc[0:L, c_tmp], func=AF.Exp),
                         reads=[sc_b], writes=[sc_b])
                    E.op("act", lambda e: e.activation(out=sc[:, c_eGL], in_=scps[:, HB:2 * HB], func=AF.Exp),
                         reads=[scps_b], writes=[sc_b])
                    E.op("dve", lambda e: e.tensor_tensor(out=sc[0:L, c_beG], in0=sc[0:L, c_eG], in1=gbt[0:L, 0:HB],
                                                          op=ALU.mult), reads=[sc_b, gbt_b], writes=[sc_b])
                    E.op("dve", lambda e: e.tensor_scalar(out=sc[0:L, c_nb], in0=gbt[0:L, 0:HB], scalar1=-1.0, scalar2=None,
                                                          op0=ALU.mult), reads=[gbt_b], writes=[sc_b])
                    if need_o:
                        osb, osb_bs = o_p.get()

                    def col(cs, h):
                        return sc[0:L, cs.start + h:cs.start + h + 1]

                    for g0 in range(0, HB, GH):
                        heads = list(range(g0, min(HB, g0 + GH)))
                        T = {h: HS[h - g0] for h in heads}
                        QT = {h: qkv[:, h, 0:L] for h in heads}
                        KT = {h: qkv[:, HB + h, 0:L] for h in heads}
                        VT = {h: qkv[:, 2 * HB + h, 0:L] for h in heads}
                        for h in heads:
                            pk, pk_b = pb_p.get()
                            E.op("pe", lambda e: e.transpose(pk[0:L, :], KT[h], ident_bf), reads=[qkv_b, cbf_b], writes=[pk_b])
                            t, b = T[h]["kbe"]
                            E.op("act", lambda e: e.activation(out=t[0:L, :], in_=pk[0:L, :], func=AF.Copy, scale=col(c_beG, h)),
                                 reads=[pk_b, sc_b], writes=[b])
                            t, b = T[h]["kt"]
                            E.op("dve", lambda e: e.tensor_scalar(out=t[0:L, :], in0=pk[0:L, :], scalar1=col(c_eGLG, h),
                                                                  scalar2=None, op0=ALU.mult), reads=[pk_b, sc_b], writes=[b])
                            pv, pv_b = pb_p.get()
                            E.op("pe", lambda e: e.transpose(pv[0:L, :], VT[h], ident_bf), reads=[qkv_b, cbf_b], writes=[pv_b])
                            t, b = T[h]["vb"]
                            E.op("act", lambda e: e.activation(out=t[0:L, :], in_=pv[0:L, :], func=AF.Copy,
                                                               scale=gbt[0:L, h:h + 1]), reads=[pv_b, gbt_b], writes=[b])
                        for h in heads:
                            gt, gt_b = T[h]["gtri"]
                            E.op("pool", lambda e: e.tensor_scalar(out=gt[0:L, 0:L], in0=triu[0:L, 0:L],
                                                                   scalar1=gbt[0:L, HB + h:HB + h + 1], scalar2=None,
                                                                   op0=ALU.mult), reads=[cst_b, gbt_b], writes=[gt_b])
                            pg, pg_b = pf_p.get()
                            E.op("pe", lambda e: e.matmul(pg[0:L, 0:L], lhsT=gt[0:L, 0:L], rhs=lstr[0:L, 0:L], start=True, stop=True),
                                 reads=[gt_b, cst_b], writes=[pg_b])
                            gm, gm_b = T[h]["gam"]
                            E.op("act", lambda e: e.activation(out=gm[0:L, 0:L], in_=pg[0:L, 0:L], func=AF.Exp),
                                 reads=[pg_b], writes=[gm_b])
                            gs, gs_b = T[h]["gams"]
                            E.op("pool", lambda e: e.tensor_tensor(out=gs[0:L, 0:L], in0=gm[0:L, 0:L], in1=lstr[0:L, 0:L], op=ALU.mult),
                                 reads=[gm_b, cst_b], writes=[gs_b])
                            pkk, pkk_b = pf_p.get()
                            E.op("pe", lambda e: e.matmul(pkk[0:L, 0:L], lhsT=KT[h], rhs=KT[h], start=True, stop=True),
                                 reads=[qkv_b], writes=[pkk_b])
                            n_, n_b = T[h]["N"]
                            E.op("dve", lambda e: e.scalar_tensor_tensor(out=n_[0:L, 0:L], in0=pkk[0:L, 0:L], scalar=col(c_nb, h),
                                                                         in1=gs[0:L, 0:L], op0=ALU.mult, op1=ALU.mult),
                                 reads=[pkk_b, sc_b, gs_b], writes=[n_b])
                            if need_o:
                                pgt, pgt_b = pf_p.get()
                                E.op("pe", lambda e: e.matmul(pgt[0:L, 0:L], lhsT=lstr[0:L, 0:L], rhs=gt[0:L, 0:L], start=True, stop=True),
                                     reads=[gt_b, cst_b], writes=[pgt_b])
                                t1, t1_b = T[h]["t1"]
                                E.op("act", lambda e: e.activation(out=t1[0:L, 0:L], in_=pgt[0:L, 0:L], func=AF.Exp),
                                     reads=[pgt_b], writes=[t1_b])
                                gT_, gT_b = T[h]["gamT"]
                                E.op("pool", lambda e: e.tensor_tensor(out=gT_[0:L, 0:L], in0=t1[0:L, 0:L], in1=triu[0:L, 0:L], op=ALU.mult),
                                     reads=[t1_b, cst_b], writes=[gT_b])
                        cur = {}
                        for h in heads:
                            n_, n_b = T[h]["N"]
                            pm, pm_b = pT_p.get()
                            E.op("pe", lambda e: e.transpose(pm[0:L, 0:L], n_[0:L, 0:L], ident_t[0:L, 0:L]), reads=[n_b, cbf_b, cst_b], writes=[pm_b])
                            m_, m_b = T[h]["M"]
                            E.op("act", lambda e: e.activation(out=m_[0:L, 0:L], in_=pm[0:L, 0:L], func=AF.Copy), reads=[pm_b], writes=[m_b])
                            p0, p0_b = T[h]["P0"]
                            E.op("dve", lambda e: e.tensor_tensor(out=p0[0:L, 0:L], in0=pm[0:L, 0:L], in1=ident_t[0:L, 0:L], op=ALU.add),
                                 reads=[pm_b, cbf_b, cst_b], writes=[p0_b])
                            cur[h] = dict(X=T[h]["M"], XT=T[h]["N"], P=T[h]["P0"], pi=0, xi=0)
                        for k in range(1, nfac):
                            last = (k == nfac - 1)
                            for h in heads:
                                c = cur[h]
                                (X, X_b), (XT, XT_b), (P, P_b) = c["X"], c["XT"], c["P"]
                                nXT = T[h]["XT%d" % c["xi"]]
                                pxt, pxt_b = pf_p.get()
                                E.op("pe", lambda e: e.matmul(pxt[0:L, 0:L], lhsT=X[0:L, 0:L], rhs=XT[0:L, 0:L], start=True, stop=True),
                                     reads=[X_b, XT_b], writes=[pxt_b])
                                E.op("dve", lambda e: e.tensor_copy(out=nXT[0][0:L, 0:L], in_=pxt[0:L, 0:L]), reads=[pxt_b], writes=[nXT[1]])
                                if not last:
                                    nX = T[h]["X%d" % c["xi"]]
                                    px, px_b = pf_p.get()
                                    E.op("pe", lambda e: e.matmul(px[0:L, 0:L], lhsT=XT[0:L, 0:L], rhs=X[0:L, 0:L], start=True, stop=True),
                                         reads=[X_b, XT_b], writes=[px_b])
                                    E.op("act", lambda e: e.activation(out=nX[0][0:L, 0:L], in_=px[0:L, 0:L], func=AF.Copy),
                                         reads=[px_b], writes=[nX[1]])
                                    c["X"] = nX
                                c["XT"] = nXT
                                c["xi"] ^= 1
                            for h in heads:
                                c = cur[h]
                                (XT, XT_b), (P, P_b) = c["XT"], c["P"]
                                nP = T[h]["P%d" % (c["pi"] ^ 1)]
                                pp, pp_b = pf_p.get()
                                E.op("pe", lambda e: e.matmul(pp[0:L, 0:L], lhsT=XT[0:L, 0:L], rhs=P[0:L, 0:L], start=True, stop=True),
                                     reads=[XT_b, P_b], writes=[pp_b])
                                E.op("dve", lambda e: e.tensor_tensor(out=nP[0][0:L, 0:L], in0=pp[0:L, 0:L], in1=P[0:L, 0:L], op=ALU.add),
                                     reads=[pp_b, P_b], writes=[nP[1]])
                                c["P"] = nP
                                c["pi"] ^= 1
                        for h in heads:
                            P, P_b = cur[h]["P"]
                            pu, pu_b = pf_p.get()
                            vb, vb_b = T[h]["vb"]
                            E.op("pe", lambda e: e.matmul(pu[0:L, :], lhsT=P[0:L, 0:L], rhs=vb[0:L, :], start=True, stop=True),
                                 reads=[P_b, vb_b], writes=[pu_b])
                            U, U_b = T[h]["U"]
                            E.op("act", lambda e: e.activation(out=U[0:L, :], in_=pu[0:L, :], func=AF.Copy), reads=[pu_b], writes=[U_b])
                            pw, pw_b = pf_p.get()
                            kbe, kbe_b = T[h]["kbe"]
                            E.op("pe", lambda e: e.matmul(pw[:, 0:L], lhsT=kbe[0:L, :], rhs=P[0:L, 0:L], start=True, stop=True),
                                 reads=[P_b, kbe_b], writes=[pw_b])
                            WT, WT_b = T[h]["WT"]
                            E.op("dve", lambda e: e.tensor_copy(out=WT[:, 0:L], in_=pw[:, 0:L]), reads=[pw_b], writes=[WT_b])
                        for h in heads:
                            WT, WT_b = T[h]["WT"]
                            pws, pws_b = pf_p.get()
                            E.op("pe", lambda e: e.matmul(pws[0:L, :], lhsT=WT[:, 0:L], rhs=S_h[:, h, :], start=True, stop=True),
                                 reads=[WT_b, Sh_b[h]], writes=[pws_b])
                            U, U_b = T[h]["U"]
                            vn, vn_b = T[h]["vnew"]
                            E.op("dve", lambda e: e.tensor_tensor(out=vn[0:L, :], in0=U[0:L, :], in1=pws[0:L, :], op=ALU.subtract),
                                 reads=[U_b, pws_b], writes=[vn_b])
                            if need_o:
                                pqs, pqs_b = pf_p.get()
                                E.op("pe", lambda e: e.matmul(pqs[0:L, :], lhsT=QT[h], rhs=S_h[:, h, :], start=True, stop=True),
                                     reads=[qkv_b, Sh_b[h]], writes=[pqs_b])
                                t1, t1_b = T[h]["t1"]
                                E.op("act", lambda e: e.activation(out=t1[0:L, :], in_=pqs[0:L, :], func=AF.Copy, scale=col(c_eG, h)),
                                     reads=[pqs_b, sc_b], writes=[t1_b])
                                pqk, pqk_b = pf_p.get()
                                E.op("pe", lambda e: e.matmul(pqk[0:L, 0:L], lhsT=KT[h], rhs=QT[h], start=True, stop=True),
                                     reads=[qkv_b], writes=[pqk_b])
                                aq, aq_b = T[h]["AqkT"]
                                gT_, gT_b = T[h]["gamT"]
                                E.op("dve", lambda e: e.tensor_tensor(out=aq[0:L, 0:L], in0=pqk[0:L, 0:L], in1=gT_[0:L, 0:L], op=ALU.mult),
                                     reads=[pqk_b, gT_b], writes=[aq_b])
                        for h in heads:
                            vn, vn_b = T[h]["vnew"]
                            if need_o:
                                aq, aq_b = T[h]["AqkT"]
                                pav, pav_b = pf_p.get()
                                E.op("pe", lambda e: e.matmul(pav[0:L, :], lhsT=aq[0:L, 0:L], rhs=vn[0:L, :], start=True, stop=True),
                                     reads=[aq_b, vn_b], writes=[pav_b])
                                t1, t1_b = T[h]["t1"]
                                E.op("dve", lambda e: e.tensor_tensor(out=osb[0:L, h, :], in0=pav[0:L, :], in1=t1[0:L, :], op=ALU.add),
                                     reads=[pav_b, t1_b], writes=[osb_bs[h]])
                            kt, kt_b = T[h]["kt"]
                            psu, psu_b = pf_p.get()
                            E.op("pe", lambda e: e.matmul(psu[:, :], lhsT=kt[0:L, :], rhs=vn[0:L, :], start=True, stop=True),
                                 reads=[kt_b, vn_b], writes=[psu_b])
                            E.op("dve", lambda e: e.scalar_tensor_tensor(out=S_f[:, h, :], in0=S_f[:, h, :],
                                                                         scalar=sc[:, c_eGL.start + h:c_eGL.start + h + 1],
                                                                         in1=psu[:, :], op0=ALU.mult, op1=ALU.add),
                                 reads=[psu_b, sc_b, S_b[h]], writes=[S_b[h]])
                            E.op("pool", lambda e: e.tensor_copy(out=S_h[:, h, :], in_=S_f[:, h, :]), reads=[S_b[h]], writes=[Sh_b[h]])
                    if need_o:
                        E.op("act", lambda e: e.activation(out=sq_t[0:L], in_=osb[0:L], func=AF.Square), reads=osb_bs, writes=[sq_tb])
                        rst, rst_b = rst_p.get()
                        E.op("dve", lambda e: e.tensor_reduce(out=rst[0:L, 0:HB], in_=sq_t[0:L], axis=AX.X, op=ALU.add),
                             reads=[sq_tb], writes=[rst_b])
                        E.op("act", lambda e: e.activation(out=rst[0:L, 0:HB], in_=rst[0:L, 0:HB], func=AF.Sqrt, scale=1.0 / 128,
                                                           bias=eps_sb[0:L, 0:1]), reads=[rst_b, eps_b], writes=[rst_b])
                        E.op("dve", lambda e: e.reciprocal(out=rst[0:L, 0:HB], in_=rst[0:L, 0:HB]), reads=[rst_b], writes=[rst_b])
                        E.op("dve", lambda e: e.tensor_tensor(out=sq_t[0:L], in0=osb[0:L],
                                                              in1=rst[0:L, 0:HB].unsqueeze(2).to_broadcast([L, HB, 128]), op=ALU.mult),
                             reads=osb_bs + [rst_b], writes=[sq_tb])
                        E.op("pool", lambda e: e.tensor_tensor(out=sq_t[0:L], in0=sq_t[0:L],
                                                               in1=nb_sb[0:L, :].unsqueeze(1).to_broadcast([L, HB, 128]), op=ALU.mult),
                             reads=[sq_tb, nb_b], writes=[sq_tb])
                        yb, yb_b = yb_p.get()
                        E.op("dve", lambda e: e.tensor_tensor(out=yb[0:L], in0=sq_t[0:L],
                                                              in1=zs[0:L, :].rearrange("p (h d) -> p h d", d=128), op=ALU.mult),
                             reads=[sq_tb, zs_b], writes=[yb_b])
                        yT, yT_bs = yT_p.get()
                        for h in range(HB):
                            pt_, pt_b = pb_p.get()
                            E.op("pe", lambda e: e.transpose(pt_[:, 0:L], yb[0:L, h, :], ident_bf[0:L, 0:L]), reads=[yb_b, cbf_b], writes=[pt_b])
                            E.op("act", lambda e: e.activation(out=yT[:, h, 0:L], in_=pt_[:, 0:L], func=AF.Copy), reads=[pt_b], writes=[yT_bs[h]])
                        E.dma("pool", dram_view(yT_s, WA * OT + orow0, [[OT, 128], [128 * OT, HB], [1, L]]), yT[:, :, 0:L],
                              reads=yT_bs, writes=[yTs_b])

                E.op("pool", lambda e: e.memset(S_f[:], 0.0), writes=S_b)
                E.op("pool", lambda e: e.memset(S_h[:], 0.0), writes=Sh_b)
                for t0 in range(0, WL, 128):
                    need = t0 >= WL - OWN
                    chunk(t0, 128, need, t0 - (WL - OWN))
                E.dma("sp", ssm_o.ap().rearrange("(h p) d -> p h d", p=128), S_f[:], reads=S_b, is_output=True)
                for b in range(SB):
                    E.dma("sp", S_f[:], ssm[b * HB * 128:(b + 1) * HB * 128, :].rearrange("(h p) d -> p h d", p=128), writes=S_b)
                    E.op("pool", lambda e: e.tensor_copy(out=S_h[:], in_=S_f[:]), reads=S_b, writes=Sh_b)
                    chunk(WL + 32 * b, 32, True, OWN + 32 * b)
                    E.dma("sp", ssm_so[b * HB * 128:(b + 1) * HB * 128, :].rearrange("(h p) d -> p h d", p=128), S_f[:],
                          reads=S_b, is_output=True)

        if 3 in phases:
            phase3()
            E.barrier()


        def phase4():
            with ExitStack() as st:
                pnb = sb(st, "pnb", [128, D], F32)
                pnb_b = Buf("pnb", const=True)
                E.dma("sp", pnb[:], dram_view(postn, 0, [[0, 128], [1, D]]), writes=[pnb_b])
                TG = 256
                YT_p = Pool([sb(st, "YTg%d" % i, [128, KC, TG], BF16) for i in range(1)], "YTg")
                wo_p = Pool([sb(st, "wo%d" % i, [128, KC, 512], BF16) for i in range(2)], "wo")
                yacc = sb(st, "yacc", [128, TG // 128, D], F32)
                yacc_b = [Buf("yacc%d" % i) for i in range(4)]
                ssq = sb(st, "ssq4", [128, 4, D // 512], F32)
                ssq_b = [Buf("ssq4_%d" % i) for i in range(4)]
                junk = sb(st, "junk4", [128, 512], F32)
                junk_b = Buf("junk4")
                r4 = sb(st, "r4", [128, 4, 2], F32)
                xo_p = Pool([sb(st, "xo%d" % i, [128, D], F32) for i in range(2)], "xo")
                mm_p = Pool([ps(st, "m4_%d" % i, [128, 512]) for i in range(4)], "m4", psum=True)
                NCB = D // 512
                if os.environ.get("P4ZERO"):
                    zt = sb(st, "zt", [128, OT], BF16)
                    zt_b = Buf("zt")
                    E.op("dve", lambda e: e.memset(zt[:], 0.0), writes=[zt_b])
                    for kc in range(KC):
                        E.dma("sp", yT_s[kc * 128:(kc + 1) * 128, :], zt[:], reads=[zt_b], writes=[yTs_b])
                for g0 in range(0, OT, TG):
                    gw = min(TG, OT - g0)
                    nsub = gw // 128
                    YT, YT_b = YT_p.get()
                    E.dma("sp", YT[:, :, 0:gw], dram_view(yT_s, g0, [[OT, 128], [128 * OT, KC], [1, gw]]), reads=[yTs_b], writes=[YT_b])
                    for cb in range(NCB):
                        wo, wo_b = wo_p.get()
                        E.dma("pool", wo[:], dram_view(wout, cb * 512, [[D, 128], [128 * D, KC], [1, 512]]), writes=[wo_b])
                        for sub in range(nsub):
                            pt, pt_b = mm_p.get()
                            for kc in range(KC):
                                E.op("pe", lambda e: e.matmul(pt[:], lhsT=YT[:, kc, sub * 128:(sub + 1) * 128], rhs=wo[:, kc, :],
                                                              start=(kc == 0), stop=(kc == KC - 1)), reads=[YT_b, wo_b], writes=[pt_b])
                            E.op("dve", lambda e: e.tensor_copy(out=yacc[:, sub, cb * 512:(cb + 1) * 512], in_=pt[:]),
                                 reads=[pt_b], writes=[yacc_b[sub]])
                            E.op("act", lambda e: e.activation(out=junk[:], in_=pt[:], func=AF.Square, accum_out=ssq[:, sub, cb:cb + 1]),
                                 reads=[pt_b], writes=[junk_b, ssq_b[sub]])
                    P4CUT = int(os.environ.get("P4CUT", "9"))
                    for sub in range(nsub):
                        if P4CUT < 2:
                            break
                        r0 = g0 + sub * 128
                        xo, xo_b = xo_p.get()
                        E.dma("sp", xo[:], xown[r0:r0 + 128, :], writes=[xo_b])
                        E.op("dve", lambda e: e.tensor_reduce(out=r4[:, sub, 0:1], in_=ssq[:, sub, :], axis=AX.X, op=ALU.add),
                             reads=[ssq_b[sub]], writes=[ssq_b[sub]])
                        E.op("act", lambda e: e.activation(out=r4[:, sub, 1:2], in_=r4[:, sub, 0:1], func=AF.Sqrt, scale=1.0 / D,
                                                           bias=eps_sb[:, 0:1]), reads=[ssq_b[sub], eps_b], writes=[ssq_b[sub]])
                        E.op("dve", lambda e: e.reciprocal(out=r4[:, sub, 1:2], in_=r4[:, sub, 1:2]), reads=[ssq_b[sub]], writes=[ssq_b[sub]])
                        if P4CUT < 3:
                            continue
                        E.op("dve", lambda e: e.scalar_tensor_tensor(out=yacc[:, sub, :], in0=yacc[:, sub, :], scalar=r4[:, sub, 1:2],
                                                                     in1=pnb[:], op0=ALU.mult, op1=ALU.mult),
                             reads=[yacc_b[sub], ssq_b[sub], pnb_b], writes=[yacc_b[sub]])
                        if P4CUT < 4:
                            continue
                        E.op("pool", lambda e: e.tensor_tensor(out=xo[:], in0=xo[:], in1=yacc[:, sub, :], op=ALU.add),
                             reads=[xo_b, yacc_b[sub]], writes=[xo_b])
                        if P4CUT < 5:
                            continue
                        E.dma("sp", y_o[r0:r0 + 128, :], xo[:], reads=[xo_b], is_output=True)

        if 4 in phases:
            phase4()
            E.barrier()

        E.finish()
        global LAST_E
        LAST_E = E
    return nc


def prep_inputs(cfg, inp):
    D, NC, OWN, SB, WL, TL, HA, HB, WA, WB = cfg.D, cfg.NC, cfg.OWN, cfg.SB, cfg.WL, cfg.TL, cfg.HA, cfg.HB, cfg.WA, cfg.WB
    f = np.float32
    w = np.asarray(inp["w_in"], f)[0]
    wF = np.ascontiguousarray(np.concatenate([w[:, WA:2 * WA], w[:, 4 * WA:4 * WA + 3 * WB], w[:, 0:WA]], axis=1))
    o = 4 * WA + 4 * WB
    wT = np.ascontiguousarray(np.concatenate([w[:, 2 * WA:3 * WA], w[:, o:o + 2 * HB], w[:, 3 * WA:4 * WA],
                                              w[:, 4 * WA + 3 * WB:4 * WA + 4 * WB]], axis=1))
    wout = np.ascontiguousarray(np.asarray(inp["w_out"], f)[0])
    pn = np.ascontiguousarray(np.asarray(inp["pre_norm"], f)[0].reshape(cfg.KC, 128).T)
    postn = np.ascontiguousarray(np.asarray(inp["post_norm"], f)[0][None])
    convw = np.ascontiguousarray(np.asarray(inp["conv_b"], f)[0].reshape(4, 3 * HB, 128).transpose(2, 1, 0)).reshape(128, -1)
    gconst = np.concatenate([np.asarray(inp["a_log_b"], f)[0], np.asarray(inp["dt_bias_b"], f)[0]])[None]
    relb = np.ascontiguousarray(np.asarray(inp["rel_bias"], f))
    lamv = np.concatenate([np.asarray(inp[k], f)[0] for k in ("lambda_q1", "lambda_k1", "lambda_q2", "lambda_k2")])[None]
    subln = np.asarray(inp["subln_a"], f)[0][None]
    normb = np.asarray(inp["norm_b"], f)[0][None]
    xp = np.asarray(inp["x_prompt"], f)[0]
    xs = np.asarray(inp["x_sample"], f)
    meta = np.asarray(inp["meta_tokens"], f)
    ck = np.asarray(inp["cache_k_a"], f)[0]
    cvv = np.asarray(inp["cache_v_a"], f)[0]
    ssm = np.asarray(inp["state_ssm_b"], f)[0]
    cs = np.asarray(inp["state_conv_b"], f)[0]
    cstc = make_cst()
    ohc = make_oh()
    dmc = make_dmask()
    maps = []
    for c in range(NC):
        pad = OWN * (NC - 1 - c)
        xa = np.zeros((TL, D), f)
        xa[pad + 112:pad + 128] = meta
        xa[pad + 128:WL] = xp[0:(c + 1) * OWN]
        xa[WL:] = xs[c * SB:(c + 1) * SB].reshape(-1, D)
        m = {
            "xT": np.ascontiguousarray(xa.T),
            "xown": np.ascontiguousarray(xa[WL - OWN:]),
            "wF": wF, "wT": wT, "wout": wout, "pn": pn, "postn": postn, "convw": convw,
            "convst": np.ascontiguousarray(cs[c * SB:(c + 1) * SB].reshape(SB, 3, 3 * HB, 128).transpose(3, 2, 0, 1)).reshape(128, -1),
            "gconst": gconst, "relb": relb, "lamv": lamv, "subln": subln, "normb": normb,
            "ckT": np.ascontiguousarray(ck[c * SB:(c + 1) * SB].transpose(0, 2, 3, 4, 1)).reshape(-1, cfg.CL),
            "cv": np.ascontiguousarray(cvv[c * SB:(c + 1) * SB]).reshape(SB * cfg.CL, WA),
            "ssm": np.ascontiguousarray(ssm[c * SB:(c + 1) * SB]).reshape(-1, 128),
            "cst": cstc,
            "valid": np.ascontiguousarray((np.arange(WL).reshape(-1, 128).T >= pad + 112).astype(f)),
            "oh": ohc, "dmask": dmc,
        }
        maps.append(m)
    return maps


def assemble(cfg, res):
    D, NC, OWN, SB, HA, HB, WA, WB, SEQ = cfg.D, cfg.NC, cfg.OWN, cfg.SB, cfg.HA, cfg.HB, cfg.WA, cfg.WB, cfg.SEQ
    f = np.float32
    DECB, DS = cfg.DECB, cfg.DECS
    y_p = np.zeros((1, SEQ, D), f)
    y_s = np.zeros((DECB, DS, D), f)
    k_p = np.zeros((1, 1, 16 + SEQ, HA, 2, 128), f)
    v_p = np.zeros((1, 1, 16 + SEQ, HA, 256), f)
    k_s = np.zeros((1, DECB, DS, HA, 2, 128), f)
    v_s = np.zeros((1, DECB, DS, HA, 256), f)
    c_s = np.zeros((1, DECB, 3, 3 * WB), f)
    s_s = np.zeros((1, DECB, HB, 128, 128), f)
    for c in range(NC):
        r = res[c]
        y = np.asarray(r["y_o"])
        y_p[0, c * OWN:(c + 1) * OWN] = y[0:OWN]
        y_s[c * SB:(c + 1) * SB] = y[OWN:].reshape(SB, DS, D)
        kt = np.asarray(r["kT_o"]).reshape(HA, 2, 128, -1).transpose(3, 0, 1, 2)
        vt = np.asarray(r["v_o"]).reshape(-1, HA, 256)
        if c == 0:
            k_p[0, 0, 0:16] = kt[112:128]
            v_p[0, 0, 0:16] = vt[112:128]
        k_p[0, 0, 16 + c * OWN:16 + (c + 1) * OWN] = kt[128:128 + OWN]
        v_p[0, 0, 16 + c * OWN:16 + (c + 1) * OWN] = vt[128:128 + OWN]
        k_s[0, c * SB:(c + 1) * SB] = kt[128 + OWN:].reshape(SB, DS, HA, 2, 128)
        v_s[0, c * SB:(c + 1) * SB] = vt[128 + OWN:].reshape(SB, DS, HA, 256)
        c_s[0, c * SB:(c + 1) * SB] = np.asarray(r["conv_so"]).reshape(3 * HB, 128, SB, 3).transpose(2, 3, 0, 1).reshape(SB, 3, 3 * WB)
        s_s[0, c * SB:(c + 1) * SB] = np.asarray(r["ssm_so"]).reshape(SB, HB, 128, 128)
    rl = res[NC - 1]
    s_p = np.asarray(rl["ssm_o"]).reshape(1, 1, HB, 128, 128).astype(f)
    c_p = np.asarray(rl["conv_o"]).reshape(3 * HB, 128, 3).transpose(2, 0, 1).reshape(1, 1, 3, 3 * WB).astype(f)
    return (y_p, y_s, k_p, v_p, s_p, c_p, k_s, v_s, s_s, c_s)


_NC_CACHE = {}


def kernel(**inputs):
    cfg = Cfg()
    if "nc" not in _NC_CACHE:
        _NC_CACHE["nc"] = build(cfg)
    nc = _NC_CACHE["nc"]
    maps = prep_inputs(cfg, inputs)
    res = run_bass_kernel_spmd(nc, maps, core_ids=list(range(cfg.NC)))
    return assemble(cfg, res.results)
```

```python
from contextlib import ExitStack
import math
import numpy as np
import concourse.bass as bass
import concourse.mybir as mybir
from concourse.bass_utils import run_bass_kernel_spmd

F32 = mybir.dt.float32
BF16 = mybir.dt.bfloat16
AF = mybir.ActivationFunctionType
ALU = mybir.AluOpType
AX = mybir.AxisListType
EPS = 1e-6
N_BUCKETS = 32
MAX_DISTANCE = 1024
CHUNK = 64
LAM_INIT = 0.8 - 0.6 * math.exp(0.0)
import os
GDN_FP32 = os.environ.get("GDN_FP32", "1") == "1"
LAST_E = None


class Cfg:
    def __init__(self, D=4096, NC=8, SEQ=16384, DECB=32, DECS=32, PAST=1024):
        self.D, self.NC, self.SEQ, self.DECB, self.DECS, self.PAST = D, NC, SEQ, DECB, DECS, PAST
        self.NMETA = 16
        self.WA = D // 2
        self.WB = D - self.WA
        self.HA = self.WA // 256
        self.HB = self.WB // 128
        self.INC = 4 * self.WA + 4 * self.WB + 2 * self.HB
        self.KC = D // 128
        self.OWN = SEQ // NC
        self.WL = 128 + SEQ
        self.SB = DECB // NC
        self.ST = self.SB * DECS
        assert self.ST == 128 and DECS == 32
        self.TL = self.WL + self.ST
        self.CL = self.NMETA + PAST
        self.OT = self.OWN + self.ST
        self.OK0 = self.WL - self.OWN - 128
        assert self.OK0 % 512 == 0 and self.WL % 512 == 128
        self.NF_W = 2 * self.HA + 3 * self.HB
        self.NF = self.NF_W + 2 * self.HA
        self.NT_W = self.WA + 2 * self.HB
        self.NT = self.NT_W + self.WA + self.WB


class Buf:
    __slots__ = ("name", "w", "r", "const", "psum")

    def __init__(self, name, const=False, psum=False):
        self.name = name
        self.w = None
        self.r = []
        self.const = const
        self.psum = psum


class Emitter:
    ENG = ("pe", "act", "dve", "pool", "sp")

    def __init__(self, nc, es, n_dma_sems=48):
        self.nc = nc
        self.eng = {"pe": nc.tensor, "act": nc.scalar, "dve": nc.vector, "pool": nc.gpsimd, "sp": nc.sync}
        self.psem = {e: es.enter_context(nc.semaphore("P_" + e)) for e in ("pe", "act", "dve", "pool")}
        self.pcnt = {e: 0 for e in self.psem}
        self.dsem = [es.enter_context(nc.semaphore("D%d" % i)) for i in range(n_dma_sems)]
        self.dval = [0] * n_dma_sems
        self.dnext = 0
        nq = n_dma_sems // 2
        self.qrange = {"sp": (0, nq), "pool": (nq, n_dma_sems), "act": (0, nq)}
        self.qnext = {"sp": 0, "pool": 0, "act": 0}
        self.waited = {}
        self.out_events = []
        self.marks = []

    def _need(self, waiter, ev):
        if ev is None:
            return None
        kind, src, val = ev
        if kind == "e" and src == waiter and waiter == "pe":
            return None
        key = (waiter, kind, src)
        if self.waited.get(key, 0) >= val:
            return None
        return key, val

    def _do_waits(self, waiter, evs):
        best = {}
        for ev in evs:
            n = self._need(waiter, ev)
            if n is not None:
                key, val = n
                if best.get(key, 0) < val:
                    best[key] = val
        for key, val in best.items():
            _, kind, src = key
            sem = self.psem[src] if kind == "e" else self.dsem[src]
            self.eng[waiter].wait_ge(sem, val)
            self.waited[key] = val

    def _deps(self, reads, writes, waiter=None):
        evs = []
        for b in reads:
            evs.append(b.w)
            if b.psum:
                evs.extend(ev for ev in b.r if not (ev[0] == "e" and ev[1] == waiter))
        for b in writes:
            evs.append(b.w)
            evs.extend(b.r)
        return evs

    def op(self, e, fn, reads=(), writes=()):
        self._do_waits(e, self._deps(reads, writes, e))
        ins = fn(self.eng[e])
        self.pcnt[e] += 1
        ins.then_inc(self.psem[e], 1)
        ev = ("e", e, self.pcnt[e])
        for b in reads:
            if not b.const:
                b.r.append(ev)
        for b in writes:
            b.w = ev
            b.r = []
        return ev

    def dma(self, q, out, in_, reads=(), writes=(), is_output=False, **kw):
        lo, hi = self.qrange[q]
        qk = "sp" if q == "act" else q
        i = lo + self.qnext[qk]
        self.qnext[qk] = (self.qnext[qk] + 1) % (hi - lo)
        evs = self._deps(reads, writes)
        if self.dval[i] > 0:
            evs.append(("d", i, self.dval[i]))
        self._do_waits(q, evs)
        ins = self.eng[q].dma_start(out=out, in_=in_, **kw)
        self.dval[i] += 16
        ins.then_inc(self.dsem[i], 16)
        ev = ("d", i, self.dval[i])
        for b in reads:
            if not b.const:
                b.r.append(ev)
        for b in writes:
            b.w = ev
            b.r = []
        if is_output:
            self.out_events.append(ev)
        return ev

    def barrier(self):
        self.marks.append(dict(self.pcnt))
        evs = []
        for i, v in enumerate(self.dval):
            if v > 0:
                evs.append(("d", i, v))
        for e in self.psem:
            if self.pcnt[e] > 0:
                evs.append(("e", e, self.pcnt[e]))
        for w in ("pe", "act", "dve", "pool", "sp"):
            self._do_waits(w, [ev for ev in evs if not (ev[0] == "e" and ev[1] == w)])

    def finish(self):
        evs = list(self.out_events)
        for i, v in enumerate(self.dval):
            if v > 0:
                evs.append(("d", i, v))
        for e in self.psem:
            if self.pcnt[e] > 0:
                evs.append(("e", e, self.pcnt[e]))
        self._do_waits("sp", evs)


class Pool:
    def __init__(self, tiles, name, nsub=0, psum=False):
        self.tiles = tiles
        if nsub:
            self.bufs = [[Buf("%s%d_%d" % (name, i, j)) for j in range(nsub)] for i in range(len(tiles))]
        else:
            self.bufs = [Buf("%s%d" % (name, i), psum=psum) for i in range(len(tiles))]
        self.i = 0

    def get(self):
        t, b = self.tiles[self.i], self.bufs[self.i]
        self.i = (self.i + 1) % len(self.tiles)
        return t, b


def dram_view(t, offset, pattern):
    return bass.AP(t, offset, pattern)


def rel_bucket_np(rel):
    nb = N_BUCKETS // 2
    max_exact = nb // 2
    n = np.abs(rel)
    nf = np.maximum(n, 1).astype(np.float32)
    large = max_exact + (np.log(nf / np.float32(max_exact)) / np.float32(math.log(MAX_DISTANCE / max_exact))
                         * np.float32(nb - max_exact)).astype(np.int32)
    large = np.minimum(large, nb - 1)
    return np.where(rel > 0, nb, 0) + np.where(n < max_exact, n, large)


C_ID, C_AID, C_TRIU, C_LINC, C_LSTR, C_ONE = 0, 128, 256, 384, 512, 640
CST_W = 768
BV_R0 = 511
BV_LEN = 1664
NEAR_D = [-640, -512, -384, -256, -128, 0, 128, 256, 384]


def make_oh():
    i = np.arange(BV_LEN)
    bk = rel_bucket_np((BV_R0 - i).astype(np.int32))
    o = np.zeros((N_BUCKETS, BV_LEN), np.float32)
    o[bk, i] = 1.0
    return o


def make_dmask():
    m = np.zeros((128, 4, 512), np.float32)
    p = np.arange(128)[:, None]
    j = np.arange(512)[None, :]
    for a in range(4):
        m[:, a, :] = ((128 * a + p) // 64) <= (j // 64)
    return m.reshape(128, 2048)


def make_cst():
    c = np.zeros((128, CST_W), np.float32)
    i = np.arange(128)
    c[:, C_ID:C_ID + 128] = np.eye(128)
    c[:, C_AID:C_AID + 128] = np.eye(128)[::-1]
    c[:, C_TRIU:C_TRIU + 128] = (i[:, None] <= i[None, :])
    c[:, C_LINC:C_LINC + 128] = (i[:, None] >= i[None, :])
    c[:, C_LSTR:C_LSTR + 128] = (i[:, None] > i[None, :])
    c[:, C_ONE:C_ONE + 128] = 1.0
    return c


def build(cfg, phases=(0, 1, 2, 3, 4)):
    nc = bass.Bass("TRN2", target_bir_lowering=False)
    D, KC, TL, WL, OWN, OT, OK0 = cfg.D, cfg.KC, cfg.TL, cfg.WL, cfg.OWN, cfg.OT, cfg.OK0
    HA, HB, WA, WB, SB, ST, CL = cfg.HA, cfg.HB, cfg.WA, cfg.WB, cfg.SB, cfg.ST, cfg.CL
    KO = TL - OK0

    def din(name, shape, dt=F32):
        return nc.dram_tensor(name, list(shape), dt, kind="ExternalInput")

    def dout(name, shape, dt=F32):
        return nc.dram_tensor(name, list(shape), dt, kind="ExternalOutput")

    def dscr(name, shape, dt):
        return nc.dram_tensor(name, list(shape), dt)

    xT = din("xT", [D, TL])
    xown = din("xown", [OT, D])
    wF = din("wF", [D, cfg.NF * 128])
    wT = din("wT", [D, cfg.NT])
    wout = din("wout", [D, D])
    pn = din("pn", [128, KC])
    postn = din("postn", [1, D])
    convw = din("convw", [128, 3 * HB * 4])
    convst = din("convst", [128, 3 * HB * SB * 3])
    gconst = din("gconst", [1, 2 * HB])
    relb = din("relb", [N_BUCKETS, HA])
    lamv = din("lamv", [1, 4 * 128])
    subln = din("subln", [1, 256])
    normb = din("normb", [1, 128])
    ckT = din("ckT", [SB * 2 * HA * 128, CL])
    cv = din("cv", [SB * CL, WA])
    ssm = din("ssm", [SB * HB * 128, 128])
    cst = din("cst", [128, CST_W])
    NKT = WL // 128
    valid = din("valid", [128, NKT])
    oh = din("oh", [N_BUCKETS, BV_LEN])
    dmask = din("dmask", [128, 4 * 512])
    bv_s = dscr("bv_s", [HA, BV_LEN], F32)
    y_o = dout("y_o", [OT, D])
    kT_o = dout("kT_o", [2 * HA * 128, KO])
    v_o = dout("v_o", [KO, WA])
    ssm_o = dout("ssm_o", [HB * 128, 128])
    ssm_so = dout("ssm_so", [SB * HB * 128, 128])
    conv_o = dout("conv_o", [3 * HB * 128, 3])
    conv_so = dout("conv_so", [3 * HB * 128, SB * 3])
    xnT_s = dscr("xnT_s", [128, KC * TL], BF16)
    kT_s = dscr("kT_s", [2 * HA * 128, TL], BF16)
    v_s = dscr("v_s", [TL, WA], BF16)
    gT_s = dscr("gT_s", [3 * HB * 128, TL], BF16)
    gb_s = dscr("gb_s", [TL, 2 * HB], F32)
    qT_s = dscr("qT_s", [2 * HA * 128, OT], BF16)
    zs_s = dscr("zs_s", [OT, WA + WB], BF16)
    yT_s = dscr("yT_s", [D, OT], BF16)

    es = ExitStack()
    with es:
        E = Emitter(nc, es)

        def sb(stack, name, shape, dt):
            return stack.enter_context(nc.sbuf_tensor(name, list(shape), dt))

        def ps(stack, name, shape, dt=F32):
            return stack.enter_context(nc.psum_tensor(name, list(shape), dt))

        cst_sb = sb(es, "cst_sb", [128, CST_W], F32)
        cst_b = Buf("cst", const=True)
        E.dma("sp", cst_sb[:], cst.ap(), writes=[cst_b])
        cbf = sb(es, "cbf", [128, CST_W], BF16)
        cbf_b = Buf("cbf", const=True)
        E.op("dve", lambda e: e.tensor_copy(out=cbf[:], in_=cst_sb[:]), reads=[cst_b], writes=[cbf_b])
        onesD = sb(es, "onesD", [128, 128], BF16)
        onesD_b = Buf("onesD", const=True)
        E.op("dve", lambda e: e.tensor_scalar(out=onesD[:], in0=cst_sb[:, C_ONE:C_ONE + 128], scalar1=1.0 / D,
                                              scalar2=None, op0=ALU.mult), reads=[cst_b], writes=[onesD_b])
        eps_sb = sb(es, "eps_sb", [128, 1], F32)
        eps_b = Buf("eps", const=True)
        E.op("dve", lambda e: e.memset(eps_sb[:], EPS), writes=[eps_b])
        pn_sb = sb(es, "pn_sb", [128, KC], F32)
        pn_b = Buf("pn", const=True)
        E.dma("sp", pn_sb[:], pn.ap(), writes=[pn_b])

        xns_b = [Buf("xns%d" % i) for i in range(TL // 256)]

        def xns_deps(t0, tw):
            return [xns_b[i] for i in range(t0 // 256, (t0 + tw - 1) // 256 + 1)]

        kTs_b = Buf("kT_s")
        vs_b = Buf("v_s")
        gTs_b = Buf("gT_s")
        gbs_b = Buf("gb_s")
        qTs_b = Buf("qT_s")
        zss_b = Buf("zs_s")
        yTs_b = Buf("yT_s")

        convw_sb = sb(es, "convw_sb", [128, 3 * HB, 4], F32)
        convst_sb = sb(es, "convst_sb", [128, 3 * HB, SB, 3], F32)
        gc_sb = sb(es, "gc_sb", [128, 2 * HB], F32)
        nA_sb = sb(es, "nA_sb", [128, HB], F32)
        ones_bf = cbf[:, C_ONE:C_ONE + 128]
        ident_bf = cbf[:, C_ID:C_ID + 128]
        smallc_b = Buf("smallc", const=True)
        E.dma("sp", convw_sb[:], convw.ap(), writes=[smallc_b])
        E.dma("sp", convst_sb[:], convst.ap(), writes=[smallc_b])
        E.dma("sp", gc_sb[:], dram_view(gconst, 0, [[0, 128], [1, 2 * HB]]), writes=[smallc_b])
        nA_b = Buf("nA", const=True)
        E.op("act", lambda e: e.activation(out=nA_sb[:], in_=gc_sb[:, 0:HB], func=AF.Exp),
             reads=[smallc_b], writes=[nA_b])
        E.op("dve", lambda e: e.tensor_scalar(out=nA_sb[:], in0=nA_sb[:], scalar1=-1.0, scalar2=None, op0=ALU.mult),
             reads=[nA_b], writes=[nA_b])
        smallc_b.w = None if False else smallc_b.w

        def phase0():
            with ExitStack() as st:
                TB = 256
                xf_p = Pool([sb(st, "xf%d" % i, [128, KC, TB], F32) for i in range(2)], "xf")
                sq_p = Pool([sb(st, "sq%d" % i, [128, KC, TB], BF16) for i in range(1)], "sq")
                xn_p = Pool([sb(st, "xn%d" % i, [128, KC, TB], BF16) for i in range(2)], "xn", nsub=KC)
                rs_p = Pool([sb(st, "rs%d" % i, [128, TB], F32) for i in range(2)], "rs")
                ss_p = Pool([ps(st, "ss%d" % i, [128, 512]) for i in range(2)], "ss", psum=True)
                for blk in range(TL // TB):
                    t0 = blk * TB
                    xf, xf_b = xf_p.get()
                    E.dma("sp", xf[:], dram_view(xT, t0, [[TL, 128], [128 * TL, KC], [1, TB]]), writes=[xf_b])
                    sq, sq_b = sq_p.get()
                    E.op("act", lambda e: e.activation(out=sq[:], in_=xf[:], func=AF.Square),
                         reads=[xf_b], writes=[sq_b])
                    ss, ss_b = ss_p.get()
                    for kc in range(KC):
                        E.op("pe", lambda e: e.matmul(ss[:, 0:TB], lhsT=onesD[:], rhs=sq[:, kc, :],
                                                      start=(kc == 0), stop=(kc == KC - 1)),
                             reads=[onesD_b, sq_b], writes=[ss_b])
                    rs, rs_b = rs_p.get()
                    E.op("act", lambda e: e.activation(out=rs[:], in_=ss[:, 0:TB], func=AF.Sqrt, bias=eps_sb[:, 0:1]),
                         reads=[ss_b, eps_b], writes=[rs_b])
                    E.op("dve", lambda e: e.reciprocal(out=rs[:], in_=rs[:]), reads=[rs_b], writes=[rs_b])
                    xn, xn_bs = xn_p.get()
                    for kc in range(KC):
                        E.op("dve", lambda e: e.scalar_tensor_tensor(out=xn[:, kc, :], in0=xf[:, kc, :],
                                                                   scalar=pn_sb[:, kc:kc + 1], in1=rs[:],
                                                                   op0=ALU.mult, op1=ALU.mult),
                             reads=[xf_b, rs_b, pn_b], writes=[xn_bs[kc]])
                    E.dma("pool", dram_view(xnT_s, t0, [[KC * TL, 128], [TL, KC], [1, TB]]), xn[:],
                          reads=xn_bs, writes=[xns_b[blk]])

        if 0 in phases:
            phase0()
            E.barrier()

        def phase1():
            with ExitStack() as st:
                WGW = 1088
                Wg = sb(st, "Wg", [128, KC, WGW], BF16)
                Wg_b = Buf("Wg")
                xnb_p = Pool([sb(st, "xnb%d" % i, [128, KC, 512], BF16) for i in range(2)], "xnb")
                mm_p = Pool([ps(st, "mm%d" % i, [128, 512]) for i in range(5)], "mm", psum=True)
                ss_p = Pool([ps(st, "ssq%d" % i, [128, 512]) for i in range(2)], "ssq", psum=True)
                f32_p = Pool([sb(st, "ev%d" % i, [128, 512], F32) for i in range(4)], "ev")
                ext_p = Pool([sb(st, "ext%d" % i, [128, 520], F32) for i in range(4)], "ext")
                ext2_p = Pool([sb(st, "ext2_%d" % i, [128, SB, 36], F32) for i in range(3)], "ext2")
                yc_p = Pool([sb(st, "yc%d" % i, [128, 512], F32) for i in range(3)], "yc")
                ys_p = Pool([sb(st, "ys%d" % i, [128, 512], F32) for i in range(5)], "ys")
                sqb_p = Pool([sb(st, "sqb%d" % i, [128, 512], BF16) for i in range(2)], "sqb")
                rs_p = Pool([sb(st, "rsq%d" % i, [128, 512], F32) for i in range(2)], "rsq")
                obf_p = Pool([sb(st, "obf%d" % i, [128, 512], BF16) for i in range(4)], "obf")
                gb_p = Pool([sb(st, "gbt%d" % i, [128, 2 * HB], F32) for i in range(2)], "gbt")
                tmp_p = Pool([sb(st, "gtmp%d" % i, [128, HB], F32) for i in range(2)], "gtmp")
                car = sb(st, "car", [128, 3 * HB, 4], F32)
                car_b = [Buf("car%d" % i) for i in range(3 * HB)]
                E.op("pool", lambda e: e.memset(car[:], 0.0), writes=car_b)

                def win_blocks():
                    return [(t0, min(512, TL - t0)) for t0 in range(0, TL, 512)]

                def own_blocks():
                    return [(t0, min(512, TL - t0)) for t0 in range(WL - OWN, TL, 512)]

                def load_W(src, ncols, c0, gw):
                    E.dma("pool", Wg[:, :, 0:gw],
                          dram_view(src, c0, [[ncols, 128], [128 * ncols, KC], [1, gw]]), writes=[Wg_b])

                def load_xn(t0, tw):
                    xnb, xnb_b = xnb_p.get()
                    E.dma("sp", xnb[:, :, 0:tw],
                          dram_view(xnT_s, t0, [[KC * TL, 128], [TL, KC], [1, tw]]),
                          reads=xns_deps(t0, tw), writes=[xnb_b])
                    return xnb, xnb_b

                def mm_F(xnb, xnb_b, tw, j):
                    pt, pt_b = mm_p.get()
                    for kc in range(KC):
                        E.op("pe", lambda e: e.matmul(pt[:, 0:tw], lhsT=Wg[:, kc, j * 128:(j + 1) * 128],
                                                      rhs=xnb[:, kc, 0:tw], start=(kc == 0), stop=(kc == KC - 1)),
                             reads=[Wg_b, xnb_b], writes=[pt_b])
                    return pt, pt_b

                def mm_T(xnb, xnb_b, sub, c0, gw):
                    pt, pt_b = mm_p.get()
                    for kc in range(KC):
                        E.op("pe", lambda e: e.matmul(pt[:, 0:gw], lhsT=xnb[:, kc, sub * 128:(sub + 1) * 128],
                                                      rhs=Wg[:, kc, c0:c0 + gw], start=(kc == 0), stop=(kc == KC - 1)),
                             reads=[Wg_b, xnb_b], writes=[pt_b])
                    return pt, pt_b

                def ev_ka(f, t0, tw, pt, pt_b):
                    kf, kf_b = f32_p.get()
                    E.op("act", lambda e: e.activation(out=kf[:, 0:tw], in_=pt[:, 0:tw], func=AF.Copy),
                         reads=[pt_b], writes=[kf_b])
                    E.dma("pool", kT_s[f * 128:(f + 1) * 128, t0:t0 + tw], kf[:, 0:tw], reads=[kf_b], writes=[kTs_b])
                    if t0 >= OK0:
                        E.dma("sp", kT_o[f * 128:(f + 1) * 128, t0 - OK0:t0 - OK0 + tw], kf[:, 0:tw],
                              reads=[kf_b], is_output=True)

                def ev_qa(f, t0, tw, pt, pt_b):
                    ob, ob_b = obf_p.get()
                    E.op("act", lambda e: e.activation(out=ob[:, 0:tw], in_=pt[:, 0:tw], func=AF.Copy),
                         reads=[pt_b], writes=[ob_b])
                    o0 = t0 - (WL - OWN)
                    E.dma("sp", qT_s[f * 128:(f + 1) * 128, o0:o0 + tw], ob[:, 0:tw], reads=[ob_b], writes=[qTs_b])

                def conv_tail(t, kind, hb, ext_v, n, yc_v, ys_v, wr, t0, s3=None):
                    ext_b, yc_b, ys_b = wr
                    E.op("dve", lambda e: e.tensor_scalar(out=yc_v, in0=ext_v(0), scalar1=convw_sb[:, t, 0:1],
                                                          scalar2=None, op0=ALU.mult),
                         reads=[ext_b, smallc_b], writes=[yc_b])
                    for i in range(1, 4):
                        E.op("dve", lambda e: e.scalar_tensor_tensor(out=yc_v, in0=ext_v(i),
                                                                     scalar=convw_sb[:, t, i:i + 1], in1=yc_v,
                                                                     op0=ALU.mult, op1=ALU.add),
                             reads=[ext_b, smallc_b, yc_b], writes=[yc_b])
                    E.op("act", lambda e: e.activation(out=ys_v, in_=yc_v, func=AF.Silu), reads=[yc_b], writes=[ys_b])

                def norm_store(t, kind, ys2, ys_b, n, t0):
                    ob, ob_b = obf_p.get()
                    if kind == 2:
                        E.op("pool", lambda e: e.tensor_copy(out=ob[:, 0:n], in_=ys2), reads=[ys_b], writes=[ob_b])
                    else:
                        sq, sq_b = sqb_p.get()
                        E.op("act", lambda e: e.activation(out=sq[:, 0:n], in_=ys2, func=AF.Square),
                             reads=[ys_b], writes=[sq_b])
                        ss, ss_b = ss_p.get()
                        E.op("pe", lambda e: e.matmul(ss[:, 0:n], lhsT=ones_bf, rhs=sq[:, 0:n], start=True, stop=True),
                             reads=[cbf_b, sq_b], writes=[ss_b])
                        rs, rs_b = rs_p.get()
                        E.op("act", lambda e: e.activation(out=rs[:, 0:n], in_=ss[:, 0:n], func=AF.Sqrt,
                                                           bias=eps_sb[:, 0:1]), reads=[ss_b, eps_b], writes=[rs_b])
                        E.op("dve", lambda e: e.reciprocal(out=rs[:, 0:n], in_=rs[:, 0:n]), reads=[rs_b], writes=[rs_b])
                        sc = (128 ** -0.5) if kind == 0 else 1.0
                        E.op("dve", lambda e: e.scalar_tensor_tensor(out=ob[:, 0:n], in0=ys2, scalar=sc, in1=rs[:, 0:n],
                                                                     op0=ALU.mult, op1=ALU.mult),
                             reads=[ys_b, rs_b], writes=[ob_b])
                    E.dma("pool", gT_s[t * 128:(t + 1) * 128, t0:t0 + n], ob[:, 0:n], reads=[ob_b], writes=[gTs_b])

                def ev_g(t, t0, tw, pt, pt_b):
                    kind = t // HB
                    nw = tw if t0 + tw <= WL else WL - t0
                    ext, ext_b = ext_p.get()
                    E.op("pool", lambda e: e.tensor_copy(out=ext[:, 0:3], in_=car[:, t, 0:3]),
                         reads=[car_b[t]], writes=[ext_b])
                    E.op("act", lambda e: e.activation(out=ext[:, 3:3 + nw], in_=pt[:, 0:nw], func=AF.Copy),
                         reads=[pt_b], writes=[ext_b])
                    E.op("pool", lambda e: e.tensor_copy(out=car[:, t, 0:3], in_=ext[:, nw:nw + 3]),
                         reads=[ext_b], writes=[car_b[t]])
                    if t0 + nw == WL:
                        E.dma("sp", conv_o[t * 128:(t + 1) * 128, :], ext[:, nw:nw + 3], reads=[ext_b], is_output=True)
                    has_s = nw < tw
                    if has_s:
                        assert tw - nw == ST
                        e2, e2_b = ext2_p.get()
                        E.op("pool", lambda e: e.tensor_copy(out=e2[:, :, 0:3], in_=convst_sb[:, t, :, :]),
                             reads=[smallc_b], writes=[e2_b])
                        E.op("act", lambda e: e.activation(out=e2[:, :, 3:35],
                                                           in_=pt[:, nw:tw].rearrange("p (b s) -> p b s", s=32),
                                                           func=AF.Copy), reads=[pt_b], writes=[e2_b])
                        E.dma("sp", conv_so[t * 128:(t + 1) * 128, :].rearrange("p (b s) -> p b s", s=3),
                              e2[:, :, 32:35], reads=[e2_b], is_output=True)
                    hold = {}

                    def stage_b():
                        yc, yc_b = yc_p.get()
                        ys, ys_b = ys_p.get()
                        conv_tail(t, kind, None, lambda i: ext[:, i:i + nw], nw, yc[:, 0:nw], ys[:, 0:nw],
                                  (ext_b, yc_b, ys_b), t0)
                        hold["ys"] = (ys, ys_b)
                        if has_s:
                            yc2, yc2_b = yc_p.get()
                            ys2, ys2_b = ys_p.get()
                            ycv = yc2[:, 0:ST].rearrange("p (b s) -> p b s", s=32)
                            ysv = ys2[:, 0:ST].rearrange("p (b s) -> p b s", s=32)
                            conv_tail(t, kind, None, lambda i: e2[:, :, i:i + 32], ST, ycv, ysv, (e2_b, yc2_b, ys2_b), t0)
                            hold["ys2"] = (ys2, ys2_b)

                    def stage_c():
                        ys, ys_b = hold["ys"]
                        norm_store(t, kind, ys[:, 0:nw], ys_b, nw, t0)
                        if has_s:
                            ys2, ys2_b = hold["ys2"]
                            norm_store(t, kind, ys2[:, 0:ST], ys2_b, ST, WL)

                    return [stage_b, stage_c]

                def ev_va(c0, gw, t0, sub, pt, pt_b):
                    vf, vf_b = f32_p.get()
                    E.op("dve", lambda e: e.tensor_copy(out=vf[:, 0:gw], in_=pt[:, 0:gw]), reads=[pt_b], writes=[vf_b])
                    r0 = t0 + sub * 128
                    E.dma("pool", v_s[r0:r0 + 128, c0:c0 + gw], vf[:, 0:gw], reads=[vf_b], writes=[vs_b])
                    if r0 >= OK0:
                        E.dma("sp", v_o[r0 - OK0:r0 - OK0 + 128, c0:c0 + gw], vf[:, 0:gw], reads=[vf_b], is_output=True)

                def ev_ba(t0, sub, pt, pt_b):
                    gbt, gbt_b = gb_p.get()
                    tmp, tmp_b = tmp_p.get()
                    E.op("act", lambda e: e.activation(out=gbt[:, 0:HB], in_=pt[:, 0:HB], func=AF.Sigmoid),
                         reads=[pt_b], writes=[gbt_b])
                    E.op("dve", lambda e: e.tensor_tensor(out=tmp[:], in0=pt[:, HB:2 * HB], in1=gc_sb[:, HB:2 * HB],
                                                          op=ALU.add), reads=[pt_b, smallc_b], writes=[tmp_b])
                    E.op("act", lambda e: e.activation(out=tmp[:], in_=tmp[:], func=AF.Exp), reads=[tmp_b], writes=[tmp_b])
                    E.op("act", lambda e: e.activation(out=tmp[:], in_=tmp[:], func=AF.Ln, bias=1.0),
                         reads=[tmp_b], writes=[tmp_b])
                    E.op("dve", lambda e: e.tensor_tensor(out=gbt[:, HB:2 * HB], in0=tmp[:], in1=nA_sb[:], op=ALU.mult),
                         reads=[tmp_b, nA_b], writes=[gbt_b])
                    r0 = t0 + sub * 128
                    E.dma("pool", gb_s[r0:r0 + 128, :], gbt[:], reads=[gbt_b], writes=[gbs_b])

                def ev_z(c0, gw, zc0, t0, sub, pt, pt_b):
                    ob, ob_b = obf_p.get()
                    E.op("act", lambda e: e.activation(out=ob[:, 0:gw], in_=pt[:, 0:gw], func=AF.Silu),
                         reads=[pt_b], writes=[ob_b])
                    r0 = t0 + sub * 128 - (WL - OWN)
                    E.dma("sp", zs_s[r0:r0 + 128, zc0:zc0 + gw], ob[:, 0:gw], reads=[ob_b], writes=[zss_b])

                GF = 8
                f_tiles = [("ka", f) for f in range(2 * HA)] + [("g", t) for t in range(3 * HB)]
                for g0 in range(0, len(f_tiles), GF):
                    grp = f_tiles[g0:g0 + GF]
                    load_W(wF, cfg.NF * 128, g0 * 128, len(grp) * 128)
                    pend = []
                    for (t0, tw) in win_blocks():
                        xnb, xnb_b = load_xn(t0, tw)
                        for j, (kind, idx) in enumerate(grp):
                            pt, pt_b = mm_F(xnb, xnb_b, tw, j)
                            if kind == "ka":
                                ev_ka(idx, t0, tw, pt, pt_b)
                            else:
                                for item in pend:
                                    item.pop(0)()
                                pend = [it for it in pend if it]
                                pend.append(ev_g(idx, t0, tw, pt, pt_b))
                    while pend:
                        for item in pend:
                            item.pop(0)()
                        pend = [it for it in pend if it]
                qa_tiles = list(range(2 * HA))
                for g0 in range(0, len(qa_tiles), GF):
                    grp = qa_tiles[g0:g0 + GF]
                    load_W(wF, cfg.NF * 128, (cfg.NF_W + g0) * 128, len(grp) * 128)
                    for (t0, tw) in own_blocks():
                        xnb, xnb_b = load_xn(t0, tw)
                        for j, f in enumerate(grp):
                            pt, pt_b = mm_F(xnb, xnb_b, tw, j)
                            ev_qa(f, t0, tw, pt, pt_b)
                segs = [("va", c, min(512, WA - c)) for c in range(0, WA, 512)] + [("ba", WA, 2 * HB)]
                groups = []
                cur, curw = [], 0
                for sg in segs:
                    if curw + sg[2] > WGW:
                        groups.append(cur)
                        cur, curw = [], 0
                    cur.append(sg)
                    curw += sg[2]
                groups.append(cur)
                for grp in groups:
                    gc0 = grp[0][1]
                    gw_tot = sum(sg[2] for sg in grp)
                    load_W(wT, cfg.NT, gc0, gw_tot)
                    for (t0, tw) in win_blocks():
                        xnb, xnb_b = load_xn(t0, tw)
                        for sub in range(tw // 128):
                            for (kind, c, w) in grp:
                                pt, pt_b = mm_T(xnb, xnb_b, sub, c - gc0, w)
                                if kind == "va":
                                    ev_va(c, w, t0, sub, pt, pt_b)
                                else:
                                    ev_ba(t0, sub, pt, pt_b)
                zsegs = [(c, min(512, WA + WB - c)) for c in range(0, WA + WB, 512)]
                for g0 in range(0, len(zsegs), 2):
                    grp = zsegs[g0:g0 + 2]
                    gc0 = grp[0][0]
                    gw_tot = sum(w for _, w in grp)
                    load_W(wT, cfg.NT, cfg.NT_W + gc0, gw_tot)
                    for (t0, tw) in own_blocks():
                        xnb, xnb_b = load_xn(t0, tw)
                        for sub in range(tw // 128):
                            for (c, w) in grp:
                                pt, pt_b = mm_T(xnb, xnb_b, sub, c - gc0, w)
                                ev_z(c - gc0, w, c, t0, sub, pt, pt_b)

        if 1 in phases:
            phase1()
            E.barrier()


        def phase2():
            with ExitStack() as st:
                QB0 = WL - OWN
                SCALE = 128 ** -0.5
                KTt = sb(st, "KTt", [128, 2, WL], BF16)
                KT_b = Buf("KTt")
                Vt = sb(st, "Vt", [128, NKT, 256], BF16)
                V_b = Buf("Vt")
                QTt = sb(st, "QTt", [128, 2, OT], BF16)
                QT_b = Buf("QTt")
                val_f = sb(st, "val_f", [128, NKT], F32)
                val_h = sb(st, "val_h", [128, NKT], BF16)
                val_b = Buf("val", const=True)
                E.dma("sp", val_f[:], valid.ap(), writes=[val_b])
                E.op("dve", lambda e: e.tensor_copy(out=val_h[:], in_=val_f[:]), reads=[val_b], writes=[val_b])
                dm_f = sb(st, "dm_f", [128, 4, 512], F32)
                dm_b = Buf("dm", const=True)
                E.dma("sp", dm_f[:], dmask.ap().rearrange("p (a j) -> p a j", j=512), writes=[dm_b])
                lv = sb(st, "lv", [128, 4, 128], F32)
                lam_t = sb(st, "lam_t", [128, 4], F32)
                sl_sb = sb(st, "sl_sb", [128, 256], F32)
                rb_sb = sb(st, "rb_sb", [128, HA], F32)
                misc_b = Buf("misc2", const=True)
                E.dma("sp", lv[:], dram_view(lamv, 0, [[0, 128], [128, 4], [1, 128]]), writes=[misc_b])
                E.dma("sp", sl_sb[:], dram_view(subln, 0, [[0, 128], [1, 256]]), writes=[misc_b])
                E.dma("sp", rb_sb[:], dram_view(relb, 15 * HA, [[0, 128], [1, HA]]), writes=[misc_b])
                E.op("dve", lambda e: e.tensor_tensor(out=lv[:, 0, :], in0=lv[:, 0, :], in1=lv[:, 1, :], op=ALU.mult),
                     reads=[misc_b], writes=[misc_b])
                E.op("dve", lambda e: e.tensor_tensor(out=lv[:, 2, :], in0=lv[:, 2, :], in1=lv[:, 3, :], op=ALU.mult),
                     reads=[misc_b], writes=[misc_b])
                E.op("dve", lambda e: e.tensor_reduce(out=lam_t[:, 0:1], in_=lv[:, 0, :], axis=AX.X, op=ALU.add),
                     reads=[misc_b], writes=[misc_b])
                E.op("dve", lambda e: e.tensor_reduce(out=lam_t[:, 1:2], in_=lv[:, 2, :], axis=AX.X, op=ALU.add),
                     reads=[misc_b], writes=[misc_b])
                E.op("act", lambda e: e.activation(out=lam_t[:, 0:2], in_=lam_t[:, 0:2], func=AF.Exp), reads=[misc_b], writes=[misc_b])
                E.op("dve", lambda e: e.tensor_tensor(out=lam_t[:, 2:3], in0=lam_t[:, 0:1], in1=lam_t[:, 1:2], op=ALU.subtract),
                     reads=[misc_b], writes=[misc_b])
                E.op("dve", lambda e: e.tensor_scalar(out=lam_t[:, 3:4], in0=lam_t[:, 2:3], scalar1=LAM_INIT, scalar2=-1.0,
                                                      op0=ALU.add, op1=ALU.mult), reads=[misc_b], writes=[misc_b])
                E.op("dve", lambda e: e.tensor_scalar(out=sl_sb[:], in0=sl_sb[:], scalar1=1.0 - LAM_INIT, scalar2=None, op0=ALU.mult),
                     reads=[misc_b], writes=[misc_b])
                sc_p = Pool([ps(st, "sct%d" % i, [128, 512]) for i in range(4)], "sct", psum=True)
                bvs_b = Buf("bv_s")
                with ExitStack() as st2:
                    relb_sb = sb(st2, "relb_sb", [N_BUCKETS, HA], F32)
                    oh_sb = sb(st2, "oh_sb", [N_BUCKETS, BV_LEN], F32)
                    bvt = sb(st2, "bvt", [HA, BV_LEN], F32)
                    bv_b = Buf("bv")
                    E.dma("sp", relb_sb[:], relb.ap(), writes=[bv_b])
                    E.dma("sp", oh_sb[:], oh.ap(), writes=[bv_b])
                    for c0 in range(0, BV_LEN, 512):
                        w = min(512, BV_LEN - c0)
                        pt, pt_b = sc_p.get()
                        E.op("pe", lambda e: e.matmul(pt[0:HA, 0:w], lhsT=relb_sb[:, :], rhs=oh_sb[:, c0:c0 + w], start=True, stop=True),
                             reads=[bv_b], writes=[pt_b])
                        E.op("act", lambda e: e.activation(out=bvt[:, c0:c0 + w], in_=pt[0:HA, 0:w], func=AF.Copy), reads=[pt_b], writes=[bv_b])
                    E.dma("sp", bv_s.ap(), bvt[:], reads=[bv_b], writes=[bvs_b])
                E.barrier()

                hk_p = Pool([sb(st, "hk%d" % i, [128, 512], F32) for i in range(2)], "hk")
                eb = sb(st, "eb", [128, 9, 512], BF16)
                eb_b = Buf("eb")
                NKS = (CL + 32 + 127) // 128
                ebs = sb(st, "ebs", [128, NKS, 32], BF16)
                ebs_b = Buf("ebs")
                ebtmp_p = Pool([sb(st, "ebtmp%d" % i, [128, 512], F32) for i in range(2)], "ebtmp")
                PT_p = Pool([sb(st, "PT%d" % i, [128, 512], BF16) for i in range(4)], "PT")
                oacc = [ps(st, "oacc%d" % i, [128, 512]) for i in range(2)]
                oden = ps(st, "oden", [128, 512])
                oacc_b = Buf("oacc", psum=True)
                o1 = sb(st, "o1", [128, 4, 256], F32)
                o1_b = Buf("o1")
                ofin_p = Pool([sb(st, "ofin%d" % i, [128, 256], F32) for i in range(2)], "ofin")
                osq = sb(st, "osq", [128, 256], F32)
                osq_b = Buf("osq")
                rd = sb(st, "rd", [128, 8], F32)
                rd_b = Buf("rd")
                st_p = Pool([sb(st, "ast%d" % i, [128, 2], F32) for i in range(2)], "ast")
                zs_p = Pool([sb(st, "azs%d" % i, [128, 256], BF16) for i in range(2)], "azs")
                ybf_p = Pool([sb(st, "aybf%d" % i, [128, 256], BF16) for i in range(2)], "aybf")
                yT_p = Pool([sb(st, "ayT%d" % i, [128, 2, 128], BF16) for i in range(2)], "ayT")
                tp_p = Pool([ps(st, "atp", [128, 1024], BF16)[:, 0:128]], "atp", psum=True)
                KTs = sb(st, "KTs", [128, 2, CL + 32], BF16)
                KTs_b = Buf("KTs")
                NKS = (CL + 32 + 127) // 128
                Vs = sb(st, "Vs", [128, NKS, 256], BF16)
                Vs_b = Buf("Vs")
                PTs_p = Pool([sb(st, "PTs%d" % i, [128, NKS, 32], BF16) for i in range(2)], "PTs")

                def build_bias(h):
                    for i, dl in enumerate(NEAR_D):
                        hk, hk_b = hk_p.get()
                        base = BV_R0 - 127 - dl
                        E.dma("sp", hk[:], dram_view(bv_s, h * BV_LEN + base, [[1, 128], [1, 512]]), reads=[bvs_b], writes=[hk_b])
                        pt, pt_b = sc_p.get()
                        E.op("pe", lambda e: e.matmul(pt[:], lhsT=cst_sb[:, C_AID:C_AID + 128], rhs=hk[:], start=True, stop=True),
                             reads=[hk_b, cst_b], writes=[pt_b])
                        if dl < 0:
                            E.op("act", lambda e: e.activation(out=eb[:, i, :], in_=pt[:], func=AF.Exp), reads=[pt_b], writes=[eb_b])
                        else:
                            t, t_b = ebtmp_p.get()
                            E.op("act", lambda e: e.activation(out=t[:], in_=pt[:], func=AF.Exp), reads=[pt_b], writes=[t_b])
                            E.op("dve", lambda e: e.tensor_tensor(out=eb[:, i, :], in0=t[:], in1=dm_f[:, dl // 128, :], op=ALU.mult),
                                 reads=[t_b, dm_b], writes=[eb_b])
                    for kt in range(NKS):
                        hk, hk_b = hk_p.get()
                        base = BV_R0 - 127 - (128 * kt - CL)
                        E.dma("sp", hk[:, 0:32], dram_view(bv_s, h * BV_LEN + base, [[1, 128], [1, 32]]), reads=[bvs_b], writes=[hk_b])
                        pt, pt_b = sc_p.get()
                        E.op("pe", lambda e: e.matmul(pt[:, 0:32], lhsT=cst_sb[:, C_AID:C_AID + 128], rhs=hk[:, 0:32], start=True, stop=True),
                             reads=[hk_b, cst_b], writes=[pt_b])
                        E.op("act", lambda e: e.activation(out=ebs[:, kt, :], in_=pt[:, 0:32], func=AF.Exp), reads=[pt_b], writes=[ebs_b])

                def finish_o(h, L, o_ps_list, den_cols, c, orow_list):
                    ns = len(o_ps_list)
                    for sub in range(ns):
                        E.op("dve", lambda e: e.reciprocal(out=rd[0:L, c * 4 + sub:c * 4 + sub + 1], in_=den_cols[sub]),
                             reads=[oacc_b], writes=[rd_b])
                    if c == 0:
                        for sub in range(ns):
                            E.op("act", lambda e: e.activation(out=o1[0:L, sub, :], in_=o_ps_list[sub], func=AF.Copy,
                                                               scale=rd[0:L, sub:sub + 1]), reads=[oacc_b, rd_b], writes=[o1_b])
                        return
                    E.op("dve", lambda e: e.tensor_scalar(out=rd[0:L, 4:4 + ns], in0=rd[0:L, 4:4 + ns], scalar1=lam_t[0:L, 3:4],
                                                          scalar2=None, op0=ALU.mult), reads=[rd_b, misc_b], writes=[rd_b])
                    for sub in range(ns):
                        of, of_b = ofin_p.get()
                        E.op("dve", lambda e: e.scalar_tensor_tensor(out=of[0:L, :], in0=o_ps_list[sub], scalar=rd[0:L, 4 + sub:5 + sub],
                                                                     in1=o1[0:L, sub, :], op0=ALU.mult, op1=ALU.add),
                             reads=[oacc_b, rd_b, o1_b], writes=[of_b])
                        stt, stt_b = st_p.get()
                        E.op("act", lambda e: e.activation(out=osq[0:L, :], in_=of[0:L, :], func=AF.Square, accum_out=stt[0:L, 0:1]),
                             reads=[of_b], writes=[osq_b, stt_b])
                        E.op("act", lambda e: e.activation(out=stt[0:L, 1:2], in_=stt[0:L, 0:1], func=AF.Sqrt, scale=1.0 / 256,
                                                           bias=eps_sb[0:L, 0:1]), reads=[stt_b, eps_b], writes=[stt_b])
                        E.op("dve", lambda e: e.reciprocal(out=stt[0:L, 1:2], in_=stt[0:L, 1:2]), reads=[stt_b], writes=[stt_b])
                        orow = orow_list[sub]
                        zs, zs_b = zs_p.get()
                        E.dma("sp", zs[0:L, :], zs_s[orow:orow + L, h * 256:(h + 1) * 256], reads=[zss_b], writes=[zs_b])
                        E.op("dve", lambda e: e.scalar_tensor_tensor(out=of[0:L, :], in0=of[0:L, :], scalar=stt[0:L, 1:2], in1=sl_sb[0:L, :],
                                                                     op0=ALU.mult, op1=ALU.mult), reads=[of_b, stt_b, misc_b], writes=[of_b])
                        yb, yb_b = ybf_p.get()
                        E.op("dve", lambda e: e.tensor_tensor(out=yb[0:L, :], in0=of[0:L, :], in1=zs[0:L, :], op=ALU.mult),
                             reads=[of_b, zs_b], writes=[yb_b])
                        yT, yT_b = yT_p.get()
                        for e2 in range(2):
                            tp, tp_b = tp_p.get()
                            E.op("pe", lambda e: e.transpose(tp[:, 0:L], yb[0:L, e2 * 128:(e2 + 1) * 128], ident_bf[0:L, 0:L]),
                                 reads=[yb_b, cbf_b], writes=[tp_b])
                            E.op("act", lambda e: e.activation(out=yT[:, e2, 0:L], in_=tp[:, 0:L], func=AF.Copy), reads=[tp_b], writes=[yT_b])
                        E.dma("pool", dram_view(yT_s, h * 256 * OT + orow, [[OT, 128], [128 * OT, 2], [1, L]]), yT[:, :, 0:L],
                              reads=[yT_b], writes=[yTs_b])

                for h in range(HA):
                    E.dma("sp", KTt[:], dram_view(kT_s, h * 256 * TL, [[TL, 128], [128 * TL, 2], [1, WL]]), reads=[kTs_b], writes=[KT_b])
                    for kq in range(0, NKT, 32):
                        nk = min(32, NKT - kq)
                        E.dma("sp", Vt[:, kq:kq + nk, :],
                              dram_view(v_s, kq * 128 * WA + h * 256, [[WA, 128], [128 * WA, nk], [1, 256]]), reads=[vs_b], writes=[V_b])
                    E.dma("sp", QTt[:], dram_view(qT_s, h * 256 * OT, [[OT, 128], [128 * OT, 2], [1, OT]]), reads=[qTs_b], writes=[QT_b])
                    build_bias(h)
                    for qb in range(OWN // 512):
                        q0 = qb * 512
                        kt_hi = (QB0 + q0) // 128 + 4
                        for c in range(2):
                            for a in oacc + [oden]:
                                E.op("dve", lambda e: e.memset(a[:], 0.0), writes=[oacc_b])
                            def emit_pv(kt, PT, PT_b):
                                for sub in range(4):
                                    lt = PT[:, sub * 128:(sub + 1) * 128]
                                    E.op("pe", lambda e: e.matmul(oacc[sub // 2][:, (sub % 2) * 256:(sub % 2 + 1) * 256], lhsT=lt,
                                                                  rhs=Vt[:, kt, :], start=False, stop=(kt == kt_hi - 1),
                                                                  skip_group_check=True), reads=[PT_b, V_b], writes=[oacc_b])
                                    E.op("pe", lambda e: e.matmul(oden[:, sub:sub + 1], lhsT=lt, rhs=val_h[:, kt:kt + 1], start=False,
                                                                  stop=(kt == kt_hi - 1), skip_group_check=True),
                                         reads=[PT_b, val_b], writes=[oacc_b])

                            prev = None
                            for kt in range(kt_hi):
                                dl = 128 * kt - QB0 - q0
                                pt, pt_b = sc_p.get()
                                E.op("pe", lambda e: e.matmul(pt[:], lhsT=KTt[:, c, kt * 128:(kt + 1) * 128], rhs=QTt[:, c, q0:q0 + 512],
                                                              start=True, stop=True), reads=[KT_b, QT_b], writes=[pt_b])
                                PT, PT_b = PT_p.get()
                                if dl < NEAR_D[0]:
                                    E.op("act", lambda e: e.activation(out=PT[:], in_=pt[:], func=AF.Exp, scale=SCALE,
                                                                       bias=rb_sb[:, h:h + 1]), reads=[pt_b, misc_b], writes=[PT_b])
                                else:
                                    E.op("act", lambda e: e.activation(out=PT[:], in_=pt[:], func=AF.Exp, scale=SCALE),
                                         reads=[pt_b], writes=[PT_b])
                                    E.op("pool", lambda e: e.tensor_tensor(out=PT[:], in0=PT[:], in1=eb[:, NEAR_D.index(dl), :], op=ALU.mult),
                                         reads=[PT_b, eb_b], writes=[PT_b])
                                if prev is not None:
                                    emit_pv(*prev)
                                prev = (kt, PT, PT_b)
                            emit_pv(*prev)
                            finish_o(h, 128, [oacc[sub // 2][:, (sub % 2) * 256:(sub % 2 + 1) * 256] for sub in range(4)],
                                     [oden[:, sub:sub + 1] for sub in range(4)], c, [q0 + sub * 128 for sub in range(4)])
                    for b in range(SB):
                        E.dma("pool", KTs[:, :, 0:CL],
                              dram_view(ckT, (b * 2 * HA + 2 * h) * 128 * CL, [[CL, 128], [128 * CL, 2], [1, CL]]), writes=[KTs_b])
                        E.dma("sp", KTs[:, :, CL:CL + 32],
                              dram_view(kT_s, h * 256 * TL + WL + 32 * b, [[TL, 128], [128 * TL, 2], [1, 32]]), reads=[kTs_b], writes=[KTs_b])
                        nfull = CL // 128
                        rem = CL - nfull * 128
                        E.dma("pool", Vs[:, 0:nfull, :],
                              dram_view(cv, b * CL * WA + h * 256, [[WA, 128], [128 * WA, nfull], [1, 256]]), writes=[Vs_b])
                        if rem:
                            E.dma("pool", Vs[0:rem, nfull, :],
                                  dram_view(cv, (b * CL + nfull * 128) * WA + h * 256, [[WA, rem], [1, 256]]), writes=[Vs_b])
                        E.dma("sp", Vs[rem:rem + 32, nfull, :],
                              dram_view(v_s, (WL + 32 * b) * WA + h * 256, [[WA, 32], [1, 256]]), reads=[vs_b], writes=[Vs_b])
                        qc0 = OWN + 32 * b
                        for c in range(2):
                            pt, pt_b = sc_p.get()
                            for kt in range(NKS):
                                n = min(128, CL + 32 - kt * 128)
                                E.op("pe", lambda e: e.matmul(pt[0:n, kt * 32:(kt + 1) * 32], lhsT=KTs[:, c, kt * 128:kt * 128 + n],
                                                              rhs=QTt[:, c, qc0:qc0 + 32], start=True, stop=True),
                                     reads=[KTs_b, QT_b], writes=[pt_b])
                            PTs, PTs_b = PTs_p.get()
                            E.op("act", lambda e: e.activation(out=PTs[:], in_=pt[:, 0:NKS * 32].rearrange("p (k q) -> p k q", q=32),
                                                               func=AF.Exp, scale=SCALE), reads=[pt_b], writes=[PTs_b])
                            E.op("pool", lambda e: e.tensor_tensor(out=PTs[:], in0=PTs[:], in1=ebs[:], op=ALU.mult),
                                 reads=[PTs_b, ebs_b], writes=[PTs_b])
                            for kt in range(NKS):
                                n = min(128, CL + 32 - kt * 128)
                                E.op("pe", lambda e: e.matmul(oacc[0][0:32, 0:256], lhsT=PTs[0:n, kt, :], rhs=Vs[0:n, kt, :],
                                                              start=(kt == 0), stop=(kt == NKS - 1)), reads=[PTs_b, Vs_b], writes=[oacc_b])
                            for kt in range(NKS):
                                n = min(128, CL + 32 - kt * 128)
                                E.op("pe", lambda e: e.matmul(oden[0:32, 0:1], lhsT=PTs[0:n, kt, :], rhs=ones_bf[0:n, 0:1],
                                                              start=(kt == 0), stop=(kt == NKS - 1)), reads=[PTs_b, cbf_b], writes=[oacc_b])
                            finish_o(h, 32, [oacc[0][0:32, 0:256]], [oden[0:32, 0:1]], c, [qc0])

        if 2 in phases:
            phase2()
            E.barrier()

        def phase3():
            with ExitStack() as st:
                GH = min(16, HB)
                S_f = sb(st, "S_f", [128, HB, 128], F32)
                S_h = sb(st, "S_h", [128, HB, 128], BF16)
                S_b = [Buf("S%d" % h) for h in range(HB)]
                Sh_b = [Buf("Sh%d" % h) for h in range(HB)]
                nb_sb = sb(st, "nb_sb", [128, 128], F32)
                nb_b = Buf("nb", const=True)
                E.dma("sp", nb_sb[:], dram_view(normb, 0, [[0, 128], [1, 128]]), writes=[nb_b])
                gbt_p = Pool([sb(st, "g3bt%d" % i, [128, 2 * HB], F32) for i in range(2)], "g3bt")
                qkv_p = Pool([sb(st, "qkv%d" % i, [128, 3 * HB, 128], BF16) for i in range(2)], "qkv")
                zs_p = Pool([sb(st, "zs%d" % i, [128, WB], BF16) for i in range(1)], "zs")
                sc_p = Pool([sb(st, "sc%d" % i, [128, 6 * HB], F32) for i in range(2)], "sc")
                o_p = Pool([sb(st, "osb%d" % i, [128, HB, 128], F32) for i in range(1)], "osb", nsub=HB)
                sq_t = sb(st, "o_sq", [128, HB, 128], F32)
                sq_tb = Buf("o_sq")
                rst_p = Pool([sb(st, "orst%d" % i, [128, 2 * HB], F32) for i in range(2)], "orst")
                yb_p = Pool([sb(st, "ybf%d" % i, [128, HB, 128], BF16) for i in range(1)], "ybf")
                yT_p = Pool([sb(st, "yTt%d" % i, [128, HB, 128], BF16) for i in range(1)], "yTt", nsub=HB)
                scps_p = Pool([ps(st, "scps", [128, 512])], "scps", psum=True)
                pf_p = Pool([ps(st, "pfb%d" % i, [128, 512])[:, 0:128] for i in range(5)], "pf", psum=True)
                pb_p = Pool([ps(st, "pbf%d" % i, [128, 1024], BF16)[:, 0:128] for i in range(2)], "pb", psum=True)
                NHS = GH
                TDT = F32 if GDN_FP32 else BF16
                ident_t = cst_sb[:, C_ID:C_ID + 128] if GDN_FP32 else ident_bf
                pT_p = pf_p if GDN_FP32 else pb_p
                def hs_tiles(i):
                    d = {}
                    for nm in ("kt", "gams", "gamT", "WT", "vnew", "AqkT"):
                        d[nm] = (sb(st, "h%d_%s" % (i, nm), [128, 128], BF16), Buf("h%d_%s" % (i, nm)))
                    for nm in ("kbe", "vb", "N", "M", "X0", "X1", "XT0", "XT1", "P0", "P1"):
                        d[nm] = (sb(st, "h%d_%s" % (i, nm), [128, 128], TDT), Buf("h%d_%s" % (i, nm)))
                    for nm in ("gtri", "U", "t1"):
                        d[nm] = (sb(st, "h%d_%s" % (i, nm), [128, 128], F32), Buf("h%d_%s" % (i, nm)))
                    return d
                HS = [hs_tiles(i) for i in range(NHS)]
                triu = cst_sb[:, C_TRIU:C_TRIU + 128]
                lstr = cst_sb[:, C_LSTR:C_LSTR + 128]
                linc = cst_sb[:, C_LINC:C_LINC + 128]
                ones_f = cst_sb[:, C_ONE:C_ONE + 128]
                lstr_bf = cbf[:, C_LSTR:C_LSTR + 128]
                triu_bf = cbf[:, C_TRIU:C_TRIU + 128]

                def chunk(t0, L, need_o, orow0):
                    nfac = int(math.ceil(math.log2(L)))
                    gbt, gbt_b = gbt_p.get()
                    E.dma("sp", gbt[0:L, :], gb_s[t0:t0 + L, :], reads=[gbs_b], writes=[gbt_b])
                    qkv, qkv_b = qkv_p.get()
                    E.dma("sp", qkv[:, :, 0:L], dram_view(gT_s, t0, [[TL, 128], [128 * TL, 3 * HB], [1, L]]),
                          reads=[gTs_b], writes=[qkv_b])
                    if need_o:
                        zs, zs_b = zs_p.get()
                        E.dma("sp", zs[0:L, :], zs_s[orow0:orow0 + L, WA:WA + WB], reads=[zss_b], writes=[zs_b])
                    scps, scps_b = scps_p.get()
                    E.op("pe", lambda e: e.matmul(scps[0:L, 0:HB], lhsT=triu[0:L, 0:L], rhs=gbt[0:L, HB:2 * HB],
                                                  start=True, stop=True), reads=[cst_b, gbt_b], writes=[scps_b])
                    E.op("pe", lambda e: e.matmul(scps[:, HB:2 * HB], lhsT=ones_f[0:L, :], rhs=gbt[0:L, HB:2 * HB],
                                                  start=True, stop=True), reads=[cst_b, gbt_b], writes=[scps_b])
                    sc, sc_b = sc_p.get()
                    c_eG, c_eGLG, c_eGL, c_beG, c_nb, c_tmp = [slice(i * HB, (i + 1) * HB) for i in range(6)]
                    E.op("act", lambda e: e.activation(out=sc[0:L, c_eG], in_=scps[0:L, 0:HB], func=AF.Exp),
                         reads=[scps_b], writes=[sc_b])
                    E.op("act", lambda e: e.activation(out=sc[0:L, c_tmp], in_=scps[0:L, 0:HB], func=AF.Copy),
                         reads=[scps_b], writes=[sc_b])
                    E.op("dve", lambda e: e.tensor_tensor(out=sc[0:L, c_tmp], in0=scps[0:L, HB:2 * HB], in1=sc[0:L, c_tmp],
                                                          op=ALU.subtract), reads=[scps_b, sc_b], writes=[sc_b])
                    E.op("act", lambda e: e.activation(out=sc[0:L, c_eGLG], in_=sc[0:L, c_tmp], func=AF.Exp),
                         reads=[sc_b], writes=[sc_b])
                    E.op("act", lambda e: e.activation(out=sc[:, c_eGL], in_=scps[:, HB:2 * HB], func=AF.Exp),
                         reads=[scps_b], writes=[sc_b])
                    E.op("dve", lambda e: e.tensor_tensor(out=sc[0:L, c_beG], in0=sc[0:L, c_eG], in1=gbt[0:L, 0:HB],
                                                          op=ALU.mult), reads=[sc_b, gbt_b], writes=[sc_b])
                    E.op("dve", lambda e: e.tensor_scalar(out=sc[0:L, c_nb], in0=gbt[0:L, 0:HB], scalar1=-1.0, scalar2=None,
                                                          op0=ALU.mult), reads=[gbt_b], writes=[sc_b])
                    if need_o:
                        osb, osb_bs = o_p.get()

                    def col(cs, h):
                        return sc[0:L, cs.start + h:cs.start + h + 1]

                    for g0 in range(0, HB, GH):
                        heads = list(range(g0, min(HB, g0 + GH)))
                        T = {h: HS[h - g0] for h in heads}
                        QT = {h: qkv[:, h, 0:L] for h in heads}
                        KT = {h: qkv[:, HB + h, 0:L] for h in heads}
                        VT = {h: qkv[:, 2 * HB + h, 0:L] for h in heads}
                        for h in heads:
                            pk, pk_b = pb_p.get()
                            E.op("pe", lambda e: e.transpose(pk[0:L, :], KT[h], ident_bf), reads=[qkv_b, cbf_b], writes=[pk_b])
                            t, b = T[h]["kbe"]
                            E.op("act", lambda e: e.activation(out=t[0:L, :], in_=pk[0:L, :], func=AF.Copy, scale=col(c_beG, h)),
                                 reads=[pk_b, sc_b], writes=[b])
                            t, b = T[h]["kt"]
                            E.op("dve", lambda e: e.tensor_scalar(out=t[0:L, :], in0=pk[0:L, :], scalar1=col(c_eGLG, h),
                                                                  scalar2=None, op0=ALU.mult), reads=[pk_b, sc_b], writes=[b])
                            pv, pv_b = pb_p.get()
                            E.op("pe", lambda e: e.transpose(pv[0:L, :], VT[h], ident_bf), reads=[qkv_b, cbf_b], writes=[pv_b])
                            t, b = T[h]["vb"]
                            E.op("act", lambda e: e.activation(out=t[0:L, :], in_=pv[0:L, :], func=AF.Copy,
                                                               scale=gbt[0:L, h:h + 1]), reads=[pv_b, gbt_b], writes=[b])
                        for h in heads:
                            gt, gt_b = T[h]["gtri"]
                            E.op("pool", lambda e: e.tensor_scalar(out=gt[0:L, 0:L], in0=triu[0:L, 0:L],
                                                                   scalar1=gbt[0:L, HB + h:HB + h + 1], scalar2=None,
                                                                   op0=ALU.mult), reads=[cst_b, gbt_b], writes=[gt_b])
                            pg, pg_b = pf_p.get()
                            E.op("pe", lambda e: e.matmul(pg[0:L, 0:L], lhsT=gt[0:L, 0:L], rhs=lstr[0:L, 0:L], start=True, stop=True),
                                 reads=[gt_b, cst_b], writes=[pg_b])
                            gm, gm_b = T[h]["U"]
                            E.op("act", lambda e: e.activation(out=gm[0:L, 0:L], in_=pg[0:L, 0:L], func=AF.Exp),
                                 reads=[pg_b], writes=[gm_b])
                            gs, gs_b = T[h]["gams"]
                            E.op("pool", lambda e: e.tensor_tensor(out=gs[0:L, 0:L], in0=gm[0:L, 0:L], in1=lstr[0:L, 0:L], op=ALU.mult),
                                 reads=[gm_b, cst_b], writes=[gs_b])
                            pkk, pkk_b = pf_p.get()
                            E.op("pe", lambda e: e.matmul(pkk[0:L, 0:L], lhsT=KT[h], rhs=KT[h], start=True, stop=True),
                                 reads=[qkv_b], writes=[pkk_b])
                            n_, n_b = T[h]["N"]
                            E.op("dve", lambda e: e.scalar_tensor_tensor(out=n_[0:L, 0:L], in0=pkk[0:L, 0:L], scalar=col(c_nb, h),
                                                                         in1=gs[0:L, 0:L], op0=ALU.mult, op1=ALU.mult),
                                 reads=[pkk_b, sc_b, gs_b], writes=[n_b])
                            if need_o:
                                pgt, pgt_b = pf_p.get()
                                E.op("pe", lambda e: e.matmul(pgt[0:L, 0:L], lhsT=lstr[0:L, 0:L], rhs=gt[0:L, 0:L], start=True, stop=True),
                                     reads=[gt_b, cst_b], writes=[pgt_b])
                                t1, t1_b = T[h]["t1"]
                                E.op("act", lambda e: e.activation(out=t1[0:L, 0:L], in_=pgt[0:L, 0:L], func=AF.Exp),
                                     reads=[pgt_b], writes=[t1_b])
                                gT_, gT_b = T[h]["gamT"]
                                E.op("pool", lambda e: e.tensor_tensor(out=gT_[0:L, 0:L], in0=t1[0:L, 0:L], in1=triu[0:L, 0:L], op=ALU.mult),
                                     reads=[t1_b, cst_b], writes=[gT_b])
                        cur = {}
                        for h in heads:
                            n_, n_b = T[h]["N"]
                            pm, pm_b = pT_p.get()
                            E.op("pe", lambda e: e.transpose(pm[0:L, 0:L], n_[0:L, 0:L], ident_t[0:L, 0:L]), reads=[n_b, cbf_b, cst_b], writes=[pm_b])
                            m_, m_b = T[h]["M"]
                            E.op("act", lambda e: e.activation(out=m_[0:L, 0:L], in_=pm[0:L, 0:L], func=AF.Copy), reads=[pm_b], writes=[m_b])
                            p0, p0_b = T[h]["P0"]
                            E.op("dve", lambda e: e.tensor_tensor(out=p0[0:L, 0:L], in0=pm[0:L, 0:L], in1=ident_t[0:L, 0:L], op=ALU.add),
                                 reads=[pm_b, cbf_b, cst_b], writes=[p0_b])
                            cur[h] = dict(X=T[h]["M"], XT=T[h]["N"], P=T[h]["P0"], pi=0, xi=0)
                        for k in range(1, nfac):
                            last = (k == nfac - 1)
                            for h in heads:
                                c = cur[h]
                                (X, X_b), (XT, XT_b), (P, P_b) = c["X"], c["XT"], c["P"]
                                nXT = T[h]["XT%d" % c["xi"]]
                                pxt, pxt_b = pf_p.get()
                                E.op("pe", lambda e: e.matmul(pxt[0:L, 0:L], lhsT=X[0:L, 0:L], rhs=XT[0:L, 0:L], start=True, stop=True),
                                     reads=[X_b, XT_b], writes=[pxt_b])
                                E.op("dve", lambda e: e.tensor_copy(out=nXT[0][0:L, 0:L], in_=pxt[0:L, 0:L]), reads=[pxt_b], writes=[nXT[1]])
                                if not last:
                                    nX = T[h]["X%d" % c["xi"]]
                                    px, px_b = pf_p.get()
                                    E.op("pe", lambda e: e.matmul(px[0:L, 0:L], lhsT=XT[0:L, 0:L], rhs=X[0:L, 0:L], start=True, stop=True),
                                         reads=[X_b, XT_b], writes=[px_b])
                                    E.op("act", lambda e: e.activation(out=nX[0][0:L, 0:L], in_=px[0:L, 0:L], func=AF.Copy),
                                         reads=[px_b], writes=[nX[1]])
                                    c["X"] = nX
                                c["XT"] = nXT
                                c["xi"] ^= 1
                            for h in heads:
                                c = cur[h]
                                (XT, XT_b), (P, P_b) = c["XT"], c["P"]
                                nP = T[h]["P%d" % (c["pi"] ^ 1)]
                                pp, pp_b = pf_p.get()
                                E.op("pe", lambda e: e.matmul(pp[0:L, 0:L], lhsT=XT[0:L, 0:L], rhs=P[0:L, 0:L], start=True, stop=True),
                                     reads=[XT_b, P_b], writes=[pp_b])
                                E.op("dve", lambda e: e.tensor_tensor(out=nP[0][0:L, 0:L], in0=pp[0:L, 0:L], in1=P[0:L, 0:L], op=ALU.add),
                                     reads=[pp_b, P_b], writes=[nP[1]])
                                c["P"] = nP
                                c["pi"] ^= 1
                        for h in heads:
                            P, P_b = cur[h]["P"]
                            pu, pu_b = pf_p.get()
                            vb, vb_b = T[h]["vb"]
                            E.op("pe", lambda e: e.matmul(pu[0:L, :], lhsT=P[0:L, 0:L], rhs=vb[0:L, :], start=True, stop=True),
                                 reads=[P_b, vb_b], writes=[pu_b])
                            U, U_b = T[h]["U"]
                            E.op("act", lambda e: e.activation(out=U[0:L, :], in_=pu[0:L, :], func=AF.Copy), reads=[pu_b], writes=[U_b])
                            pw, pw_b = pf_p.get()
                            kbe, kbe_b = T[h]["kbe"]
                            E.op("pe", lambda e: e.matmul(pw[:, 0:L], lhsT=kbe[0:L, :], rhs=P[0:L, 0:L], start=True, stop=True),
                                 reads=[P_b, kbe_b], writes=[pw_b])
                            WT, WT_b = T[h]["WT"]
                            E.op("dve", lambda e: e.tensor_copy(out=WT[:, 0:L], in_=pw[:, 0:L]), reads=[pw_b], writes=[WT_b])
                        for h in heads:
                            WT, WT_b = T[h]["WT"]
                            pws, pws_b = pf_p.get()
                            E.op("pe", lambda e: e.matmul(pws[0:L, :], lhsT=WT[:, 0:L], rhs=S_h[:, h, :], start=True, stop=True),
                                 reads=[WT_b, Sh_b[h]], writes=[pws_b])
                            U, U_b = T[h]["U"]
                            vn, vn_b = T[h]["vnew"]
                            E.op("dve", lambda e: e.tensor_tensor(out=vn[0:L, :], in0=U[0:L, :], in1=pws[0:L, :], op=ALU.subtract),
                                 reads=[U_b, pws_b], writes=[vn_b])
                            if need_o:
                                pqs, pqs_b = pf_p.get()
                                E.op("pe", lambda e: e.matmul(pqs[0:L, :], lhsT=QT[h], rhs=S_h[:, h, :], start=True, stop=True),
                                     reads=[qkv_b, Sh_b[h]], writes=[pqs_b])
                                t1, t1_b = T[h]["t1"]
                                E.op("act", lambda e: e.activation(out=t1[0:L, :], in_=pqs[0:L, :], func=AF.Copy, scale=col(c_eG, h)),
                                     reads=[pqs_b, sc_b], writes=[t1_b])
                                pqk, pqk_b = pf_p.get()
                                E.op("pe", lambda e: e.matmul(pqk[0:L, 0:L], lhsT=KT[h], rhs=QT[h], start=True, stop=True),
                                     reads=[qkv_b], writes=[pqk_b])
                                aq, aq_b = T[h]["AqkT"]
                                gT_, gT_b = T[h]["gamT"]
                                E.op("dve", lambda e: e.tensor_tensor(out=aq[0:L, 0:L], in0=pqk[0:L, 0:L], in1=gT_[0:L, 0:L], op=ALU.mult),
                                     reads=[pqk_b, gT_b], writes=[aq_b])
                        for h in heads:
                            vn, vn_b = T[h]["vnew"]
                            if need_o:
                                aq, aq_b = T[h]["AqkT"]
                                pav, pav_b = pf_p.get()
                                E.op("pe", lambda e: e.matmul(pav[0:L, :], lhsT=aq[0:L, 0:L], rhs=vn[0:L, :], start=True, stop=True),
                                     reads=[aq_b, vn_b], writes=[pav_b])
                                t1, t1_b = T[h]["t1"]
                                E.op("dve", lambda e: e.tensor_tensor(out=osb[0:L, h, :], in0=pav[0:L, :], in1=t1[0:L, :], op=ALU.add),
                                     reads=[pav_b, t1_b], writes=[osb_bs[h]])
                            kt, kt_b = T[h]["kt"]
                            psu, psu_b = pf_p.get()
                            E.op("pe", lambda e: e.matmul(psu[:, :], lhsT=kt[0:L, :], rhs=vn[0:L, :], start=True, stop=True),
                                 reads=[kt_b, vn_b], writes=[psu_b])
                            E.op("dve", lambda e: e.scalar_tensor_tensor(out=S_f[:, h, :], in0=S_f[:, h, :],
                                                                         scalar=sc[:, c_eGL.start + h:c_eGL.start + h + 1],
                                                                         in1=psu[:, :], op0=ALU.mult, op1=ALU.add),
                                 reads=[psu_b, sc_b, S_b[h]], writes=[S_b[h]])
                            E.op("pool", lambda e: e.tensor_copy(out=S_h[:, h, :], in_=S_f[:, h, :]), reads=[S_b[h]], writes=[Sh_b[h]])
                    if need_o:
                        E.op("act", lambda e: e.activation(out=sq_t[0:L], in_=osb[0:L], func=AF.Square), reads=osb_bs, writes=[sq_tb])
                        rst, rst_b = rst_p.get()
                        E.op("dve", lambda e: e.tensor_reduce(out=rst[0:L, 0:HB], in_=sq_t[0:L], axis=AX.X, op=ALU.add),
                             reads=[sq_tb], writes=[rst_b])
                        E.op("act", lambda e: e.activation(out=rst[0:L, 0:HB], in_=rst[0:L, 0:HB], func=AF.Sqrt, scale=1.0 / 128,
                                                           bias=eps_sb[0:L, 0:1]), reads=[rst_b, eps_b], writes=[rst_b])
                        E.op("dve", lambda e: e.reciprocal(out=rst[0:L, 0:HB], in_=rst[0:L, 0:HB]), reads=[rst_b], writes=[rst_b])
                        E.op("dve", lambda e: e.tensor_tensor(out=sq_t[0:L], in0=osb[0:L],
                                                              in1=rst[0:L, 0:HB].unsqueeze(2).to_broadcast([L, HB, 128]), op=ALU.mult),
                             reads=osb_bs + [rst_b], writes=[sq_tb])
                        E.op("pool", lambda e: e.tensor_tensor(out=sq_t[0:L], in0=sq_t[0:L],
                                                               in1=nb_sb[0:L, :].unsqueeze(1).to_broadcast([L, HB, 128]), op=ALU.mult),
                             reads=[sq_tb, nb_b], writes=[sq_tb])
                        yb, yb_b = yb_p.get()
                        E.op("dve", lambda e: e.tensor_tensor(out=yb[0:L], in0=sq_t[0:L],
                                                              in1=zs[0:L, :].rearrange("p (h d) -> p h d", d=128), op=ALU.mult),
                             reads=[sq_tb, zs_b], writes=[yb_b])
                        yT, yT_bs = yT_p.get()
                        for h in range(HB):
                            pt_, pt_b = pb_p.get()
                            E.op("pe", lambda e: e.transpose(pt_[:, 0:L], yb[0:L, h, :], ident_bf[0:L, 0:L]), reads=[yb_b, cbf_b], writes=[pt_b])
                            E.op("act", lambda e: e.activation(out=yT[:, h, 0:L], in_=pt_[:, 0:L], func=AF.Copy), reads=[pt_b], writes=[yT_bs[h]])
                        E.dma("pool", dram_view(yT_s, WA * OT + orow0, [[OT, 128], [128 * OT, HB], [1, L]]), yT[:, :, 0:L],
                              reads=yT_bs, writes=[yTs_b])

                E.op("pool", lambda e: e.memset(S_f[:], 0.0), writes=S_b)
                E.op("pool", lambda e: e.memset(S_h[:], 0.0), writes=Sh_b)
                for t0 in range(0, WL, 128):
                    need = t0 >= WL - OWN
                    chunk(t0, 128, need, t0 - (WL - OWN))
                E.dma("sp", ssm_o.ap().rearrange("(h p) d -> p h d", p=128), S_f[:], reads=S_b, is_output=True)
                for b in range(SB):
                    E.dma("sp", S_f[:], ssm[b * HB * 128:(b + 1) * HB * 128, :].rearrange("(h p) d -> p h d", p=128), writes=S_b)
                    E.op("pool", lambda e: e.tensor_copy(out=S_h[:], in_=S_f[:]), reads=S_b, writes=Sh_b)
                    chunk(WL + 32 * b, 32, True, OWN + 32 * b)
                    E.dma("sp", ssm_so[b * HB * 128:(b + 1) * HB * 128, :].rearrange("(h p) d -> p h d", p=128), S_f[:],
                          reads=S_b, is_output=True)

        if 3 in phases:
            phase3()
            E.barrier()


        def phase4():
            with ExitStack() as st:
                pnb = sb(st, "pnb", [128, D], F32)
                pnb_b = Buf("pnb", const=True)
                E.dma("sp", pnb[:], dram_view(postn, 0, [[0, 128], [1, D]]), writes=[pnb_b])
                TG = 256
                YT_p = Pool([sb(st, "YTg%d" % i, [128, KC, TG], BF16) for i in range(1)], "YTg")
                wo_p = Pool([sb(st, "wo%d" % i, [128, KC, 512], BF16) for i in range(2)], "wo")
                yacc = sb(st, "yacc", [128, TG // 128, D], F32)
                yacc_b = [Buf("yacc%d" % i) for i in range(4)]
                ssq = sb(st, "ssq4", [128, 4, D // 512], F32)
                ssq_b = [Buf("ssq4_%d" % i) for i in range(4)]
                junk = sb(st, "junk4", [128, 512], F32)
                junk_b = Buf("junk4")
                r4 = sb(st, "r4", [128, 4, 2], F32)
                xo_p = Pool([sb(st, "xo%d" % i, [128, D], F32) for i in range(2)], "xo")
                mm_p = Pool([ps(st, "m4_%d" % i, [128, 512]) for i in range(4)], "m4", psum=True)
                NCB = D // 512
                if os.environ.get("P4ZERO"):
                    zt = sb(st, "zt", [128, OT], BF16)
                    zt_b = Buf("zt")
                    E.op("dve", lambda e: e.memset(zt[:], 0.0), writes=[zt_b])
                    for kc in range(KC):
                        E.dma("sp", yT_s[kc * 128:(kc + 1) * 128, :], zt[:], reads=[zt_b], writes=[yTs_b])
                for g0 in range(0, OT, TG):
                    gw = min(TG, OT - g0)
                    nsub = gw // 128
                    YT, YT_b = YT_p.get()
                    E.dma("sp", YT[:, :, 0:gw], dram_view(yT_s, g0, [[OT, 128], [128 * OT, KC], [1, gw]]), reads=[yTs_b], writes=[YT_b])
                    for cb in range(NCB):
                        wo, wo_b = wo_p.get()
                        E.dma("pool", wo[:], dram_view(wout, cb * 512, [[D, 128], [128 * D, KC], [1, 512]]), writes=[wo_b])
                        for sub in range(nsub):
                            pt, pt_b = mm_p.get()
                            for kc in range(KC):
                                E.op("pe", lambda e: e.matmul(pt[:], lhsT=YT[:, kc, sub * 128:(sub + 1) * 128], rhs=wo[:, kc, :],
                                                              start=(kc == 0), stop=(kc == KC - 1)), reads=[YT_b, wo_b], writes=[pt_b])
                            E.op("dve", lambda e: e.tensor_copy(out=yacc[:, sub, cb * 512:(cb + 1) * 512], in_=pt[:]),
                                 reads=[pt_b], writes=[yacc_b[sub]])
                            E.op("act", lambda e: e.activation(out=junk[:], in_=pt[:], func=AF.Square, accum_out=ssq[:, sub, cb:cb + 1]),
                                 reads=[pt_b], writes=[junk_b, ssq_b[sub]])
                    P4CUT = int(os.environ.get("P4CUT", "9"))
                    for sub in range(nsub):
                        if P4CUT < 2:
                            break
                        r0 = g0 + sub * 128
                        xo, xo_b = xo_p.get()
                        E.dma("sp", xo[:], xown[r0:r0 + 128, :], writes=[xo_b])
                        E.op("dve", lambda e: e.tensor_reduce(out=r4[:, sub, 0:1], in_=ssq[:, sub, :], axis=AX.X, op=ALU.add),
                             reads=[ssq_b[sub]], writes=[ssq_b[sub]])
                        E.op("act", lambda e: e.activation(out=r4[:, sub, 1:2], in_=r4[:, sub, 0:1], func=AF.Sqrt, scale=1.0 / D,
                                                           bias=eps_sb[:, 0:1]), reads=[ssq_b[sub], eps_b], writes=[ssq_b[sub]])
                        E.op("dve", lambda e: e.reciprocal(out=r4[:, sub, 1:2], in_=r4[:, sub, 1:2]), reads=[ssq_b[sub]], writes=[ssq_b[sub]])
                        if P4CUT < 3:
                            continue
                        E.op("dve", lambda e: e.scalar_tensor_tensor(out=yacc[:, sub, :], in0=yacc[:, sub, :], scalar=r4[:, sub, 1:2],
                                                                     in1=pnb[:], op0=ALU.mult, op1=ALU.mult),
                             reads=[yacc_b[sub], ssq_b[sub], pnb_b], writes=[yacc_b[sub]])
                        if P4CUT < 4:
                            continue
                        E.op("pool", lambda e: e.tensor_tensor(out=xo[:], in0=xo[:], in1=yacc[:, sub, :], op=ALU.add),
                             reads=[xo_b, yacc_b[sub]], writes=[xo_b])
                        if P4CUT < 5:
                            continue
                        E.dma("sp", y_o[r0:r0 + 128, :], xo[:], reads=[xo_b], is_output=True)

        if 4 in phases:
            phase4()
            E.barrier()

        E.finish()
        global LAST_E
        LAST_E = E
    return nc


def prep_inputs(cfg, inp):
    D, NC, OWN, SB, WL, TL, HA, HB, WA, WB = cfg.D, cfg.NC, cfg.OWN, cfg.SB, cfg.WL, cfg.TL, cfg.HA, cfg.HB, cfg.WA, cfg.WB
    f = np.float32
    w = np.asarray(inp["w_in"], f)[0]
    wF = np.ascontiguousarray(np.concatenate([w[:, WA:2 * WA], w[:, 4 * WA:4 * WA + 3 * WB], w[:, 0:WA]], axis=1))
    o = 4 * WA + 4 * WB
    wT = np.ascontiguousarray(np.concatenate([w[:, 2 * WA:3 * WA], w[:, o:o + 2 * HB], w[:, 3 * WA:4 * WA],
                                              w[:, 4 * WA + 3 * WB:4 * WA + 4 * WB]], axis=1))
    wout = np.ascontiguousarray(np.asarray(inp["w_out"], f)[0])
    pn = np.ascontiguousarray(np.asarray(inp["pre_norm"], f)[0].reshape(cfg.KC, 128).T)
    postn = np.ascontiguousarray(np.asarray(inp["post_norm"], f)[0][None])
    convw = np.ascontiguousarray(np.asarray(inp["conv_b"], f)[0].reshape(4, 3 * HB, 128).transpose(2, 1, 0)).reshape(128, -1)
    gconst = np.concatenate([np.asarray(inp["a_log_b"], f)[0], np.asarray(inp["dt_bias_b"], f)[0]])[None]
    relb = np.ascontiguousarray(np.asarray(inp["rel_bias"], f))
    lamv = np.concatenate([np.asarray(inp[k], f)[0] for k in ("lambda_q1", "lambda_k1", "lambda_q2", "lambda_k2")])[None]
    subln = np.asarray(inp["subln_a"], f)[0][None]
    normb = np.asarray(inp["norm_b"], f)[0][None]
    xp = np.asarray(inp["x_prompt"], f)[0]
    xs = np.asarray(inp["x_sample"], f)
    meta = np.asarray(inp["meta_tokens"], f)
    ck = np.asarray(inp["cache_k_a"], f)[0]
    cvv = np.asarray(inp["cache_v_a"], f)[0]
    ssm = np.asarray(inp["state_ssm_b"], f)[0]
    cs = np.asarray(inp["state_conv_b"], f)[0]
    cstc = make_cst()
    ohc = make_oh()
    dmc = make_dmask()
    maps = []
    for c in range(NC):
        pad = OWN * (NC - 1 - c)
        xa = np.zeros((TL, D), f)
        xa[pad + 112:pad + 128] = meta
        xa[pad + 128:WL] = xp[0:(c + 1) * OWN]
        xa[WL:] = xs[c * SB:(c + 1) * SB].reshape(-1, D)
        m = {
            "xT": np.ascontiguousarray(xa.T),
            "xown": np.ascontiguousarray(xa[WL - OWN:]),
            "wF": wF, "wT": wT, "wout": wout, "pn": pn, "postn": postn, "convw": convw,
            "convst": np.ascontiguousarray(cs[c * SB:(c + 1) * SB].reshape(SB, 3, 3 * HB, 128).transpose(3, 2, 0, 1)).reshape(128, -1),
            "gconst": gconst, "relb": relb, "lamv": lamv, "subln": subln, "normb": normb,
            "ckT": np.ascontiguousarray(ck[c * SB:(c + 1) * SB].transpose(0, 2, 3, 4, 1)).reshape(-1, cfg.CL),
            "cv": np.ascontiguousarray(cvv[c * SB:(c + 1) * SB]).reshape(SB * cfg.CL, WA),
            "ssm": np.ascontiguousarray(ssm[c * SB:(c + 1) * SB]).reshape(-1, 128),
            "cst": cstc,
            "valid": np.ascontiguousarray((np.arange(WL).reshape(-1, 128).T >= pad + 112).astype(f)),
            "oh": ohc, "dmask": dmc,
        }
        maps.append(m)
    return maps


def assemble(cfg, res):
    D, NC, OWN, SB, HA, HB, WA, WB, SEQ = cfg.D, cfg.NC, cfg.OWN, cfg.SB, cfg.HA, cfg.HB, cfg.WA, cfg.WB, cfg.SEQ
    f = np.float32
    DECB, DS = cfg.DECB, cfg.DECS
    y_p = np.zeros((1, SEQ, D), f)
    y_s = np.zeros((DECB, DS, D), f)
    k_p = np.zeros((1, 1, 16 + SEQ, HA, 2, 128), f)
    v_p = np.zeros((1, 1, 16 + SEQ, HA, 256), f)
    k_s = np.zeros((1, DECB, DS, HA, 2, 128), f)
    v_s = np.zeros((1, DECB, DS, HA, 256), f)
    c_s = np.zeros((1, DECB, 3, 3 * WB), f)
    s_s = np.zeros((1, DECB, HB, 128, 128), f)
    for c in range(NC):
        r = res[c]
        y = np.asarray(r["y_o"])
        y_p[0, c * OWN:(c + 1) * OWN] = y[0:OWN]
        y_s[c * SB:(c + 1) * SB] = y[OWN:].reshape(SB, DS, D)
        kt = np.asarray(r["kT_o"]).reshape(HA, 2, 128, -1).transpose(3, 0, 1, 2)
        vt = np.asarray(r["v_o"]).reshape(-1, HA, 256)
        if c == 0:
            k_p[0, 0, 0:16] = kt[112:128]
            v_p[0, 0, 0:16] = vt[112:128]
        k_p[0, 0, 16 + c * OWN:16 + (c + 1) * OWN] = kt[128:128 + OWN]
        v_p[0, 0, 16 + c * OWN:16 + (c + 1) * OWN] = vt[128:128 + OWN]
        k_s[0, c * SB:(c + 1) * SB] = kt[128 + OWN:].reshape(SB, DS, HA, 2, 128)
        v_s[0, c * SB:(c + 1) * SB] = vt[128 + OWN:].reshape(SB, DS, HA, 256)
        c_s[0, c * SB:(c + 1) * SB] = np.asarray(r["conv_so"]).reshape(3 * HB, 128, SB, 3).transpose(2, 3, 0, 1).reshape(SB, 3, 3 * WB)
        s_s[0, c * SB:(c + 1) * SB] = np.asarray(r["ssm_so"]).reshape(SB, HB, 128, 128)
    rl = res[NC - 1]
    s_p = np.asarray(rl["ssm_o"]).reshape(1, 1, HB, 128, 128).astype(f)
    c_p = np.asarray(rl["conv_o"]).reshape(3 * HB, 128, 3).transpose(2, 0, 1).reshape(1, 1, 3, 3 * WB).astype(f)
    return (y_p, y_s, k_p, v_p, s_p, c_p, k_s, v_s, s_s, c_s)


_NC_CACHE = {}


def kernel(**inputs):
    cfg = Cfg()
    if "nc" not in _NC_CACHE:
        _NC_CACHE["nc"] = build(cfg)
    nc = _NC_CACHE["nc"]
    maps = prep_inputs(cfg, inputs)
    res = run_bass_kernel_spmd(nc, maps, core_ids=list(range(cfg.NC)))
    return assemble(cfg, res.results)
```

```python
from contextlib import ExitStack
import math
import numpy as np
import concourse.bass as bass
import concourse.mybir as mybir
from concourse.bass_utils import run_bass_kernel_spmd

F32 = mybir.dt.float32
BF16 = mybir.dt.bfloat16
AF = mybir.ActivationFunctionType
ALU = mybir.AluOpType
AX = mybir.AxisListType
EPS = 1e-6
N_BUCKETS = 32
MAX_DISTANCE = 1024
CHUNK = 64
LAM_INIT = 0.8 - 0.6 * math.exp(0.0)
import os
GDN_FP32 = os.environ.get("GDN_FP32", "1") == "1"
LAST_E = None


class Cfg:
    def __init__(self, D=4096, NC=8, SEQ=16384, DECB=32, DECS=32, PAST=1024):
        self.D, self.NC, self.SEQ, self.DECB, self.DECS, self.PAST = D, NC, SEQ, DECB, DECS, PAST
        self.NMETA = 16
        self.WA = D // 2
        self.WB = D - self.WA
        self.HA = self.WA // 256
        self.HB = self.WB // 128
        self.INC = 4 * self.WA + 4 * self.WB + 2 * self.HB
        self.KC = D // 128
        self.OWN = SEQ // NC
        self.WL = 128 + SEQ
        self.SB = DECB // NC
        self.ST = self.SB * DECS
        assert self.ST == 128 and DECS == 32
        self.TL = self.WL + self.ST
        self.CL = self.NMETA + PAST
        self.OT = self.OWN + self.ST
        self.OK0 = self.WL - self.OWN - 128
        assert self.OK0 % 512 == 0 and self.WL % 512 == 128
        self.NF_W = 2 * self.HA + 3 * self.HB
        self.NF = self.NF_W + 2 * self.HA
        self.NT_W = self.WA + 2 * self.HB
        self.NT = self.NT_W + self.WA + self.WB


class Buf:
    __slots__ = ("name", "w", "r", "const", "psum")

    def __init__(self, name, const=False, psum=False):
        self.name = name
        self.w = None
        self.r = []
        self.const = const
        self.psum = psum


class Emitter:
    ENG = ("pe", "act", "dve", "pool", "sp")

    def __init__(self, nc, es, n_dma_sems=48):
        self.nc = nc
        self.eng = {"pe": nc.tensor, "act": nc.scalar, "dve": nc.vector, "pool": nc.gpsimd, "sp": nc.sync}
        self.psem = {e: es.enter_context(nc.semaphore("P_" + e)) for e in ("pe", "act", "dve", "pool")}
        self.pcnt = {e: 0 for e in self.psem}
        self.dsem = [es.enter_context(nc.semaphore("D%d" % i)) for i in range(n_dma_sems)]
        self.dval = [0] * n_dma_sems
        self.dnext = 0
        nq = n_dma_sems // 2
        self.qrange = {"sp": (0, nq), "pool": (nq, n_dma_sems), "act": (0, nq)}
        self.qnext = {"sp": 0, "pool": 0, "act": 0}
        self.waited = {}
        self.out_events = []
        self.marks = []

    def _need(self, waiter, ev):
        if ev is None:
            return None
        kind, src, val = ev
        if kind == "e" and src == waiter and waiter == "pe":
            return None
        key = (waiter, kind, src)
        if self.waited.get(key, 0) >= val:
            return None
        return key, val

    def _do_waits(self, waiter, evs):
        best = {}
        for ev in evs:
            n = self._need(waiter, ev)
            if n is not None:
                key, val = n
                if best.get(key, 0) < val:
                    best[key] = val
        for key, val in best.items():
            _, kind, src = key
            sem = self.psem[src] if kind == "e" else self.dsem[src]
            self.eng[waiter].wait_ge(sem, val)
            self.waited[key] = val

    def _deps(self, reads, writes, waiter=None):
        evs = []
        for b in reads:
            evs.append(b.w)
            if b.psum:
                evs.extend(ev for ev in b.r if not (ev[0] == "e" and ev[1] == waiter))
        for b in writes:
            evs.append(b.w)
            evs.extend(b.r)
        return evs

    def op(self, e, fn, reads=(), writes=()):
        self._do_waits(e, self._deps(reads, writes, e))
        ins = fn(self.eng[e])
        self.pcnt[e] += 1
        ins.then_inc(self.psem[e], 1)
        ev = ("e", e, self.pcnt[e])
        for b in reads:
            if not b.const:
                b.r.append(ev)
        for b in writes:
            b.w = ev
            b.r = []
        return ev

    def dma(self, q, out, in_, reads=(), writes=(), is_output=False, **kw):
        lo, hi = self.qrange[q]
        qk = "sp" if q == "act" else q
        i = lo + self.qnext[qk]
        self.qnext[qk] = (self.qnext[qk] + 1) % (hi - lo)
        evs = self._deps(reads, writes)
        if self.dval[i] > 0:
            evs.append(("d", i, self.dval[i]))
        self._do_waits(q, evs)
        ins = self.eng[q].dma_start(out=out, in_=in_, **kw)
        self.dval[i] += 16
        ins.then_inc(self.dsem[i], 16)
        ev = ("d", i, self.dval[i])
        for b in reads:
            if not b.const:
                b.r.append(ev)
        for b in writes:
            b.w = ev
            b.r = []
        if is_output:
            self.out_events.append(ev)
        return ev

    def barrier(self):
        self.marks.append(dict(self.pcnt))
        evs = []
        for i, v in enumerate(self.dval):
            if v > 0:
                evs.append(("d", i, v))
        for e in self.psem:
            if self.pcnt[e] > 0:
                evs.append(("e", e, self.pcnt[e]))
        for w in ("pe", "act", "dve", "pool", "sp"):
            self._do_waits(w, [ev for ev in evs if not (ev[0] == "e" and ev[1] == w)])

    def finish(self):
        evs = list(self.out_events)
        for i, v in enumerate(self.dval):
            if v > 0:
                evs.append(("d", i, v))
        for e in self.psem:
            if self.pcnt[e] > 0:
                evs.append(("e", e, self.pcnt[e]))
        self._do_waits("sp", evs)


class Pool:
    def __init__(self, tiles, name, nsub=0, psum=False):
        self.tiles = tiles
        if nsub:
            self.bufs = [[Buf("%s%d_%d" % (name, i, j)) for j in range(nsub)] for i in range(len(tiles))]
        else:
            self.bufs = [Buf("%s%d" % (name, i), psum=psum) for i in range(len(tiles))]
        self.i = 0

    def get(self):
        t, b = self.tiles[self.i], self.bufs[self.i]
        self.i = (self.i + 1) % len(self.tiles)
        return t, b


def dram_view(t, offset, pattern):
    return bass.AP(t, offset, pattern)


def rel_bucket_np(rel):
    nb = N_BUCKETS // 2
    max_exact = nb // 2
    n = np.abs(rel)
    nf = np.maximum(n, 1).astype(np.float32)
    large = max_exact + (np.log(nf / np.float32(max_exact)) / np.float32(math.log(MAX_DISTANCE / max_exact))
                         * np.float32(nb - max_exact)).astype(np.int32)
    large = np.minimum(large, nb - 1)
    return np.where(rel > 0, nb, 0) + np.where(n < max_exact, n, large)


C_ID, C_AID, C_TRIU, C_LINC, C_LSTR, C_ONE = 0, 128, 256, 384, 512, 640
CST_W = 768
BV_R0 = 511
BV_LEN = 1664
NEAR_D = [-640, -512, -384, -256, -128, 0, 128, 256, 384]


def make_oh():
    i = np.arange(BV_LEN)
    bk = rel_bucket_np((BV_R0 - i).astype(np.int32))
    o = np.zeros((N_BUCKETS, BV_LEN), np.float32)
    o[bk, i] = 1.0
    return o


def make_dmask():
    m = np.zeros((128, 4, 512), np.float32)
    p = np.arange(128)[:, None]
    j = np.arange(512)[None, :]
    for a in range(4):
        m[:, a, :] = ((128 * a + p) // 64) <= (j // 64)
    return m.reshape(128, 2048)


def make_cst():
    c = np.zeros((128, CST_W), np.float32)
    i = np.arange(128)
    c[:, C_ID:C_ID + 128] = np.eye(128)
    c[:, C_AID:C_AID + 128] = np.eye(128)[::-1]
    c[:, C_TRIU:C_TRIU + 128] = (i[:, None] <= i[None, :])
    c[:, C_LINC:C_LINC + 128] = (i[:, None] >= i[None, :])
    c[:, C_LSTR:C_LSTR + 128] = (i[:, None] > i[None, :])
    c[:, C_ONE:C_ONE + 128] = 1.0
    return c


def build(cfg, phases=(0, 1, 2, 3, 4)):
    nc = bass.Bass("TRN2", target_bir_lowering=False)
    D, KC, TL, WL, OWN, OT, OK0 = cfg.D, cfg.KC, cfg.TL, cfg.WL, cfg.OWN, cfg.OT, cfg.OK0
    HA, HB, WA, WB, SB, ST, CL = cfg.HA, cfg.HB, cfg.WA, cfg.WB, cfg.SB, cfg.ST, cfg.CL
    KO = TL - OK0

    def din(name, shape, dt=F32):
        return nc.dram_tensor(name, list(shape), dt, kind="ExternalInput")

    def dout(name, shape, dt=F32):
        return nc.dram_tensor(name, list(shape), dt, kind="ExternalOutput")

    def dscr(name, shape, dt):
        return nc.dram_tensor(name, list(shape), dt)

    xT = din("xT", [D, TL])
    xown = din("xown", [OT, D])
    wF = din("wF", [D, cfg.NF * 128])
    wT = din("wT", [D, cfg.NT])
    wout = din("wout", [D, D])
    pn = din("pn", [128, KC])
    postn = din("postn", [1, D])
    convw = din("convw", [128, 3 * HB * 4])
    convst = din("convst", [128, 3 * HB * SB * 3])
    gconst = din("gconst", [1, 2 * HB])
    relb = din("relb", [N_BUCKETS, HA])
    lamv = din("lamv", [1, 4 * 128])
    subln = din("subln", [1, 256])
    normb = din("normb", [1, 128])
    ckT = din("ckT", [SB * 2 * HA * 128, CL])
    cv = din("cv", [SB * CL, WA])
    ssm = din("ssm", [SB * HB * 128, 128])
    cst = din("cst", [128, CST_W])
    NKT = WL // 128
    valid = din("valid", [128, NKT])
    oh = din("oh", [N_BUCKETS, BV_LEN])
    dmask = din("dmask", [128, 4 * 512])
    bv_s = dscr("bv_s", [HA, BV_LEN], F32)
    y_o = dout("y_o", [OT, D])
    kT_o = dout("kT_o", [2 * HA * 128, KO])
    v_o = dout("v_o", [KO, WA])
    ssm_o = dout("ssm_o", [HB * 128, 128])
    ssm_so = dout("ssm_so", [SB * HB * 128, 128])
    conv_o = dout("conv_o", [3 * HB * 128, 3])
    conv_so = dout("conv_so", [3 * HB * 128, SB * 3])
    xnT_s = dscr("xnT_s", [128, KC * TL], BF16)
    kT_s = dscr("kT_s", [2 * HA * 128, TL], BF16)
    v_s = dscr("v_s", [TL, WA], BF16)
    gT_s = dscr("gT_s", [3 * HB * 128, TL], BF16)
    gb_s = dscr("gb_s", [TL, 2 * HB], F32)
    qT_s = dscr("qT_s", [2 * HA * 128, OT], BF16)
    zs_s = dscr("zs_s", [OT, WA + WB], BF16)
    yT_s = dscr("yT_s", [D, OT], BF16)

    es = ExitStack()
    with es:
        E = Emitter(nc, es)

        def sb(stack, name, shape, dt):
            return stack.enter_context(nc.sbuf_tensor(name, list(shape), dt))

        def ps(stack, name, shape, dt=F32):
            return stack.enter_context(nc.psum_tensor(name, list(shape), dt))

        cst_sb = sb(es, "cst_sb", [128, CST_W], F32)
        cst_b = Buf("cst", const=True)
        E.dma("sp", cst_sb[:], cst.ap(), writes=[cst_b])
        cbf = sb(es, "cbf", [128, CST_W], BF16)
        cbf_b = Buf("cbf", const=True)
        E.op("dve", lambda e: e.tensor_copy(out=cbf[:], in_=cst_sb[:]), reads=[cst_b], writes=[cbf_b])
        onesD = sb(es, "onesD", [128, 128], BF16)
        onesD_b = Buf("onesD", const=True)
        E.op("dve", lambda e: e.tensor_scalar(out=onesD[:], in0=cst_sb[:, C_ONE:C_ONE + 128], scalar1=1.0 / D,
                                              scalar2=None, op0=ALU.mult), reads=[cst_b], writes=[onesD_b])
        eps_sb = sb(es, "eps_sb", [128, 1], F32)
        eps_b = Buf("eps", const=True)
        E.op("dve", lambda e: e.memset(eps_sb[:], EPS), writes=[eps_b])
        pn_sb = sb(es, "pn_sb", [128, KC], F32)
        pn_b = Buf("pn", const=True)
        E.dma("sp", pn_sb[:], pn.ap(), writes=[pn_b])

        xns_b = [Buf("xns%d" % i) for i in range(TL // 256)]

        def xns_deps(t0, tw):
            return [xns_b[i] for i in range(t0 // 256, (t0 + tw - 1) // 256 + 1)]

        kTs_b = Buf("kT_s")
        vs_b = Buf("v_s")
        gTs_b = Buf("gT_s")
        gbs_b = Buf("gb_s")
        qTs_b = Buf("qT_s")
        zss_b = Buf("zs_s")
        yTs_b = Buf("yT_s")

        convw_sb = sb(es, "convw_sb", [128, 3 * HB, 4], F32)
        convst_sb = sb(es, "convst_sb", [128, 3 * HB, SB, 3], F32)
        gc_sb = sb(es, "gc_sb", [128, 2 * HB], F32)
        nA_sb = sb(es, "nA_sb", [128, HB], F32)
        ones_bf = cbf[:, C_ONE:C_ONE + 128]
        ident_bf = cbf[:, C_ID:C_ID + 128]
        smallc_b = Buf("smallc", const=True)
        E.dma("sp", convw_sb[:], convw.ap(), writes=[smallc_b])
        E.dma("sp", convst_sb[:], convst.ap(), writes=[smallc_b])
        E.dma("sp", gc_sb[:], dram_view(gconst, 0, [[0, 128], [1, 2 * HB]]), writes=[smallc_b])
        nA_b = Buf("nA", const=True)
        E.op("act", lambda e: e.activation(out=nA_sb[:], in_=gc_sb[:, 0:HB], func=AF.Exp),
             reads=[smallc_b], writes=[nA_b])
        E.op("dve", lambda e: e.tensor_scalar(out=nA_sb[:], in0=nA_sb[:], scalar1=-1.0, scalar2=None, op0=ALU.mult),
             reads=[nA_b], writes=[nA_b])
        smallc_b.w = None if False else smallc_b.w

        def phase0():
            with ExitStack() as st:
                TB = 256
                xf_p = Pool([sb(st, "xf%d" % i, [128, KC, TB], F32) for i in range(2)], "xf")
                sq_p = Pool([sb(st, "sq%d" % i, [128, KC, TB], BF16) for i in range(1)], "sq")
                xn_p = Pool([sb(st, "xn%d" % i, [128, KC, TB], BF16) for i in range(2)], "xn", nsub=KC)
                rs_p = Pool([sb(st, "rs%d" % i, [128, TB], F32) for i in range(2)], "rs")
                ss_p = Pool([ps(st, "ss%d" % i, [128, 512]) for i in range(2)], "ss", psum=True)
                for blk in range(TL // TB):
                    t0 = blk * TB
                    xf, xf_b = xf_p.get()
                    E.dma("sp", xf[:], dram_view(xT, t0, [[TL, 128], [128 * TL, KC], [1, TB]]), writes=[xf_b])
                    sq, sq_b = sq_p.get()
                    E.op("act", lambda e: e.activation(out=sq[:], in_=xf[:], func=AF.Square),
                         reads=[xf_b], writes=[sq_b])
                    ss, ss_b = ss_p.get()
                    for kc in range(KC):
                        E.op("pe", lambda e: e.matmul(ss[:, 0:TB], lhsT=onesD[:], rhs=sq[:, kc, :],
                                                      start=(kc == 0), stop=(kc == KC - 1)),
                             reads=[onesD_b, sq_b], writes=[ss_b])
                    rs, rs_b = rs_p.get()
                    E.op("act", lambda e: e.activation(out=rs[:], in_=ss[:, 0:TB], func=AF.Sqrt, bias=eps_sb[:, 0:1]),
                         reads=[ss_b, eps_b], writes=[rs_b])
                    E.op("dve", lambda e: e.reciprocal(out=rs[:], in_=rs[:]), reads=[rs_b], writes=[rs_b])
                    xn, xn_bs = xn_p.get()
                    for kc in range(KC):
                        E.op("dve", lambda e: e.scalar_tensor_tensor(out=xn[:, kc, :], in0=xf[:, kc, :],
                                                                   scalar=pn_sb[:, kc:kc + 1], in1=rs[:],
                                                                   op0=ALU.mult, op1=ALU.mult),
                             reads=[xf_b, rs_b, pn_b], writes=[xn_bs[kc]])
                    E.dma("pool", dram_view(xnT_s, t0, [[KC * TL, 128], [TL, KC], [1, TB]]), xn[:],
                          reads=xn_bs, writes=[xns_b[blk]])

        if 0 in phases:
            phase0()
            E.barrier()

        def phase1():
            with ExitStack() as st:
                WGW = 1088
                Wg = sb(st, "Wg", [128, KC, WGW], BF16)
                Wg_b = Buf("Wg")
                xnb_p = Pool([sb(st, "xnb%d" % i, [128, KC, 512], BF16) for i in range(2)], "xnb")
                mm_p = Pool([ps(st, "mm%d" % i, [128, 512]) for i in range(5)], "mm", psum=True)
                ss_p = Pool([ps(st, "ssq%d" % i, [128, 512]) for i in range(2)], "ssq", psum=True)
                f32_p = Pool([sb(st, "ev%d" % i, [128, 512], F32) for i in range(4)], "ev")
                ext_p = Pool([sb(st, "ext%d" % i, [128, 520], F32) for i in range(4)], "ext")
                ext2_p = Pool([sb(st, "ext2_%d" % i, [128, SB, 36], F32) for i in range(3)], "ext2")
                yc_p = Pool([sb(st, "yc%d" % i, [128, 512], F32) for i in range(3)], "yc")
                ys_p = Pool([sb(st, "ys%d" % i, [128, 512], F32) for i in range(5)], "ys")
                sqb_p = Pool([sb(st, "sqb%d" % i, [128, 512], BF16) for i in range(4)], "sqb")
                rs_p = Pool([sb(st, "rsq%d" % i, [128, 512], F32) for i in range(2)], "rsq")
                obf_p = Pool([sb(st, "obf%d" % i, [128, 512], BF16) for i in range(4)], "obf")
                gb_p = Pool([sb(st, "gbt%d" % i, [128, 2 * HB], F32) for i in range(2)], "gbt")
                tmp_p = Pool([sb(st, "gtmp%d" % i, [128, HB], F32) for i in range(2)], "gtmp")
                car = sb(st, "car", [128, 3 * HB, 4], F32)
                car_b = [Buf("car%d" % i) for i in range(3 * HB)]
                E.op("pool", lambda e: e.memset(car[:], 0.0), writes=car_b)

                def win_blocks():
                    return [(t0, min(512, TL - t0)) for t0 in range(0, TL, 512)]

                def own_blocks():
                    return [(t0, min(512, TL - t0)) for t0 in range(WL - OWN, TL, 512)]

                def load_W(src, ncols, c0, gw):
                    E.dma("pool", Wg[:, :, 0:gw],
                          dram_view(src, c0, [[ncols, 128], [128 * ncols, KC], [1, gw]]), writes=[Wg_b])

                def load_xn(t0, tw):
                    xnb, xnb_b = xnb_p.get()
                    E.dma("sp", xnb[:, :, 0:tw],
                          dram_view(xnT_s, t0, [[KC * TL, 128], [TL, KC], [1, tw]]),
                          reads=xns_deps(t0, tw), writes=[xnb_b])
                    return xnb, xnb_b

                def mm_F(xnb, xnb_b, tw, j):
                    pt, pt_b = mm_p.get()
                    for kc in range(KC):
                        E.op("pe", lambda e: e.matmul(pt[:, 0:tw], lhsT=Wg[:, kc, j * 128:(j + 1) * 128],
                                                      rhs=xnb[:, kc, 0:tw], start=(kc == 0), stop=(kc == KC - 1)),
                             reads=[Wg_b, xnb_b], writes=[pt_b])
                    return pt, pt_b

                def mm_T(xnb, xnb_b, sub, c0, gw):
                    pt, pt_b = mm_p.get()
                    for kc in range(KC):
                        E.op("pe", lambda e: e.matmul(pt[:, 0:gw], lhsT=xnb[:, kc, sub * 128:(sub + 1) * 128],
                                                      rhs=Wg[:, kc, c0:c0 + gw], start=(kc == 0), stop=(kc == KC - 1)),
                             reads=[Wg_b, xnb_b], writes=[pt_b])
                    return pt, pt_b

                def ev_ka(f, t0, tw, pt, pt_b):
                    kf, kf_b = f32_p.get()
                    E.op("act", lambda e: e.activation(out=kf[:, 0:tw], in_=pt[:, 0:tw], func=AF.Copy),
                         reads=[pt_b], writes=[kf_b])
                    E.dma("pool", kT_s[f * 128:(f + 1) * 128, t0:t0 + tw], kf[:, 0:tw], reads=[kf_b], writes=[kTs_b])
                    if t0 >= OK0:
                        E.dma("sp", kT_o[f * 128:(f + 1) * 128, t0 - OK0:t0 - OK0 + tw], kf[:, 0:tw],
                              reads=[kf_b], is_output=True)

                def ev_qa(f, t0, tw, pt, pt_b):
                    ob, ob_b = obf_p.get()
                    E.op("act", lambda e: e.activation(out=ob[:, 0:tw], in_=pt[:, 0:tw], func=AF.Copy),
                         reads=[pt_b], writes=[ob_b])
                    o0 = t0 - (WL - OWN)
                    E.dma("sp", qT_s[f * 128:(f + 1) * 128, o0:o0 + tw], ob[:, 0:tw], reads=[ob_b], writes=[qTs_b])

                def conv_tail(t, kind, hb, ext_v, n, yc_v, ys_v, wr, t0, s3=None):
                    ext_b, yc_b, ys_b = wr
                    E.op("dve", lambda e: e.tensor_scalar(out=yc_v, in0=ext_v(0), scalar1=convw_sb[:, t, 0:1],
                                                          scalar2=None, op0=ALU.mult),
                         reads=[ext_b, smallc_b], writes=[yc_b])
                    for i in range(1, 4):
                        E.op("dve", lambda e: e.scalar_tensor_tensor(out=yc_v, in0=ext_v(i),
                                                                     scalar=convw_sb[:, t, i:i + 1], in1=yc_v,
                                                                     op0=ALU.mult, op1=ALU.add),
                             reads=[ext_b, smallc_b, yc_b], writes=[yc_b])
                    E.op("act", lambda e: e.activation(out=ys_v, in_=yc_v, func=AF.Silu), reads=[yc_b], writes=[ys_b])

                def pre_square(kind, ys2, ys_b, n):
                    if kind == 2:
                        return None
                    sq, sq_b = sqb_p.get()
                    E.op("act", lambda e: e.activation(out=sq[:, 0:n], in_=ys2, func=AF.Square),
                         reads=[ys_b], writes=[sq_b])
                    return sq, sq_b

                def norm_store(t, kind, ys2, ys_b, n, t0, sqp=None):
                    ob, ob_b = obf_p.get()
                    if kind == 2:
                        E.op("pool", lambda e: e.tensor_copy(out=ob[:, 0:n], in_=ys2), reads=[ys_b], writes=[ob_b])
                    else:
                        sq, sq_b = sqp
                        ss, ss_b = ss_p.get()
                        E.op("pe", lambda e: e.matmul(ss[:, 0:n], lhsT=ones_bf, rhs=sq[:, 0:n], start=True, stop=True),
                             reads=[cbf_b, sq_b], writes=[ss_b])
                        rs, rs_b = rs_p.get()
                        E.op("act", lambda e: e.activation(out=rs[:, 0:n], in_=ss[:, 0:n], func=AF.Sqrt,
                                                           bias=eps_sb[:, 0:1]), reads=[ss_b, eps_b], writes=[rs_b])
                        E.op("dve", lambda e: e.reciprocal(out=rs[:, 0:n], in_=rs[:, 0:n]), reads=[rs_b], writes=[rs_b])
                        sc = (128 ** -0.5) if kind == 0 else 1.0
                        E.op("dve", lambda e: e.scalar_tensor_tensor(out=ob[:, 0:n], in0=ys2, scalar=sc, in1=rs[:, 0:n],
                                                                     op0=ALU.mult, op1=ALU.mult),
                             reads=[ys_b, rs_b], writes=[ob_b])
                    E.dma("pool", gT_s[t * 128:(t + 1) * 128, t0:t0 + n], ob[:, 0:n], reads=[ob_b], writes=[gTs_b])

                def ev_g(t, t0, tw, pt, pt_b):
                    kind = t // HB
                    nw = tw if t0 + tw <= WL else WL - t0
                    ext, ext_b = ext_p.get()
                    E.op("pool", lambda e: e.tensor_copy(out=ext[:, 0:3], in_=car[:, t, 0:3]),
                         reads=[car_b[t]], writes=[ext_b])
                    E.op("act", lambda e: e.activation(out=ext[:, 3:3 + nw], in_=pt[:, 0:nw], func=AF.Copy),
                         reads=[pt_b], writes=[ext_b])
                    E.op("pool", lambda e: e.tensor_copy(out=car[:, t, 0:3], in_=ext[:, nw:nw + 3]),
                         reads=[ext_b], writes=[car_b[t]])
                    if t0 + nw == WL:
                        E.dma("sp", conv_o[t * 128:(t + 1) * 128, :], ext[:, nw:nw + 3], reads=[ext_b], is_output=True)
                    has_s = nw < tw
                    if has_s:
                        assert tw - nw == ST
                        e2, e2_b = ext2_p.get()
                        E.op("pool", lambda e: e.tensor_copy(out=e2[:, :, 0:3], in_=convst_sb[:, t, :, :]),
                             reads=[smallc_b], writes=[e2_b])
                        E.op("act", lambda e: e.activation(out=e2[:, :, 3:35],
                                                           in_=pt[:, nw:tw].rearrange("p (b s) -> p b s", s=32),
                                                           func=AF.Copy), reads=[pt_b], writes=[e2_b])
                        E.dma("sp", conv_so[t * 128:(t + 1) * 128, :].rearrange("p (b s) -> p b s", s=3),
                              e2[:, :, 32:35], reads=[e2_b], is_output=True)
                    hold = {}

                    def stage_b():
                        yc, yc_b = yc_p.get()
                        ys, ys_b = ys_p.get()
                        conv_tail(t, kind, None, lambda i: ext[:, i:i + nw], nw, yc[:, 0:nw], ys[:, 0:nw],
                                  (ext_b, yc_b, ys_b), t0)
                        hold["ys"] = (ys, ys_b)
                        hold["sq"] = pre_square(kind, ys[:, 0:nw], ys_b, nw)
                        if has_s:
                            yc2, yc2_b = yc_p.get()
                            ys2, ys2_b = ys_p.get()
                            ycv = yc2[:, 0:ST].rearrange("p (b s) -> p b s", s=32)
                            ysv = ys2[:, 0:ST].rearrange("p (b s) -> p b s", s=32)
                            conv_tail(t, kind, None, lambda i: e2[:, :, i:i + 32], ST, ycv, ysv, (e2_b, yc2_b, ys2_b), t0)
                            hold["ys2"] = (ys2, ys2_b)
                            hold["sq2"] = pre_square(kind, ys2[:, 0:ST], ys2_b, ST)

                    def stage_c():
                        ys, ys_b = hold["ys"]
                        norm_store(t, kind, ys[:, 0:nw], ys_b, nw, t0, hold["sq"])
                        if has_s:
                            ys2, ys2_b = hold["ys2"]
                            norm_store(t, kind, ys2[:, 0:ST], ys2_b, ST, WL, hold["sq2"])

                    return [stage_b, stage_c]

                def ev_va(c0, gw, t0, sub, pt, pt_b):
                    vf, vf_b = f32_p.get()
                    E.op("dve", lambda e: e.tensor_copy(out=vf[:, 0:gw], in_=pt[:, 0:gw]), reads=[pt_b], writes=[vf_b])
                    r0 = t0 + sub * 128
                    E.dma("pool", v_s[r0:r0 + 128, c0:c0 + gw], vf[:, 0:gw], reads=[vf_b], writes=[vs_b])
                    if r0 >= OK0:
                        E.dma("sp", v_o[r0 - OK0:r0 - OK0 + 128, c0:c0 + gw], vf[:, 0:gw], reads=[vf_b], is_output=True)

                def ev_ba(t0, sub, pt, pt_b):
                    gbt, gbt_b = gb_p.get()
                    tmp, tmp_b = tmp_p.get()
                    E.op("act", lambda e: e.activation(out=gbt[:, 0:HB], in_=pt[:, 0:HB], func=AF.Sigmoid),
                         reads=[pt_b], writes=[gbt_b])
                    E.op("dve", lambda e: e.tensor_tensor(out=tmp[:], in0=pt[:, HB:2 * HB], in1=gc_sb[:, HB:2 * HB],
                                                          op=ALU.add), reads=[pt_b, smallc_b], writes=[tmp_b])
                    E.op("act", lambda e: e.activation(out=tmp[:], in_=tmp[:], func=AF.Exp), reads=[tmp_b], writes=[tmp_b])
                    E.op("act", lambda e: e.activation(out=tmp[:], in_=tmp[:], func=AF.Ln, bias=1.0),
                         reads=[tmp_b], writes=[tmp_b])
                    E.op("dve", lambda e: e.tensor_tensor(out=gbt[:, HB:2 * HB], in0=tmp[:], in1=nA_sb[:], op=ALU.mult),
                         reads=[tmp_b, nA_b], writes=[gbt_b])
                    r0 = t0 + sub * 128
                    E.dma("pool", gb_s[r0:r0 + 128, :], gbt[:], reads=[gbt_b], writes=[gbs_b])

                def ev_z(c0, gw, zc0, t0, sub, pt, pt_b):
                    ob, ob_b = obf_p.get()
                    E.op("act", lambda e: e.activation(out=ob[:, 0:gw], in_=pt[:, 0:gw], func=AF.Silu),
                         reads=[pt_b], writes=[ob_b])
                    r0 = t0 + sub * 128 - (WL - OWN)
                    E.dma("sp", zs_s[r0:r0 + 128, zc0:zc0 + gw], ob[:, 0:gw], reads=[ob_b], writes=[zss_b])

                GF = 8
                f_tiles = [("ka", f) for f in range(2 * HA)] + [("g", t) for t in range(3 * HB)]
                for g0 in range(0, len(f_tiles), GF):
                    grp = f_tiles[g0:g0 + GF]
                    load_W(wF, cfg.NF * 128, g0 * 128, len(grp) * 128)
                    pend = []
                    for (t0, tw) in win_blocks():
                        xnb, xnb_b = load_xn(t0, tw)
                        for j, (kind, idx) in enumerate(grp):
                            pt, pt_b = mm_F(xnb, xnb_b, tw, j)
                            if kind == "ka":
                                ev_ka(idx, t0, tw, pt, pt_b)
                            else:
                                for item in pend:
                                    item.pop(0)()
                                pend = [it for it in pend if it]
                                pend.append(ev_g(idx, t0, tw, pt, pt_b))
                    while pend:
                        for item in pend:
                            item.pop(0)()
                        pend = [it for it in pend if it]
                qa_tiles = list(range(2 * HA))
                for g0 in range(0, len(qa_tiles), GF):
                    grp = qa_tiles[g0:g0 + GF]
                    load_W(wF, cfg.NF * 128, (cfg.NF_W + g0) * 128, len(grp) * 128)
                    for (t0, tw) in own_blocks():
                        xnb, xnb_b = load_xn(t0, tw)
                        for j, f in enumerate(grp):
                            pt, pt_b = mm_F(xnb, xnb_b, tw, j)
                            ev_qa(f, t0, tw, pt, pt_b)
                segs = [("va", c, min(512, WA - c)) for c in range(0, WA, 512)] + [("ba", WA, 2 * HB)]
                groups = []
                cur, curw = [], 0
                for sg in segs:
                    if curw + sg[2] > WGW:
                        groups.append(cur)
                        cur, curw = [], 0
                    cur.append(sg)
                    curw += sg[2]
                groups.append(cur)
                for grp in groups:
                    gc0 = grp[0][1]
                    gw_tot = sum(sg[2] for sg in grp)
                    load_W(wT, cfg.NT, gc0, gw_tot)
                    for (t0, tw) in win_blocks():
                        xnb, xnb_b = load_xn(t0, tw)
                        for sub in range(tw // 128):
                            for (kind, c, w) in grp:
                                pt, pt_b = mm_T(xnb, xnb_b, sub, c - gc0, w)
                                if kind == "va":
                                    ev_va(c, w, t0, sub, pt, pt_b)
                                else:
                                    ev_ba(t0, sub, pt, pt_b)
                zsegs = [(c, min(512, WA + WB - c)) for c in range(0, WA + WB, 512)]
                for g0 in range(0, len(zsegs), 2):
                    grp = zsegs[g0:g0 + 2]
                    gc0 = grp[0][0]
                    gw_tot = sum(w for _, w in grp)
                    load_W(wT, cfg.NT, cfg.NT_W + gc0, gw_tot)
                    for (t0, tw) in own_blocks():
                        xnb, xnb_b = load_xn(t0, tw)
                        for sub in range(tw // 128):
                            for (c, w) in grp:
                                pt, pt_b = mm_T(xnb, xnb_b, sub, c - gc0, w)
                                ev_z(c - gc0, w, c, t0, sub, pt, pt_b)

        if 1 in phases:
            phase1()
            E.barrier()


        def phase2():
            with ExitStack() as st:
                QB0 = WL - OWN
                SCALE = 128 ** -0.5
                KTt = sb(st, "KTt", [128, 2, WL], BF16)
                KT_b = Buf("KTt")
                Vt = sb(st, "Vt", [128, NKT, 256], BF16)
                V_b = Buf("Vt")
                QTt = sb(st, "QTt", [128, 2, OT], BF16)
                QT_b = Buf("QTt")
                val_f = sb(st, "val_f", [128, NKT], F32)
                val_h = sb(st, "val_h", [128, NKT], BF16)
                val_b = Buf("val", const=True)
                E.dma("sp", val_f[:], valid.ap(), writes=[val_b])
                E.op("dve", lambda e: e.tensor_copy(out=val_h[:], in_=val_f[:]), reads=[val_b], writes=[val_b])
                dm_f = sb(st, "dm_f", [128, 4, 512], F32)
                dm_b = Buf("dm", const=True)
                E.dma("sp", dm_f[:], dmask.ap().rearrange("p (a j) -> p a j", j=512), writes=[dm_b])
                lv = sb(st, "lv", [128, 4, 128], F32)
                lam_t = sb(st, "lam_t", [128, 4], F32)
                sl_sb = sb(st, "sl_sb", [128, 256], F32)
                rb_sb = sb(st, "rb_sb", [128, HA], F32)
                misc_b = Buf("misc2", const=True)
                E.dma("sp", lv[:], dram_view(lamv, 0, [[0, 128], [128, 4], [1, 128]]), writes=[misc_b])
                E.dma("sp", sl_sb[:], dram_view(subln, 0, [[0, 128], [1, 256]]), writes=[misc_b])
                E.dma("sp", rb_sb[:], dram_view(relb, 15 * HA, [[0, 128], [1, HA]]), writes=[misc_b])
                E.op("dve", lambda e: e.tensor_tensor(out=lv[:, 0, :], in0=lv[:, 0, :], in1=lv[:, 1, :], op=ALU.mult),
                     reads=[misc_b], writes=[misc_b])
                E.op("dve", lambda e: e.tensor_tensor(out=lv[:, 2, :], in0=lv[:, 2, :], in1=lv[:, 3, :], op=ALU.mult),
                     reads=[misc_b], writes=[misc_b])
                E.op("dve", lambda e: e.tensor_reduce(out=lam_t[:, 0:1], in_=lv[:, 0, :], axis=AX.X, op=ALU.add),
                     reads=[misc_b], writes=[misc_b])
                E.op("dve", lambda e: e.tensor_reduce(out=lam_t[:, 1:2], in_=lv[:, 2, :], axis=AX.X, op=ALU.add),
                     reads=[misc_b], writes=[misc_b])
                E.op("act", lambda e: e.activation(out=lam_t[:, 0:2], in_=lam_t[:, 0:2], func=AF.Exp), reads=[misc_b], writes=[misc_b])
                E.op("dve", lambda e: e.tensor_tensor(out=lam_t[:, 2:3], in0=lam_t[:, 0:1], in1=lam_t[:, 1:2], op=ALU.subtract),
                     reads=[misc_b], writes=[misc_b])
                E.op("dve", lambda e: e.tensor_scalar(out=lam_t[:, 3:4], in0=lam_t[:, 2:3], scalar1=LAM_INIT, scalar2=-1.0,
                                                      op0=ALU.add, op1=ALU.mult), reads=[misc_b], writes=[misc_b])
                E.op("dve", lambda e: e.tensor_scalar(out=sl_sb[:], in0=sl_sb[:], scalar1=1.0 - LAM_INIT, scalar2=None, op0=ALU.mult),
                     reads=[misc_b], writes=[misc_b])
                sc_p = Pool([ps(st, "sct%d" % i, [128, 512]) for i in range(4)], "sct", psum=True)
                bvs_b = Buf("bv_s")
                with ExitStack() as st2:
                    relb_sb = sb(st2, "relb_sb", [N_BUCKETS, HA], F32)
                    oh_sb = sb(st2, "oh_sb", [N_BUCKETS, BV_LEN], F32)
                    bvt = sb(st2, "bvt", [HA, BV_LEN], F32)
                    bv_b = Buf("bv")
                    E.dma("sp", relb_sb[:], relb.ap(), writes=[bv_b])
                    E.dma("sp", oh_sb[:], oh.ap(), writes=[bv_b])
                    for c0 in range(0, BV_LEN, 512):
                        w = min(512, BV_LEN - c0)
                        pt, pt_b = sc_p.get()
                        E.op("pe", lambda e: e.matmul(pt[0:HA, 0:w], lhsT=relb_sb[:, :], rhs=oh_sb[:, c0:c0 + w], start=True, stop=True),
                             reads=[bv_b], writes=[pt_b])
                        E.op("act", lambda e: e.activation(out=bvt[:, c0:c0 + w], in_=pt[0:HA, 0:w], func=AF.Copy), reads=[pt_b], writes=[bv_b])
                    E.dma("sp", bv_s.ap(), bvt[:], reads=[bv_b], writes=[bvs_b])
                E.barrier()

                hk_p = Pool([sb(st, "hk%d" % i, [128, 512], F32) for i in range(2)], "hk")
                eb = sb(st, "eb", [128, 9, 512], BF16)
                eb_b = Buf("eb")
                NKS = (CL + 32 + 127) // 128
                ebs = sb(st, "ebs", [128, NKS, 32], BF16)
                ebs_b = Buf("ebs")
                ebtmp_p = Pool([sb(st, "ebtmp%d" % i, [128, 512], F32) for i in range(2)], "ebtmp")
                PT_p = Pool([sb(st, "PT%d" % i, [128, 512], BF16) for i in range(4)], "PT")
                oacc = [ps(st, "oacc%d" % i, [128, 512]) for i in range(2)]
                oden = ps(st, "oden", [128, 512])
                oacc_b = Buf("oacc", psum=True)
                o1 = sb(st, "o1", [128, 4, 256], F32)
                o1_b = Buf("o1")
                ofin_p = Pool([sb(st, "ofin%d" % i, [128, 256], F32) for i in range(2)], "ofin")
                osq = sb(st, "osq", [128, 256], F32)
                osq_b = Buf("osq")
                rd = sb(st, "rd", [128, 8], F32)
                rd_b = Buf("rd")
                st_p = Pool([sb(st, "ast%d" % i, [128, 2], F32) for i in range(2)], "ast")
                zs_p = Pool([sb(st, "azs%d" % i, [128, 256], BF16) for i in range(2)], "azs")
                ybf_p = Pool([sb(st, "aybf%d" % i, [128, 256], BF16) for i in range(2)], "aybf")
                yT_p = Pool([sb(st, "ayT%d" % i, [128, 2, 128], BF16) for i in range(2)], "ayT")
                tp_p = Pool([ps(st, "atp", [128, 1024], BF16)[:, 0:128]], "atp", psum=True)
                KTs = sb(st, "KTs", [128, 2, CL + 32], BF16)
                KTs_b = Buf("KTs")
                NKS = (CL + 32 + 127) // 128
                Vs = sb(st, "Vs", [128, NKS, 256], BF16)
                Vs_b = Buf("Vs")
                PTs_p = Pool([sb(st, "PTs%d" % i, [128, NKS, 32], BF16) for i in range(2)], "PTs")

                def build_bias(h):
                    for i, dl in enumerate(NEAR_D):
                        hk, hk_b = hk_p.get()
                        base = BV_R0 - 127 - dl
                        E.dma("sp", hk[:], dram_view(bv_s, h * BV_LEN + base, [[1, 128], [1, 512]]), reads=[bvs_b], writes=[hk_b])
                        pt, pt_b = sc_p.get()
                        E.op("pe", lambda e: e.matmul(pt[:], lhsT=cst_sb[:, C_AID:C_AID + 128], rhs=hk[:], start=True, stop=True),
                             reads=[hk_b, cst_b], writes=[pt_b])
                        if dl < 0:
                            E.op("act", lambda e: e.activation(out=eb[:, i, :], in_=pt[:], func=AF.Exp), reads=[pt_b], writes=[eb_b])
                        else:
                            t, t_b = ebtmp_p.get()
                            E.op("act", lambda e: e.activation(out=t[:], in_=pt[:], func=AF.Exp), reads=[pt_b], writes=[t_b])
                            E.op("dve", lambda e: e.tensor_tensor(out=eb[:, i, :], in0=t[:], in1=dm_f[:, dl // 128, :], op=ALU.mult),
                                 reads=[t_b, dm_b], writes=[eb_b])
                    for kt in range(NKS):
                        hk, hk_b = hk_p.get()
                        base = BV_R0 - 127 - (128 * kt - CL)
                        E.dma("sp", hk[:, 0:32], dram_view(bv_s, h * BV_LEN + base, [[1, 128], [1, 32]]), reads=[bvs_b], writes=[hk_b])
                        pt, pt_b = sc_p.get()
                        E.op("pe", lambda e: e.matmul(pt[:, 0:32], lhsT=cst_sb[:, C_AID:C_AID + 128], rhs=hk[:, 0:32], start=True, stop=True),
                             reads=[hk_b, cst_b], writes=[pt_b])
                        E.op("act", lambda e: e.activation(out=ebs[:, kt, :], in_=pt[:, 0:32], func=AF.Exp), reads=[pt_b], writes=[ebs_b])

                def finish_o(h, L, o_ps_list, den_cols, c, orow_list):
                    ns = len(o_ps_list)
                    for sub in range(ns):
                        E.op("dve", lambda e: e.reciprocal(out=rd[0:L, c * 4 + sub:c * 4 + sub + 1], in_=den_cols[sub]),
                             reads=[oacc_b], writes=[rd_b])
                    if c == 0:
                        for sub in range(ns):
                            E.op("act", lambda e: e.activation(out=o1[0:L, sub, :], in_=o_ps_list[sub], func=AF.Copy,
                                                               scale=rd[0:L, sub:sub + 1]), reads=[oacc_b, rd_b], writes=[o1_b])
                        return
                    E.op("dve", lambda e: e.tensor_scalar(out=rd[0:L, 4:4 + ns], in0=rd[0:L, 4:4 + ns], scalar1=lam_t[0:L, 3:4],
                                                          scalar2=None, op0=ALU.mult), reads=[rd_b, misc_b], writes=[rd_b])
                    for sub in range(ns):
                        of, of_b = ofin_p.get()
                        E.op("dve", lambda e: e.scalar_tensor_tensor(out=of[0:L, :], in0=o_ps_list[sub], scalar=rd[0:L, 4 + sub:5 + sub],
                                                                     in1=o1[0:L, sub, :], op0=ALU.mult, op1=ALU.add),
                             reads=[oacc_b, rd_b, o1_b], writes=[of_b])
                        stt, stt_b = st_p.get()
                        E.op("act", lambda e: e.activation(out=osq[0:L, :], in_=of[0:L, :], func=AF.Square, accum_out=stt[0:L, 0:1]),
                             reads=[of_b], writes=[osq_b, stt_b])
                        E.op("act", lambda e: e.activation(out=stt[0:L, 1:2], in_=stt[0:L, 0:1], func=AF.Sqrt, scale=1.0 / 256,
                                                           bias=eps_sb[0:L, 0:1]), reads=[stt_b, eps_b], writes=[stt_b])
                        E.op("dve", lambda e: e.reciprocal(out=stt[0:L, 1:2], in_=stt[0:L, 1:2]), reads=[stt_b], writes=[stt_b])
                        orow = orow_list[sub]
                        zs, zs_b = zs_p.get()
                        E.dma("sp", zs[0:L, :], zs_s[orow:orow + L, h * 256:(h + 1) * 256], reads=[zss_b], writes=[zs_b])
                        E.op("dve", lambda e: e.scalar_tensor_tensor(out=of[0:L, :], in0=of[0:L, :], scalar=stt[0:L, 1:2], in1=sl_sb[0:L, :],
                                                                     op0=ALU.mult, op1=ALU.mult), reads=[of_b, stt_b, misc_b], writes=[of_b])
                        yb, yb_b = ybf_p.get()
                        E.op("dve", lambda e: e.tensor_tensor(out=yb[0:L, :], in0=of[0:L, :], in1=zs[0:L, :], op=ALU.mult),
                             reads=[of_b, zs_b], writes=[yb_b])
                        yT, yT_b = yT_p.get()
                        for e2 in range(2):
                            tp, tp_b = tp_p.get()
                            E.op("pe", lambda e: e.transpose(tp[:, 0:L], yb[0:L, e2 * 128:(e2 + 1) * 128], ident_bf[0:L, 0:L]),
                                 reads=[yb_b, cbf_b], writes=[tp_b])
                            E.op("act", lambda e: e.activation(out=yT[:, e2, 0:L], in_=tp[:, 0:L], func=AF.Copy), reads=[tp_b], writes=[yT_b])
                        E.dma("pool", dram_view(yT_s, h * 256 * OT + orow, [[OT, 128], [128 * OT, 2], [1, L]]), yT[:, :, 0:L],
                              reads=[yT_b], writes=[yTs_b])

                for h in range(HA):
                    E.dma("sp", KTt[:], dram_view(kT_s, h * 256 * TL, [[TL, 128], [128 * TL, 2], [1, WL]]), reads=[kTs_b], writes=[KT_b])
                    for kq in range(0, NKT, 32):
                        nk = min(32, NKT - kq)
                        E.dma("sp", Vt[:, kq:kq + nk, :],
                              dram_view(v_s, kq * 128 * WA + h * 256, [[WA, 128], [128 * WA, nk], [1, 256]]), reads=[vs_b], writes=[V_b])
                    E.dma("sp", QTt[:], dram_view(qT_s, h * 256 * OT, [[OT, 128], [128 * OT, 2], [1, OT]]), reads=[qTs_b], writes=[QT_b])
                    build_bias(h)
                    for qb in range(OWN // 512):
                        q0 = qb * 512
                        kt_hi = (QB0 + q0) // 128 + 4
                        for c in range(2):
                            for a in oacc + [oden]:
                                E.op("dve", lambda e: e.memset(a[:], 0.0), writes=[oacc_b])
                            def emit_pv(kt, PT, PT_b):
                                for sub in range(4):
                                    lt = PT[:, sub * 128:(sub + 1) * 128]
                                    E.op("pe", lambda e: e.matmul(oacc[sub // 2][:, (sub % 2) * 256:(sub % 2 + 1) * 256], lhsT=lt,
                                                                  rhs=Vt[:, kt, :], start=False, stop=(kt == kt_hi - 1),
                                                                  skip_group_check=True), reads=[PT_b, V_b], writes=[oacc_b])
                                    E.op("pe", lambda e: e.matmul(oden[:, sub:sub + 1], lhsT=lt, rhs=val_h[:, kt:kt + 1], start=False,
                                                                  stop=(kt == kt_hi - 1), skip_group_check=True),
                                         reads=[PT_b, val_b], writes=[oacc_b])

                            prev = None
                            for kt in range(kt_hi):
                                dl = 128 * kt - QB0 - q0
                                pt, pt_b = sc_p.get()
                                E.op("pe", lambda e: e.matmul(pt[:], lhsT=KTt[:, c, kt * 128:(kt + 1) * 128], rhs=QTt[:, c, q0:q0 + 512],
                                                              start=True, stop=True), reads=[KT_b, QT_b], writes=[pt_b])
                                PT, PT_b = PT_p.get()
                                if dl < NEAR_D[0]:
                                    E.op("act", lambda e: e.activation(out=PT[:], in_=pt[:], func=AF.Exp, scale=SCALE,
                                                                       bias=rb_sb[:, h:h + 1]), reads=[pt_b, misc_b], writes=[PT_b])
                                else:
                                    E.op("act", lambda e: e.activation(out=PT[:], in_=pt[:], func=AF.Exp, scale=SCALE),
                                         reads=[pt_b], writes=[PT_b])
                                    E.op("pool", lambda e: e.tensor_tensor(out=PT[:], in0=PT[:], in1=eb[:, NEAR_D.index(dl), :], op=ALU.mult),
                                         reads=[PT_b, eb_b], writes=[PT_b])
                                if prev is not None:
                                    emit_pv(*prev)
                                prev = (kt, PT, PT_b)
                            emit_pv(*prev)
                            finish_o(h, 128, [oacc[sub // 2][:, (sub % 2) * 256:(sub % 2 + 1) * 256] for sub in range(4)],
                                     [oden[:, sub:sub + 1] for sub in range(4)], c, [q0 + sub * 128 for sub in range(4)])
                    for b in range(SB):
                        E.dma("pool", KTs[:, :, 0:CL],
                              dram_view(ckT, (b * 2 * HA + 2 * h) * 128 * CL, [[CL, 128], [128 * CL, 2], [1, CL]]), writes=[KTs_b])
                        E.dma("sp", KTs[:, :, CL:CL + 32],
                              dram_view(kT_s, h * 256 * TL + WL + 32 * b, [[TL, 128], [128 * TL, 2], [1, 32]]), reads=[kTs_b], writes=[KTs_b])
                        nfull = CL // 128
                        rem = CL - nfull * 128
                        E.dma("pool", Vs[:, 0:nfull, :],
                              dram_view(cv, b * CL * WA + h * 256, [[WA, 128], [128 * WA, nfull], [1, 256]]), writes=[Vs_b])
                        if rem:
                            E.dma("pool", Vs[0:rem, nfull, :],
                                  dram_view(cv, (b * CL + nfull * 128) * WA + h * 256, [[WA, rem], [1, 256]]), writes=[Vs_b])
                        E.dma("sp", Vs[rem:rem + 32, nfull, :],
                              dram_view(v_s, (WL + 32 * b) * WA + h * 256, [[WA, 32], [1, 256]]), reads=[vs_b], writes=[Vs_b])
                        qc0 = OWN + 32 * b
                        for c in range(2):
                            pt, pt_b = sc_p.get()
                            for kt in range(NKS):
                                n = min(128, CL + 32 - kt * 128)
                                E.op("pe", lambda e: e.matmul(pt[0:n, kt * 32:(kt + 1) * 32], lhsT=KTs[:, c, kt * 128:kt * 128 + n],
                                                              rhs=QTt[:, c, qc0:qc0 + 32], start=True, stop=True),
                                     reads=[KTs_b, QT_b], writes=[pt_b])
                            PTs, PTs_b = PTs_p.get()
                            E.op("act", lambda e: e.activation(out=PTs[:], in_=pt[:, 0:NKS * 32].rearrange("p (k q) -> p k q", q=32),
                                                               func=AF.Exp, scale=SCALE), reads=[pt_b], writes=[PTs_b])
                            E.op("pool", lambda e: e.tensor_tensor(out=PTs[:], in0=PTs[:], in1=ebs[:], op=ALU.mult),
                                 reads=[PTs_b, ebs_b], writes=[PTs_b])
                            for kt in range(NKS):
                                n = min(128, CL + 32 - kt * 128)
                                E.op("pe", lambda e: e.matmul(oacc[0][0:32, 0:256], lhsT=PTs[0:n, kt, :], rhs=Vs[0:n, kt, :],
                                                              start=(kt == 0), stop=(kt == NKS - 1)), reads=[PTs_b, Vs_b], writes=[oacc_b])
                            for kt in range(NKS):
                                n = min(128, CL + 32 - kt * 128)
                                E.op("pe", lambda e: e.matmul(oden[0:32, 0:1], lhsT=PTs[0:n, kt, :], rhs=ones_bf[0:n, 0:1],
                                                              start=(kt == 0), stop=(kt == NKS - 1)), reads=[PTs_b, cbf_b], writes=[oacc_b])
                            finish_o(h, 32, [oacc[0][0:32, 0:256]], [oden[0:32, 0:1]], c, [qc0])

        if 2 in phases:
            phase2()
            E.barrier()

        def phase3():
            with ExitStack() as st:
                GH = min(8, HB)
                S_f = sb(st, "S_f", [128, HB, 128], F32)
                S_h = sb(st, "S_h", [128, HB, 128], BF16)
                S_b = [Buf("S%d" % h) for h in range(HB)]
                Sh_b = [Buf("Sh%d" % h) for h in range(HB)]
                nb_sb = sb(st, "nb_sb", [128, 128], F32)
                nb_b = Buf("nb", const=True)
                E.dma("sp", nb_sb[:], dram_view(normb, 0, [[0, 128], [1, 128]]), writes=[nb_b])
                gbt_p = Pool([sb(st, "g3bt%d" % i, [128, 2 * HB], F32) for i in range(2)], "g3bt")
                qkv_p = Pool([sb(st, "qkv%d" % i, [128, 3 * HB, 128], BF16) for i in range(2)], "qkv")
                zs_p = Pool([sb(st, "zs%d" % i, [128, WB], BF16) for i in range(2)], "zs")
                sc_p = Pool([sb(st, "sc%d" % i, [128, 6 * HB], F32) for i in range(2)], "sc")
                o_p = Pool([sb(st, "osb%d" % i, [128, HB, 128], F32) for i in range(2)], "osb", nsub=HB)
                sq_t = sb(st, "o_sq", [128, HB, 128], F32)
                sq_tb = Buf("o_sq")
                rst_p = Pool([sb(st, "orst%d" % i, [128, 2 * HB], F32) for i in range(2)], "orst")
                yb_p = Pool([sb(st, "ybf%d" % i, [128, HB, 128], BF16) for i in range(2)], "ybf")
                yT_p = Pool([sb(st, "yTt%d" % i, [128, HB, 128], BF16) for i in range(2)], "yTt", nsub=HB)
                scps_p = Pool([ps(st, "scps", [128, 512])], "scps", psum=True)
                pf_p = Pool([ps(st, "pfb%d" % i, [128, 512])[:, 0:128] for i in range(5)], "pf", psum=True)
                pb_p = Pool([ps(st, "pbf%d" % i, [128, 1024], BF16)[:, 0:128] for i in range(2)], "pb", psum=True)
                NHS = GH
                TDT = F32 if GDN_FP32 else BF16
                ident_t = cst_sb[:, C_ID:C_ID + 128] if GDN_FP32 else ident_bf
                pT_p = pf_p if GDN_FP32 else pb_p
                def hs_tiles(i):
                    d = {}
                    for nm in ("kt", "gams", "gamT", "WT", "vnew", "AqkT"):
                        d[nm] = (sb(st, "h%d_%s" % (i, nm), [128, 128], BF16), Buf("h%d_%s" % (i, nm)))
                    for nm in ("kbe", "vb", "N", "M", "X0", "X1", "XT0", "XT1", "P0", "P1"):
                        d[nm] = (sb(st, "h%d_%s" % (i, nm), [128, 128], TDT), Buf("h%d_%s" % (i, nm)))
                    for nm in ("gtri", "gam", "U", "t1"):
                        d[nm] = (sb(st, "h%d_%s" % (i, nm), [128, 128], F32), Buf("h%d_%s" % (i, nm)))
                    return d
                HS = [hs_tiles(i) for i in range(NHS)]
                triu = cst_sb[:, C_TRIU:C_TRIU + 128]
                lstr = cst_sb[:, C_LSTR:C_LSTR + 128]
                linc = cst_sb[:, C_LINC:C_LINC + 128]
                ones_f = cst_sb[:, C_ONE:C_ONE + 128]
                lstr_bf = cbf[:, C_LSTR:C_LSTR + 128]
                triu_bf = cbf[:, C_TRIU:C_TRIU + 128]

                def chunk(t0, L, need_o, orow0):
                    nfac = int(math.ceil(math.log2(L)))
                    gbt, gbt_b = gbt_p.get()
                    E.dma("sp", gbt[0:L, :], gb_s[t0:t0 + L, :], reads=[gbs_b], writes=[gbt_b])
                    qkv, qkv_b = qkv_p.get()
                    E.dma("sp", qkv[:, :, 0:L], dram_view(gT_s, t0, [[TL, 128], [128 * TL, 3 * HB], [1, L]]),
                          reads=[gTs_b], writes=[qkv_b])
                    if need_o:
                        zs, zs_b = zs_p.get()
                        E.dma("sp", zs[0:L, :], zs_s[orow0:orow0 + L, WA:WA + WB], reads=[zss_b], writes=[zs_b])
                    scps, scps_b = scps_p.get()
                    E.op("pe", lambda e: e.matmul(scps[0:L, 0:HB], lhsT=triu[0:L, 0:L], rhs=gbt[0:L, HB:2 * HB],
                                                  start=True, stop=True), reads=[cst_b, gbt_b], writes=[scps_b])
                    E.op("pe", lambda e: e.matmul(scps[:, HB:2 * HB], lhsT=ones_f[0:L, :], rhs=gbt[0:L, HB:2 * HB],
                                                  start=True, stop=True), reads=[cst_b, gbt_b], writes=[scps_b])
                    sc, sc_b = sc_p.get()
                    c_eG, c_eGLG, c_eGL, c_beG, c_nb, c_tmp = [slice(i * HB, (i + 1) * HB) for i in range(6)]
                    E.op("act", lambda e: e.activation(out=sc[0:L, c_eG], in_=scps[0:L, 0:HB], func=AF.Exp),
                         reads=[scps_b], writes=[sc_b])
                    E.op("act", lambda e: e.activation(out=sc[0:L, c_tmp], in_=scps[0:L, 0:HB], func=AF.Copy),
                         reads=[scps_b], writes=[sc_b])
                    E.op("dve", lambda e: e.tensor_tensor(out=sc[0:L, c_tmp], in0=scps[0:L, HB:2 * HB], in1=sc[0:L, c_tmp],
                                                          op=ALU.subtract), reads=[scps_b, sc_b], writes=[sc_b])
                    E.op("act", lambda e: e.activation(out=sc[0:L, c_eGLG], in_=sc[0:L, c_tmp], func=AF.Exp),
                         reads=[sc_b], writes=[sc_b])
                    E.op("act", lambda e: e.activation(out=sc[:, c_eGL], in_=scps[:, HB:2 * HB], func=AF.Exp),
                         reads=[scps_b], writes=[sc_b])
                    E.op("dve", lambda e: e.tensor_tensor(out=sc[0:L, c_beG], in0=sc[0:L, c_eG], in1=gbt[0:L, 0:HB],
                                                          op=ALU.mult), reads=[sc_b, gbt_b], writes=[sc_b])
                    E.op("dve", lambda e: e.tensor_scalar(out=sc[0:L, c_nb], in0=gbt[0:L, 0:HB], scalar1=-1.0, scalar2=None,
                                                          op0=ALU.mult), reads=[gbt_b], writes=[sc_b])
                    if need_o:
                        osb, osb_bs = o_p.get()

                    def col(cs, h):
                        return sc[0:L, cs.start + h:cs.start + h + 1]

                    for g0 in range(0, HB, GH):
                        heads = list(range(g0, min(HB, g0 + GH)))
                        T = {h: HS[h - g0] for h in heads}
                        QT = {h: qkv[:, h, 0:L] for h in heads}
                        KT = {h: qkv[:, HB + h, 0:L] for h in heads}
                        VT = {h: qkv[:, 2 * HB + h, 0:L] for h in heads}
                        for h in heads:
                            pk, pk_b = pb_p.get()
                            E.op("pe", lambda e: e.transpose(pk[0:L, :], KT[h], ident_bf), reads=[qkv_b, cbf_b], writes=[pk_b])
                            t, b = T[h]["kbe"]
                            E.op("act", lambda e: e.activation(out=t[0:L, :], in_=pk[0:L, :], func=AF.Copy, scale=col(c_beG, h)),
                                 reads=[pk_b, sc_b], writes=[b])
                            t, b = T[h]["kt"]
                            E.op("dve", lambda e: e.tensor_scalar(out=t[0:L, :], in0=pk[0:L, :], scalar1=col(c_eGLG, h),
                                                                  scalar2=None, op0=ALU.mult), reads=[pk_b, sc_b], writes=[b])
                            pv, pv_b = pb_p.get()
                            E.op("pe", lambda e: e.transpose(pv[0:L, :], VT[h], ident_bf), reads=[qkv_b, cbf_b], writes=[pv_b])
                            t, b = T[h]["vb"]
                            E.op("act", lambda e: e.activation(out=t[0:L, :], in_=pv[0:L, :], func=AF.Copy,
                                                               scale=gbt[0:L, h:h + 1]), reads=[pv_b, gbt_b], writes=[b])
                        for h in heads:
                            gt, gt_b = T[h]["gtri"]
                            E.op("pool", lambda e: e.tensor_scalar(out=gt[0:L, 0:L], in0=triu[0:L, 0:L],
                                                                   scalar1=gbt[0:L, HB + h:HB + h + 1], scalar2=None,
                                                                   op0=ALU.mult), reads=[cst_b, gbt_b], writes=[gt_b])
                            pg, pg_b = pf_p.get()
                            E.op("pe", lambda e: e.matmul(pg[0:L, 0:L], lhsT=gt[0:L, 0:L], rhs=lstr[0:L, 0:L], start=True, stop=True),
                                 reads=[gt_b, cst_b], writes=[pg_b])
                            gm, gm_b = T[h]["gam"]
                            E.op("act", lambda e: e.activation(out=gm[0:L, 0:L], in_=pg[0:L, 0:L], func=AF.Exp),
                                 reads=[pg_b], writes=[gm_b])
                            gs, gs_b = T[h]["gams"]
                            E.op("pool", lambda e: e.tensor_tensor(out=gs[0:L, 0:L], in0=gm[0:L, 0:L], in1=lstr[0:L, 0:L], op=ALU.mult),
                                 reads=[gm_b, cst_b], writes=[gs_b])
                            pkk, pkk_b = pf_p.get()
                            E.op("pe", lambda e: e.matmul(pkk[0:L, 0:L], lhsT=KT[h], rhs=KT[h], start=True, stop=True),
                                 reads=[qkv_b], writes=[pkk_b])
                            n_, n_b = T[h]["N"]
                            E.op("dve", lambda e: e.scalar_tensor_tensor(out=n_[0:L, 0:L], in0=pkk[0:L, 0:L], scalar=col(c_nb, h),
                                                                         in1=gs[0:L, 0:L], op0=ALU.mult, op1=ALU.mult),
                                 reads=[pkk_b, sc_b, gs_b], writes=[n_b])
                            if need_o:
                                pgt, pgt_b = pf_p.get()
                                E.op("pe", lambda e: e.matmul(pgt[0:L, 0:L], lhsT=lstr[0:L, 0:L], rhs=gt[0:L, 0:L], start=True, stop=True),
                                     reads=[gt_b, cst_b], writes=[pgt_b])
                                t1, t1_b = T[h]["t1"]
                                E.op("act", lambda e: e.activation(out=t1[0:L, 0:L], in_=pgt[0:L, 0:L], func=AF.Exp),
                                     reads=[pgt_b], writes=[t1_b])
                                gT_, gT_b = T[h]["gamT"]
                                E.op("pool", lambda e: e.tensor_tensor(out=gT_[0:L, 0:L], in0=t1[0:L, 0:L], in1=triu[0:L, 0:L], op=ALU.mult),
                                     reads=[t1_b, cst_b], writes=[gT_b])
                        cur = {}
                        for h in heads:
                            n_, n_b = T[h]["N"]
                            pm, pm_b = pT_p.get()
                            E.op("pe", lambda e: e.transpose(pm[0:L, 0:L], n_[0:L, 0:L], ident_t[0:L, 0:L]), reads=[n_b, cbf_b, cst_b], writes=[pm_b])
                            m_, m_b = T[h]["M"]
                            E.op("act", lambda e: e.activation(out=m_[0:L, 0:L], in_=pm[0:L, 0:L], func=AF.Copy), reads=[pm_b], writes=[m_b])
                            p0, p0_b = T[h]["P0"]
                            E.op("dve", lambda e: e.tensor_tensor(out=p0[0:L, 0:L], in0=pm[0:L, 0:L], in1=ident_t[0:L, 0:L], op=ALU.add),
                                 reads=[pm_b, cbf_b, cst_b], writes=[p0_b])
                            cur[h] = dict(X=T[h]["M"], XT=T[h]["N"], P=T[h]["P0"], pi=0, xi=0)
                        for k in range(1, nfac):
                            last = (k == nfac - 1)
                            for h in heads:
                                c = cur[h]
                                (X, X_b), (XT, XT_b), (P, P_b) = c["X"], c["XT"], c["P"]
                                nXT = T[h]["XT%d" % c["xi"]]
                                pxt, pxt_b = pf_p.get()
                                E.op("pe", lambda e: e.matmul(pxt[0:L, 0:L], lhsT=X[0:L, 0:L], rhs=XT[0:L, 0:L], start=True, stop=True),
                                     reads=[X_b, XT_b], writes=[pxt_b])
                                E.op("dve", lambda e: e.tensor_copy(out=nXT[0][0:L, 0:L], in_=pxt[0:L, 0:L]), reads=[pxt_b], writes=[nXT[1]])
                                if not last:
                                    nX = T[h]["X%d" % c["xi"]]
                                    px, px_b = pf_p.get()
                                    E.op("pe", lambda e: e.matmul(px[0:L, 0:L], lhsT=XT[0:L, 0:L], rhs=X[0:L, 0:L], start=True, stop=True),
                                         reads=[X_b, XT_b], writes=[px_b])
                                    E.op("act", lambda e: e.activation(out=nX[0][0:L, 0:L], in_=px[0:L, 0:L], func=AF.Copy),
                                         reads=[px_b], writes=[nX[1]])
                                    c["X"] = nX
                                c["XT"] = nXT
                                c["xi"] ^= 1
                            for h in heads:
                                c = cur[h]
                                (XT, XT_b), (P, P_b) = c["XT"], c["P"]
                                nP = T[h]["P%d" % (c["pi"] ^ 1)]
                                pp, pp_b = pf_p.get()
                                E.op("pe", lambda e: e.matmul(pp[0:L, 0:L], lhsT=XT[0:L, 0:L], rhs=P[0:L, 0:L], start=True, stop=True),
                                     reads=[XT_b, P_b], writes=[pp_b])
                                E.op("dve", lambda e: e.tensor_tensor(out=nP[0][0:L, 0:L], in0=pp[0:L, 0:L], in1=P[0:L, 0:L], op=ALU.add),
                                     reads=[pp_b, P_b], writes=[nP[1]])
                                c["P"] = nP
                                c["pi"] ^= 1
                        for h in heads:
                            P, P_b = cur[h]["P"]
                            pu, pu_b = pf_p.get()
                            vb, vb_b = T[h]["vb"]
                            E.op("pe", lambda e: e.matmul(pu[0:L, :], lhsT=P[0:L, 0:L], rhs=vb[0:L, :], start=True, stop=True),
                                 reads=[P_b, vb_b], writes=[pu_b])
                            U, U_b = T[h]["U"]
                            E.op("act", lambda e: e.activation(out=U[0:L, :], in_=pu[0:L, :], func=AF.Copy), reads=[pu_b], writes=[U_b])
                            pw, pw_b = pf_p.get()
                            kbe, kbe_b = T[h]["kbe"]
                            E.op("pe", lambda e: e.matmul(pw[:, 0:L], lhsT=kbe[0:L, :], rhs=P[0:L, 0:L], start=True, stop=True),
                                 reads=[P_b, kbe_b], writes=[pw_b])
                            WT, WT_b = T[h]["WT"]
                            E.op("dve", lambda e: e.tensor_copy(out=WT[:, 0:L], in_=pw[:, 0:L]), reads=[pw_b], writes=[WT_b])
                        for h in heads:
                            WT, WT_b = T[h]["WT"]
                            pws, pws_b = pf_p.get()
                            E.op("pe", lambda e: e.matmul(pws[0:L, :], lhsT=WT[:, 0:L], rhs=S_h[:, h, :], start=True, stop=True),
                                 reads=[WT_b, Sh_b[h]], writes=[pws_b])
                            U, U_b = T[h]["U"]
                            vn, vn_b = T[h]["vnew"]
                            E.op("dve", lambda e: e.tensor_tensor(out=vn[0:L, :], in0=U[0:L, :], in1=pws[0:L, :], op=ALU.subtract),
                                 reads=[U_b, pws_b], writes=[vn_b])
                            if need_o:
                                pqs, pqs_b = pf_p.get()
                                E.op("pe", lambda e: e.matmul(pqs[0:L, :], lhsT=QT[h], rhs=S_h[:, h, :], start=True, stop=True),
                                     reads=[qkv_b, Sh_b[h]], writes=[pqs_b])
                                t1, t1_b = T[h]["t1"]
                                E.op("act", lambda e: e.activation(out=t1[0:L, :], in_=pqs[0:L, :], func=AF.Copy, scale=col(c_eG, h)),
                                     reads=[pqs_b, sc_b], writes=[t1_b])
                                pqk, pqk_b = pf_p.get()
                                E.op("pe", lambda e: e.matmul(pqk[0:L, 0:L], lhsT=KT[h], rhs=QT[h], start=True, stop=True),
                                     reads=[qkv_b], writes=[pqk_b])
                                aq, aq_b = T[h]["AqkT"]
                                gT_, gT_b = T[h]["gamT"]
                                E.op("dve", lambda e: e.tensor_tensor(out=aq[0:L, 0:L], in0=pqk[0:L, 0:L], in1=gT_[0:L, 0:L], op=ALU.mult),
                                     reads=[pqk_b, gT_b], writes=[aq_b])
                        for h in heads:
                            vn, vn_b = T[h]["vnew"]
                            if need_o:
                                aq, aq_b = T[h]["AqkT"]
                                pav, pav_b = pf_p.get()
                                E.op("pe", lambda e: e.matmul(pav[0:L, :], lhsT=aq[0:L, 0:L], rhs=vn[0:L, :], start=True, stop=True),
                                     reads=[aq_b, vn_b], writes=[pav_b])
                                t1, t1_b = T[h]["t1"]
                                E.op("dve", lambda e: e.tensor_tensor(out=osb[0:L, h, :], in0=pav[0:L, :], in1=t1[0:L, :], op=ALU.add),
                                     reads=[pav_b, t1_b], writes=[osb_bs[h]])
                            kt, kt_b = T[h]["kt"]
                            psu, psu_b = pf_p.get()
                            E.op("pe", lambda e: e.matmul(psu[:, :], lhsT=kt[0:L, :], rhs=vn[0:L, :], start=True, stop=True),
                                 reads=[kt_b, vn_b], writes=[psu_b])
                            E.op("dve", lambda e: e.scalar_tensor_tensor(out=S_f[:, h, :], in0=S_f[:, h, :],
                                                                         scalar=sc[:, c_eGL.start + h:c_eGL.start + h + 1],
                                                                         in1=psu[:, :], op0=ALU.mult, op1=ALU.add),
                                 reads=[psu_b, sc_b, S_b[h]], writes=[S_b[h]])
                            E.op("pool", lambda e: e.tensor_copy(out=S_h[:, h, :], in_=S_f[:, h, :]), reads=[S_b[h]], writes=[Sh_b[h]])
                    if need_o:
                        E.op("act", lambda e: e.activation(out=sq_t[0:L], in_=osb[0:L], func=AF.Square), reads=osb_bs, writes=[sq_tb])
                        rst, rst_b = rst_p.get()
                        E.op("dve", lambda e: e.tensor_reduce(out=rst[0:L, 0:HB], in_=sq_t[0:L], axis=AX.X, op=ALU.add),
                             reads=[sq_tb], writes=[rst_b])
                        E.op("act", lambda e: e.activation(out=rst[0:L, 0:HB], in_=rst[0:L, 0:HB], func=AF.Sqrt, scale=1.0 / 128,
                                                           bias=eps_sb[0:L, 0:1]), reads=[rst_b, eps_b], writes=[rst_b])
                        E.op("dve", lambda e: e.reciprocal(out=rst[0:L, 0:HB], in_=rst[0:L, 0:HB]), reads=[rst_b], writes=[rst_b])
                        E.op("dve", lambda e: e.tensor_tensor(out=sq_t[0:L], in0=osb[0:L],
                                                              in1=rst[0:L, 0:HB].unsqueeze(2).to_broadcast([L, HB, 128]), op=ALU.mult),
                             reads=osb_bs + [rst_b], writes=[sq_tb])
                        E.op("pool", lambda e: e.tensor_tensor(out=sq_t[0:L], in0=sq_t[0:L],
                                                               in1=nb_sb[0:L, :].unsqueeze(1).to_broadcast([L, HB, 128]), op=ALU.mult),
                             reads=[sq_tb, nb_b], writes=[sq_tb])
                        yb, yb_b = yb_p.get()
                        E.op("dve", lambda e: e.tensor_tensor(out=yb[0:L], in0=sq_t[0:L],
                                                              in1=zs[0:L, :].rearrange("p (h d) -> p h d", d=128), op=ALU.mult),
                             reads=[sq_tb, zs_b], writes=[yb_b])
                        yT, yT_bs = yT_p.get()
                        for h in range(HB):
                            pt_, pt_b = pb_p.get()
                            E.op("pe", lambda e: e.transpose(pt_[:, 0:L], yb[0:L, h, :], ident_bf[0:L, 0:L]), reads=[yb_b, cbf_b], writes=[pt_b])
                            E.op("act", lambda e: e.activation(out=yT[:, h, 0:L], in_=pt_[:, 0:L], func=AF.Copy), reads=[pt_b], writes=[yT_bs[h]])
                        E.dma("pool", dram_view(yT_s, WA * OT + orow0, [[OT, 128], [128 * OT, HB], [1, L]]), yT[:, :, 0:L],
                              reads=yT_bs, writes=[yTs_b])

                E.op("pool", lambda e: e.memset(S_f[:], 0.0), writes=S_b)
                E.op("pool", lambda e: e.memset(S_h[:], 0.0), writes=Sh_b)
                for t0 in range(0, WL, 128):
                    need = t0 >= WL - OWN
                    chunk(t0, 128, need, t0 - (WL - OWN))
                E.dma("sp", ssm_o.ap().rearrange("(h p) d -> p h d", p=128), S_f[:], reads=S_b, is_output=True)
                for b in range(SB):
                    E.dma("sp", S_f[:], ssm[b * HB * 128:(b + 1) * HB * 128, :].rearrange("(h p) d -> p h d", p=128), writes=S_b)
                    E.op("pool", lambda e: e.tensor_copy(out=S_h[:], in_=S_f[:]), reads=S_b, writes=Sh_b)
                    chunk(WL + 32 * b, 32, True, OWN + 32 * b)
                    E.dma("sp", ssm_so[b * HB * 128:(b + 1) * HB * 128, :].rearrange("(h p) d -> p h d", p=128), S_f[:],
                          reads=S_b, is_output=True)

        if 3 in phases:
            phase3()
            E.barrier()


        def phase4():
            with ExitStack() as st:
                pnb = sb(st, "pnb", [128, D], F32)
                pnb_b = Buf("pnb", const=True)
                E.dma("sp", pnb[:], dram_view(postn, 0, [[0, 128], [1, D]]), writes=[pnb_b])
                TG = 256
                YT_p = Pool([sb(st, "YTg%d" % i, [128, KC, TG], BF16) for i in range(1)], "YTg")
                wo_p = Pool([sb(st, "wo%d" % i, [128, KC, 512], BF16) for i in range(2)], "wo")
                yacc = sb(st, "yacc", [128, TG // 128, D], F32)
                yacc_b = [Buf("yacc%d" % i) for i in range(4)]
                ssq = sb(st, "ssq4", [128, 4, D // 512], F32)
                ssq_b = [Buf("ssq4_%d" % i) for i in range(4)]
                junk = sb(st, "junk4", [128, 512], F32)
                junk_b = Buf("junk4")
                r4 = sb(st, "r4", [128, 4, 2], F32)
                xo_p = Pool([sb(st, "xo%d" % i, [128, D], F32) for i in range(2)], "xo")
                mm_p = Pool([ps(st, "m4_%d" % i, [128, 512]) for i in range(4)], "m4", psum=True)
                NCB = D // 512
                if os.environ.get("P4ZERO"):
                    zt = sb(st, "zt", [128, OT], BF16)
                    zt_b = Buf("zt")
                    E.op("dve", lambda e: e.memset(zt[:], 0.0), writes=[zt_b])
                    for kc in range(KC):
                        E.dma("sp", yT_s[kc * 128:(kc + 1) * 128, :], zt[:], reads=[zt_b], writes=[yTs_b])
                for g0 in range(0, OT, TG):
                    gw = min(TG, OT - g0)
                    nsub = gw // 128
                    YT, YT_b = YT_p.get()
                    E.dma("sp", YT[:, :, 0:gw], dram_view(yT_s, g0, [[OT, 128], [128 * OT, KC], [1, gw]]), reads=[yTs_b], writes=[YT_b])
                    for cb in range(NCB):
                        wo, wo_b = wo_p.get()
                        E.dma("pool", wo[:], dram_view(wout, cb * 512, [[D, 128], [128 * D, KC], [1, 512]]), writes=[wo_b])
                        for sub in range(nsub):
                            pt, pt_b = mm_p.get()
                            for kc in range(KC):
                                E.op("pe", lambda e: e.matmul(pt[:], lhsT=YT[:, kc, sub * 128:(sub + 1) * 128], rhs=wo[:, kc, :],
                                                              start=(kc == 0), stop=(kc == KC - 1)), reads=[YT_b, wo_b], writes=[pt_b])
                            E.op("dve", lambda e: e.tensor_copy(out=yacc[:, sub, cb * 512:(cb + 1) * 512], in_=pt[:]),
                                 reads=[pt_b], writes=[yacc_b[sub]])
                            E.op("act", lambda e: e.activation(out=junk[:], in_=pt[:], func=AF.Square, accum_out=ssq[:, sub, cb:cb + 1]),
                                 reads=[pt_b], writes=[junk_b, ssq_b[sub]])
                    P4CUT = int(os.environ.get("P4CUT", "9"))
                    for sub in range(nsub):
                        if P4CUT < 2:
                            break
                        r0 = g0 + sub * 128
                        xo, xo_b = xo_p.get()
                        E.dma("sp", xo[:], xown[r0:r0 + 128, :], writes=[xo_b])
                        E.op("dve", lambda e: e.tensor_reduce(out=r4[:, sub, 0:1], in_=ssq[:, sub, :], axis=AX.X, op=ALU.add),
                             reads=[ssq_b[sub]], writes=[ssq_b[sub]])
                        E.op("act", lambda e: e.activation(out=r4[:, sub, 1:2], in_=r4[:, sub, 0:1], func=AF.Sqrt, scale=1.0 / D,
                                                           bias=eps_sb[:, 0:1]), reads=[ssq_b[sub], eps_b], writes=[ssq_b[sub]])
                        E.op("dve", lambda e: e.reciprocal(out=r4[:, sub, 1:2], in_=r4[:, sub, 1:2]), reads=[ssq_b[sub]], writes=[ssq_b[sub]])
                        if P4CUT < 3:
                            continue
                        E.op("dve", lambda e: e.scalar_tensor_tensor(out=yacc[:, sub, :], in0=yacc[:, sub, :], scalar=r4[:, sub, 1:2],
                                                                     in1=pnb[:], op0=ALU.mult, op1=ALU.mult),
                             reads=[yacc_b[sub], ssq_b[sub], pnb_b], writes=[yacc_b[sub]])
                        if P4CUT < 4:
                            continue
                        E.op("pool", lambda e: e.tensor_tensor(out=xo[:], in0=xo[:], in1=yacc[:, sub, :], op=ALU.add),
                             reads=[xo_b, yacc_b[sub]], writes=[xo_b])
                        if P4CUT < 5:
                            continue
                        E.dma("sp", y_o[r0:r0 + 128, :], xo[:], reads=[xo_b], is_output=True)

        if 4 in phases:
            phase4()
            E.barrier()

        E.finish()
        global LAST_E
        LAST_E = E
    return nc


def prep_inputs(cfg, inp):
    D, NC, OWN, SB, WL, TL, HA, HB, WA, WB = cfg.D, cfg.NC, cfg.OWN, cfg.SB, cfg.WL, cfg.TL, cfg.HA, cfg.HB, cfg.WA, cfg.WB
    f = np.float32
    w = np.asarray(inp["w_in"], f)[0]
    wF = np.ascontiguousarray(np.concatenate([w[:, WA:2 * WA], w[:, 4 * WA:4 * WA + 3 * WB], w[:, 0:WA]], axis=1))
    o = 4 * WA + 4 * WB
    wT = np.ascontiguousarray(np.concatenate([w[:, 2 * WA:3 * WA], w[:, o:o + 2 * HB], w[:, 3 * WA:4 * WA],
                                              w[:, 4 * WA + 3 * WB:4 * WA + 4 * WB]], axis=1))
    wout = np.ascontiguousarray(np.asarray(inp["w_out"], f)[0])
    pn = np.ascontiguousarray(np.asarray(inp["pre_norm"], f)[0].reshape(cfg.KC, 128).T)
    postn = np.ascontiguousarray(np.asarray(inp["post_norm"], f)[0][None])
    convw = np.ascontiguousarray(np.asarray(inp["conv_b"], f)[0].reshape(4, 3 * HB, 128).transpose(2, 1, 0)).reshape(128, -1)
    gconst = np.concatenate([np.asarray(inp["a_log_b"], f)[0], np.asarray(inp["dt_bias_b"], f)[0]])[None]
    relb = np.ascontiguousarray(np.asarray(inp["rel_bias"], f))
    lamv = np.concatenate([np.asarray(inp[k], f)[0] for k in ("lambda_q1", "lambda_k1", "lambda_q2", "lambda_k2")])[None]
    subln = np.asarray(inp["subln_a"], f)[0][None]
    normb = np.asarray(inp["norm_b"], f)[0][None]
    xp = np.asarray(inp["x_prompt"], f)[0]
    xs = np.asarray(inp["x_sample"], f)
    meta = np.asarray(inp["meta_tokens"], f)
    ck = np.asarray(inp["cache_k_a"], f)[0]
    cvv = np.asarray(inp["cache_v_a"], f)[0]
    ssm = np.asarray(inp["state_ssm_b"], f)[0]
    cs = np.asarray(inp["state_conv_b"], f)[0]
    cstc = make_cst()
    ohc = make_oh()
    dmc = make_dmask()
    maps = []
    for c in range(NC):
        pad = OWN * (NC - 1 - c)
        xa = np.zeros((TL, D), f)
        xa[pad + 112:pad + 128] = meta
        xa[pad + 128:WL] = xp[0:(c + 1) * OWN]
        xa[WL:] = xs[c * SB:(c + 1) * SB].reshape(-1, D)
        m = {
            "xT": np.ascontiguousarray(xa.T),
            "xown": np.ascontiguousarray(xa[WL - OWN:]),
            "wF": wF, "wT": wT, "wout": wout, "pn": pn, "postn": postn, "convw": convw,
            "convst": np.ascontiguousarray(cs[c * SB:(c + 1) * SB].reshape(SB, 3, 3 * HB, 128).transpose(3, 2, 0, 1)).reshape(128, -1),
            "gconst": gconst, "relb": relb, "lamv": lamv, "subln": subln, "normb": normb,
            "ckT": np.ascontiguousarray(ck[c * SB:(c + 1) * SB].transpose(0, 2, 3, 4, 1)).reshape(-1, cfg.CL),
            "cv": np.ascontiguousarray(cvv[c * SB:(c + 1) * SB]).reshape(SB * cfg.CL, WA),
            "ssm": np.ascontiguousarray(ssm[c * SB:(c + 1) * SB]).reshape(-1, 128),
            "cst": cstc,
            "valid": np.ascontiguousarray((np.arange(WL).reshape(-1, 128).T >= pad + 112).astype(f)),
            "oh": ohc, "dmask": dmc,
        }
        maps.append(m)
    return maps


def assemble(cfg, res):
    D, NC, OWN, SB, HA, HB, WA, WB, SEQ = cfg.D, cfg.NC, cfg.OWN, cfg.SB, cfg.HA, cfg.HB, cfg.WA, cfg.WB, cfg.SEQ
    f = np.float32
    DECB, DS = cfg.DECB, cfg.DECS
    y_p = np.zeros((1, SEQ, D), f)
    y_s = np.zeros((DECB, DS, D), f)
    k_p = np.zeros((1, 1, 16 + SEQ, HA, 2, 128), f)
    v_p = np.zeros((1, 1, 16 + SEQ, HA, 256), f)
    k_s = np.zeros((1, DECB, DS, HA, 2, 128), f)
    v_s = np.zeros((1, DECB, DS, HA, 256), f)
    c_s = np.zeros((1, DECB, 3, 3 * WB), f)
    s_s = np.zeros((1, DECB, HB, 128, 128), f)
    for c in range(NC):
        r = res[c]
        y = np.asarray(r["y_o"])
        y_p[0, c * OWN:(c + 1) * OWN] = y[0:OWN]
        y_s[c * SB:(c + 1) * SB] = y[OWN:].reshape(SB, DS, D)
        kt = np.asarray(r["kT_o"]).reshape(HA, 2, 128, -1).transpose(3, 0, 1, 2)
        vt = np.asarray(r["v_o"]).reshape(-1, HA, 256)
        if c == 0:
            k_p[0, 0, 0:16] = kt[112:128]
            v_p[0, 0, 0:16] = vt[112:128]
        k_p[0, 0, 16 + c * OWN:16 + (c + 1) * OWN] = kt[128:128 + OWN]
        v_p[0, 0, 16 + c * OWN:16 + (c + 1) * OWN] = vt[128:128 + OWN]
        k_s[0, c * SB:(c + 1) * SB] = kt[128 + OWN:].reshape(SB, DS, HA, 2, 128)
        v_s[0, c * SB:(c + 1) * SB] = vt[128 + OWN:].reshape(SB, DS, HA, 256)
        c_s[0, c * SB:(c + 1) * SB] = np.asarray(r["conv_so"]).reshape(3 * HB, 128, SB, 3).transpose(2, 3, 0, 1).reshape(SB, 3, 3 * WB)
        s_s[0, c * SB:(c + 1) * SB] = np.asarray(r["ssm_so"]).reshape(SB, HB, 128, 128)
    rl = res[NC - 1]
    s_p = np.asarray(rl["ssm_o"]).reshape(1, 1, HB, 128, 128).astype(f)
    c_p = np.asarray(rl["conv_o"]).reshape(3 * HB, 128, 3).transpose(2, 0, 1).reshape(1, 1, 3, 3 * WB).astype(f)
    return (y_p, y_s, k_p, v_p, s_p, c_p, k_s, v_s, s_s, c_s)


_NC_CACHE = {}


def kernel(**inputs):
    cfg = Cfg()
    if "nc" not in _NC_CACHE:
        _NC_CACHE["nc"] = build(cfg)
    nc = _NC_CACHE["nc"]
    maps = prep_inputs(cfg, inputs)
    res = run_bass_kernel_spmd(nc, maps, core_ids=list(range(cfg.NC)))
    return assemble(cfg, res.results)
```
